# Optimizing a Trainium2 kernel written in Bass

```python
import math
import jax
import jax.numpy as jnp
from jax import lax
import numpy as np

D_MODEL = 1024
BATCH = 8
SEQ = 4096
DEPTH = 2

CHUNK = 64
Q_BLOCK = 128

N_MIXERS = 4
HEAD_DIM = 64
GROUP_WIDTH = D_MODEL // N_MIXERS
N_HEADS_GROUP = GROUP_WIDTH // HEAD_DIM
D_MIX = N_MIXERS * GROUP_WIDTH

MLA_Q_RANK = D_MODEL // 4
MLA_KV_RANK = D_MODEL // 8
MLA_NOPE_DIM = HEAD_DIM
MLA_ROPE_DIM = HEAD_DIM // 2
MLA_V_DIM = HEAD_DIM

RET_DECAY_OFFSET = 5.0

D_FF = 4 * D_MODEL

ROPE_BASE = 10000.0
EPS = 1e-6
FORGET_BIAS = 4.0

D_IN_PROJ = 10 * GROUP_WIDTH + N_HEADS_GROUP + MLA_Q_RANK + MLA_KV_RANK + MLA_ROPE_DIM

kernel_name = 'hybrid_fox_mla_retention_stickbreaking_trunk'

F32 = jnp.float32


def rms_norm(x, gain):
    xf = x.astype(F32)
    y = xf * lax.rsqrt(jnp.mean(xf * xf, axis=-1, keepdims=True) + EPS)
    return (y * gain.astype(F32)).astype(x.dtype)


def head_group_norm(o, gain):
    mu = jnp.mean(o, axis=-1, keepdims=True)
    var = jnp.mean(jnp.square(o - mu), axis=-1, keepdims=True)
    y = (o - mu) * lax.rsqrt(var + EPS)
    return y * gain.astype(F32).reshape(o.shape[2], o.shape[3])


def apply_rope(x, positions):
    half = x.shape[-1] // 2
    inv_freq = ROPE_BASE ** (-jnp.arange(half, dtype=F32) / half)
    ang = positions.astype(F32)[:, :, None, None] * inv_freq
    cos, sin = jnp.cos(ang), jnp.sin(ang)
    xf = x.astype(F32)
    x1, x2 = xf[..., :half], xf[..., half:]
    return jnp.concatenate([x1 * cos - x2 * sin, x1 * sin + x2 * cos], axis=-1).astype(x.dtype)


def to_blocks(t):
    b, s = t.shape[:2]
    return jnp.moveaxis(t.reshape((b, s // Q_BLOCK, Q_BLOCK) + t.shape[2:]), 1, 0)


def from_blocks(t):
    nb, b, qb = t.shape[:3]
    return jnp.moveaxis(t, 0, 1).reshape((b, nb * qb) + t.shape[3:])


def forgetting_attention(q, k, v, f_logit):
    seq = q.shape[1]
    scale = q.shape[-1] ** -0.5
    cum = jnp.cumsum(jax.nn.log_sigmoid(f_logit.astype(F32)), axis=1)
    cum_k = jnp.swapaxes(cum, 1, 2)
    k_pos = jnp.arange(seq)
    q_pos = k_pos.reshape(-1, Q_BLOCK)

    def block(xs):
        qb, cqb, qp = xs
        s = jnp.einsum('bqhd,bkhd->bhqk', qb, k, preferred_element_type=F32) * scale
        s = s + jnp.swapaxes(cqb, 1, 2)[..., None] - cum_k[:, :, None, :]
        s = jnp.where(k_pos[None, :] <= qp[:, None], s, -jnp.inf)
        p = jax.nn.softmax(s, axis=-1)
        return jnp.einsum('bhqk,bkhd->bqhd', p.astype(v.dtype), v)

    return from_blocks(lax.map(block, (to_blocks(q), to_blocks(cum), q_pos)))


def chunk_causal_softmax_attention(q, k, v):
    seq = q.shape[1]
    scale = q.shape[-1] ** -0.5
    k_chunk = jnp.arange(seq) // CHUNK
    q_pos = jnp.arange(seq).reshape(-1, Q_BLOCK)

    def block(xs):
        qb, qp = xs
        s = jnp.einsum('bqhd,bkhd->bhqk', qb, k, preferred_element_type=F32) * scale
        s = jnp.where(k_chunk[None, :] <= (qp // CHUNK)[:, None], s, -jnp.inf)
        p = jax.nn.softmax(s, axis=-1)
        return jnp.einsum('bhqk,bkhd->bqhd', p.astype(v.dtype), v)

    return from_blocks(lax.map(block, (to_blocks(q), q_pos)))


def stick_breaking_attention(q, k, v):
    seq = q.shape[1]
    scale = q.shape[-1] ** -0.5
    k_pos = jnp.arange(seq)
    q_pos = k_pos.reshape(-1, Q_BLOCK)

    def block(xs):
        qb, qp = xs
        z = jnp.einsum('bqhd,bkhd->bhqk', qb, k, preferred_element_type=F32) * scale
        visible = k_pos[None, :] < qp[:, None]
        log_stay = jnp.where(visible, jax.nn.log_sigmoid(-z), 0.0)
        later = lax.cumsum(log_stay, axis=3, reverse=True) - log_stay
        w = jnp.where(visible, jnp.exp(jax.nn.log_sigmoid(z) + later), 0.0)
        return jnp.einsum('bhqk,bkhd->bqhd', w.astype(v.dtype), v)

    return from_blocks(lax.map(block, (to_blocks(q), q_pos)))


def chunkwise_retention(q, k, v, positions):
    b, seq, h, d = q.shape
    n = seq // CHUNK
    qf = apply_rope(q, positions).astype(F32)
    kf = apply_rope(k, positions).astype(F32) * (d ** -0.5)
    vf = v.astype(F32)
    log_gamma = jnp.log1p(-jnp.power(2.0, -RET_DECAY_OFFSET - jnp.arange(h, dtype=F32)))
    idx = jnp.arange(CHUNK, dtype=F32)
    qc = qf.reshape(b, n, CHUNK, h, d)
    kc = kf.reshape(b, n, CHUNK, h, d)
    vc = vf.reshape(b, n, CHUNK, h, d)
    intra_decay = jnp.exp(log_gamma[:, None, None] * jnp.abs(idx[:, None] - idx[None, :]))
    scores = jnp.einsum('bncht,bnmht->bnhcm', qc, kc) * intra_decay
    intra = jnp.einsum('bnhcm,bnmhe->bnche', scores, vc)
    k_tail = kc * jnp.exp(log_gamma[None, :] * (CHUNK - 1 - idx)[:, None])[None, None, :, :, None]
    chunk_kv = jnp.einsum('bnmht,bnmhe->nbhte', k_tail, vc)
    chunk_decay = jnp.exp(log_gamma * CHUNK)[None, :, None, None]

    def step(state, kv):
        return state * chunk_decay + kv, state

    _, prev_state = lax.scan(step, jnp.zeros((b, h, d, d), F32), chunk_kv)
    q_head = qc * jnp.exp(log_gamma[None, :] * (idx + 1.0)[:, None])[None, None, :, :, None]
    inter = jnp.einsum('bncht,nbhte->bnche', q_head, prev_state)
    return (intra + inter).reshape(b, seq, h, d)


def split_columns(proj):
    sizes = [GROUP_WIDTH, GROUP_WIDTH, GROUP_WIDTH, N_HEADS_GROUP,
             MLA_Q_RANK, MLA_KV_RANK, MLA_ROPE_DIM,
             GROUP_WIDTH, GROUP_WIDTH, GROUP_WIDTH, GROUP_WIDTH,
             GROUP_WIDTH, GROUP_WIDTH, GROUP_WIDTH]
    offsets = [int(o) for o in np.cumsum(sizes)[:-1]]
    return jnp.split(proj, offsets, axis=-1)


def hybrid_mixer(h, positions, w_in, b_forget, g_q_lora, w_q_up, g_kv_lora, w_kv_up, g_mix_out, w_out):
    b, seq, _ = h.shape
    heads = lambda t: t.reshape(b, seq, N_HEADS_GROUP, -1)
    proj = jnp.einsum('bsd,dn->bsn', h, w_in)
    (fq, fk, fv, ff, cq, ckv, kr, rq, rk, rv, rg, sq, sk, sv) = split_columns(proj)

    out_a = forgetting_attention(heads(fq), heads(fk), heads(fv), ff + b_forget)
    out_a = rms_norm(out_a.reshape(b, seq, GROUP_WIDTH), g_mix_out[0:GROUP_WIDTH])

    q = jnp.einsum('bsr,rn->bsn', rms_norm(cq, g_q_lora), w_q_up).reshape(b, seq, N_HEADS_GROUP, MLA_NOPE_DIM + MLA_ROPE_DIM)
    q = jnp.concatenate([q[..., :MLA_NOPE_DIM], apply_rope(q[..., MLA_NOPE_DIM:], positions)], axis=-1)
    kv = jnp.einsum('bsr,rn->bsn', rms_norm(ckv, g_kv_lora), w_kv_up).reshape(b, seq, N_HEADS_GROUP, MLA_NOPE_DIM + MLA_V_DIM)
    k_rope = apply_rope(kr[:, :, None, :], positions)
    k = jnp.concatenate([kv[..., :MLA_NOPE_DIM], jnp.broadcast_to(k_rope, (b, seq, N_HEADS_GROUP, MLA_ROPE_DIM))], axis=-1)
    out_b = chunk_causal_softmax_attention(q, k, kv[..., MLA_NOPE_DIM:])
    out_b = rms_norm(out_b.reshape(b, seq, GROUP_WIDTH), g_mix_out[GROUP_WIDTH:2 * GROUP_WIDTH])

    ret = chunkwise_retention(heads(rq), heads(rk), heads(rv), positions)
    ret = head_group_norm(ret, g_mix_out[2 * GROUP_WIDTH:3 * GROUP_WIDTH]).reshape(b, seq, GROUP_WIDTH)
    out_c = (ret * jax.nn.silu(rg.astype(F32))).astype(h.dtype)

    out_d = stick_breaking_attention(heads(sq), heads(sk), heads(sv))
    out_d = rms_norm(out_d.reshape(b, seq, GROUP_WIDTH), g_mix_out[3 * GROUP_WIDTH:])

    mixed = jnp.concatenate([out_a, out_b, out_c, out_d], axis=-1)
    return jnp.einsum('bsn,nd->bsd', mixed, w_out)


def squared_relu_mlp(h, w_up, w_down):
    u = jnp.square(jax.nn.relu(jnp.einsum('bsd,df->bsf', h, w_up)))
    return jnp.einsum('bsf,fd->bsd', u, w_down)


def setup_inputs(seed: int = 0) -> dict:
    key = jax.random.key(seed)
    ks = jax.random.split(key, 20)
    nrm = lambda k, shape, fan_in: jax.random.normal(k, shape, F32) * (fan_in ** -0.5)
    gain = lambda k, shape: 1.0 + 0.05 * jax.random.normal(k, shape, F32)
    x = jax.random.normal(ks[0], (BATCH, SEQ, D_MODEL), F32)
    start = jax.random.randint(ks[1], (BATCH, 1), 0, 1024, dtype=jnp.int32)
    positions = start + jnp.arange(SEQ, dtype=jnp.int32)[None, :]
    return {
        'x': x,
        'positions': positions,
        'g_mix_pre': gain(ks[2], (DEPTH, D_MODEL)),
        'w_in': nrm(ks[3], (DEPTH, D_MODEL, D_IN_PROJ), D_MODEL),
        'b_forget': FORGET_BIAS + 0.5 * jax.random.normal(ks[4], (DEPTH, N_HEADS_GROUP), F32),
        'g_q_lora': gain(ks[5], (DEPTH, MLA_Q_RANK)),
        'w_q_up': nrm(ks[6], (DEPTH, MLA_Q_RANK, N_HEADS_GROUP * (MLA_NOPE_DIM + MLA_ROPE_DIM)), MLA_Q_RANK),
        'g_kv_lora': gain(ks[7], (DEPTH, MLA_KV_RANK)),
        'w_kv_up': nrm(ks[8], (DEPTH, MLA_KV_RANK, N_HEADS_GROUP * (MLA_NOPE_DIM + MLA_V_DIM)), MLA_KV_RANK),
        'g_mix_out': gain(ks[9], (DEPTH, D_MIX)),
        'w_out': nrm(ks[10], (DEPTH, D_MIX, D_MODEL), D_MIX),
        'g_mix_post': gain(ks[11], (DEPTH, D_MODEL)),
        'g_ffn_pre': gain(ks[12], (DEPTH, D_MODEL)),
        'w_ffn_up': nrm(ks[13], (DEPTH, D_MODEL, D_FF), D_MODEL),
        'w_ffn_down': nrm(ks[14], (DEPTH, D_FF, D_MODEL), D_FF),
        'g_ffn_post': gain(ks[15], (DEPTH, D_MODEL)),
    }


def reference(x, positions, g_mix_pre, w_in, b_forget, g_q_lora, w_q_up, g_kv_lora, w_kv_up,
              g_mix_out, w_out, g_mix_post, g_ffn_pre, w_ffn_up, w_ffn_down, g_ffn_post):
    for layer in range(DEPTH):
        h = rms_norm(x, g_mix_pre[layer])
        mix = hybrid_mixer(h, positions, w_in[layer], b_forget[layer], g_q_lora[layer], w_q_up[layer],
                           g_kv_lora[layer], w_kv_up[layer], g_mix_out[layer], w_out[layer])
        x = x + rms_norm(mix, g_mix_post[layer])
        h = rms_norm(x, g_ffn_pre[layer])
        x = x + rms_norm(squared_relu_mlp(h, w_ffn_up[layer], w_ffn_down[layer]), g_ffn_post[layer])
    return x
```

```python
import math
import numpy as np
import ml_dtypes
from contextlib import ExitStack
import concourse.bass as bass
import concourse.mybir as mybir
from concourse.bass_utils import run_bass_kernel_spmd

F32 = mybir.dt.float32
BF16 = mybir.dt.bfloat16
I32 = mybir.dt.int32
AF = mybir.ActivationFunctionType
ALU = mybir.AluOpType

COMPUTE = ("pe", "act", "dve", "pool", "sp")
N_DMA_SEMS = 48
SAME_ENGINE_SYNC = True

D = 1024
DFF = 4096
NIN = 2980
EPS = 1e-6
OFF = dict(fq=0, fk=256, fv=512, ff=768, cq=772, ckv=1028, kr=1156, rq=1188, rk=1444, rv=1700,
           rg=1956, sq=2212, sk=2468, sv=2724, rqp=2980, rkp=3236, krp=3492)
NINP = 3524


class _Op:
    __slots__ = ("fn", "deps", "raw", "ndma", "signal", "sem", "val", "clock", "queue")

    def __init__(self, fn, deps, ndma, queue, raw=()):
        self.fn = fn
        self.deps = deps
        self.raw = raw
        self.ndma = ndma
        self.signal = False
        self.sem = None
        self.val = 0
        self.clock = None
        self.queue = queue


class Sched:
    def __init__(self):
        self.ops = []
        self.lastw = {}
        self.readers = {}

    def add(self, eng, fn, reads=(), writes=(), ndma=0, force=False):
        import os
        mx = int(os.environ.get("MAX_OPS", "0"))
        if mx and len(self.ops) >= mx and not force:
            return -1
        i = len(self.ops)
        deps = set()
        if any(isinstance(t, str) and t.startswith("ps") for t in reads):
            writes = list(writes) + [t for t in reads if isinstance(t, str) and t.startswith("ps") and t not in writes]
            reads = [t for t in reads if not (isinstance(t, str) and t.startswith("ps"))]
        raw = set()
        for t in reads:
            w = self.lastw.get(t)
            if w is not None:
                deps.add(w)
                raw.add(w)
        for t in writes:
            w = self.lastw.get(t)
            if w is not None:
                deps.add(w)
            r = self.readers.get(t)
            if r:
                deps.update(r)
        for t in reads:
            self.readers.setdefault(t, []).append(i)
        for t in writes:
            self.lastw[t] = i
            self.readers[t] = []
        self.ops.append(_Op(fn, deps, ndma, eng, raw))
        return i

    def emit(self, nc, stack):
        ops = self.ops
        queues = {}
        for i, op in enumerate(ops):
            queues.setdefault(op.queue, []).append(i)
        esem = {q: stack.enter_context(nc.semaphore("s_" + q)) for q in COMPUTE}
        dsems = [stack.enter_context(nc.semaphore("d_%d" % k)) for k in range(N_DMA_SEMS)]
        dcount = [0] * N_DMA_SEMS
        dlast = [None] * N_DMA_SEMS
        N_SW = 8
        kk = {"pool": 0, "sp": 0}
        for i, op in enumerate(ops):
            if op.ndma:
                if op.queue == "pool":
                    k = kk["pool"] % N_SW
                    kk["pool"] += 1
                else:
                    k = N_SW + kk["sp"] % (N_DMA_SEMS - N_SW)
                    kk["sp"] += 1
                op.sem = ("d", k)
                if dlast[k] is not None:
                    op.deps.add(dlast[k])
                dlast[k] = i
                dcount[k] += 16 * op.ndma
                op.val = dcount[k]
                op.signal = True

        def skip(dop, op):
            if dop.ndma or op.ndma or dop.queue != op.queue:
                return False
            return dop.queue == "pe" or not SAME_ENGINE_SYNC

        opidx = {id(o): i for i, o in enumerate(ops)}

        for op in ops:
            for d in op.deps:
                dop = ops[d]
                if dop.ndma or skip(dop, op):
                    continue
                dop.signal = True
        cnt = {q: 0 for q in COMPUTE}
        for op in ops:
            if not op.ndma:
                if op.signal:
                    cnt[op.queue] += 1
                op.sem = ("e", op.queue)
                op.val = cnt[op.queue]
        kn = {q: {} for q in queues}
        for op in ops:
            kq = kn[op.queue]
            for d in op.deps:
                dop = ops[d]
                if kq.get(dop.sem, 0) < dop.val:
                    kq[dop.sem] = dop.val
                if dop.clock:
                    for s, v in dop.clock.items():
                        if kq.get(s, 0) < v:
                            kq[s] = v
            if op.signal:
                c = dict(kq)
                c[op.sem] = op.val
                op.clock = c

        def semobj(s):
            return esem[s[1]] if s[0] == "e" else dsems[s[1]]

        block = stack.enter_context(nc.Block())
        self.nwaits = 0

        def run_queue(q, eng):
            known = {}
            for i in queues[q]:
                op = ops[i]
                need = {}
                for d in op.deps:
                    dop = ops[d]
                    if skip(dop, op):
                        continue
                    if known.get(dop.sem, 0) >= dop.val:
                        continue
                    if need.get(dop.sem, 0) < dop.val:
                        need[dop.sem] = dop.val
                for d in op.deps:
                    dop = ops[d]
                    if skip(dop, op):
                        continue
                    if dop.clock:
                        for s, v in dop.clock.items():
                            if known.get(s, 0) < v:
                                known[s] = v
                for s, v in need.items():
                    eng.wait_ge(semobj(s), v)
                    self.nwaits += 1
                    if known.get(s, 0) < v:
                        known[s] = v
                ins = op.fn(eng)
                if op.ndma:
                    lst = ins if isinstance(ins, (list, tuple)) else [ins]
                    assert len(lst) == op.ndma
                    for x_ in lst:
                        x_.then_inc(semobj(op.sem), 16)
                elif op.signal:
                    ins.then_inc(semobj(op.sem), 1)

        def mk(q):
            return lambda eng: run_queue(q, eng)

        handlers = {"pe": block.tensor, "act": block.scalar, "dve": block.vector,
                    "pool": block.gpsimd, "sp": block.sync}
        for q in queues:
            handlers[q](mk(q))


def host_consts():
    c = {}
    bf = ml_dtypes.bfloat16
    c["c_ident"] = np.eye(128, dtype=np.float32).astype(bf)
    kk = np.arange(128)[:, None, None]
    jj = np.arange(4)[None, :, None]
    qq = np.arange(512)[None, None, :]
    key = 128 * jj + kk
    m = np.zeros((128, 3, 4, 512), np.float32)
    m[:, 0] = (key <= qq)
    m[:, 1] = ((key // 64) <= (qq // 64))
    m[:, 2] = (key < qq)
    c["c_masks"] = m.astype(bf)
    j = np.arange(128)[:, None]
    s = np.arange(128)[None, :]
    c["c_uincl"] = (-(j >= s).astype(np.float32)).astype(bf)
    c["c_nlower"] = (-(j < s).astype(np.float32)).astype(bf)
    h = np.arange(4, dtype=np.float32)
    log_gamma = np.log1p(-np.power(np.float32(2.0), np.float32(-5.0) - h)).astype(np.float32)
    idx = np.arange(64, dtype=np.float32)
    m_ = np.arange(128)
    dec = np.zeros((128, 4, 128), np.float32)
    for hh in range(4):
        dd = np.exp(log_gamma[hh] * np.abs(m_[:, None] - m_[None, :]).astype(np.float32)).astype(np.float32)
        same = (m_[:, None] // 64) == (m_[None, :] // 64)
        dec[:, hh, :] = np.where(same, dd, 0.0)
    c["c_rdec"] = np.tile(dec[:, :, None, :], (1, 1, 4, 1)).reshape(128, 4, 512).astype(np.float32)
    tail = np.exp(log_gamma[None, :] * (63.0 - idx)[:, None]).astype(np.float32)
    c["c_tail"] = np.tile(tail, (2, 1)).astype(np.float32)
    qh = np.exp(log_gamma[None, :] * (idx + 1.0)[:, None]).astype(np.float32)
    qd = np.tile(qh.T[None, :, None, :], (128, 1, 8, 1)).reshape(128, 4, 512)
    c["c_qdec"] = qd.astype(np.float32)
    a_ = np.exp(log_gamma * np.float32(64.0)).astype(np.float32)
    ad = np.zeros((128, 2), np.float32)
    for hp_ in range(2):
        ad[0:64, hp_] = a_[2 * hp_]
        ad[64:128, hp_] = a_[2 * hp_ + 1]
    c["c_adec"] = ad
    r = np.arange(128)
    invf_r = (np.float32(10000.0) ** (-(r % 32).astype(np.float32) / np.float32(32))).astype(np.float32)
    invf_m = (np.float32(10000.0) ** (-(r % 16).astype(np.float32) / np.float32(16))).astype(np.float32)
    sg_r = np.where((r % 64) < 32, -1.0, 1.0).astype(np.float32)
    sg_m = np.where((r % 32) < 16, -1.0, 1.0).astype(np.float32)
    c["c_rope"] = np.stack([invf_r, invf_m, sg_r, sg_m], axis=1).astype(np.float32)
    return c


CONST_SHAPES = dict(c_ident=([128, 128], BF16), c_masks=([128, 3, 4, 512], BF16), c_uincl=([128, 128], BF16), c_nlower=([128, 128], BF16),
                    c_rdec=([128, 4, 512], F32), c_tail=([128, 4], F32), c_qdec=([128, 4, 512], F32),
                    c_adec=([128, 2], F32), c_rope=([128, 4], F32))


def build(S, dbg=False, nlayers=2):
    NT = S // 512
    NB = S // 128
    NCH = S // 64
    nc = bass.Bass("TRN2", target_bir_lowering=False)
    sch = Sched()

    def din(name, shape, dt=F32):
        return nc.dram_tensor(name, shape, dt, kind="ExternalInput").ap()

    def dscr(name, shape, dt=F32):
        return nc.dram_tensor(name, shape, dt, kind=("ExternalOutput" if dbg else "Internal")).ap()

    x_in = din("x", [S, D])
    pos = din("pos", [1, S], I32)
    w_in = din("w_in", [2, D, NIN])
    w_q_up = din("w_q_up", [2, 256, 384])
    w_kv_up = din("w_kv_up", [2, 128, 512])
    w_out = din("w_out", [2, D, D])
    w_up = din("w_ffn_up", [2, D, DFF])
    w_dn = din("w_ffn_down", [2, DFF, D])
    g_pre = din("g_mix_pre", [2, D])
    g_post = din("g_mix_post", [2, D])
    g_fpre = din("g_ffn_pre", [2, D])
    g_fpost = din("g_ffn_post", [2, D])
    gq_col = din("gq_col", [2, 128, 2])
    gkv_col = din("gkv_col", [2, 128, 1])
    gmo_col = din("gmo_col", [2, 128, 16])
    bf_col = din("bf_col", [2, 4, 1])
    cst = {k: din(k, sh, dt) for k, (sh, dt) in CONST_SHAPES.items()}
    y_out = nc.dram_tensor("y", [S, D], F32, kind="ExternalOutput").ap()

    COSR = dscr("COSR", [128, S]); SINR = dscr("SINR", [128, S])
    COSM = dscr("COSM", [128, S]); SINM = dscr("SINM", [128, S])
    QF = dscr("QF", [4, 67, S], BF16); KF = dscr("KF", [4, 67, S], BF16)
    VF = dscr("VF", [S, 256], BF16); FFL = dscr("FFL", [4, S])
    CNEG = dscr("CNEG", [S, 4])
    QM = dscr("QM", [4, 96, S], BF16); KM = dscr("KM", [4, 96, S], BF16); VM = dscr("VM", [S, 256], BF16)
    QR = dscr("QR", [4, 64, S], BF16); KR = dscr("KR", [4, 64, S], BF16)
    KRT = dscr("KRT", [S, 4, 64], BF16); VR = dscr("VR", [S, 256], BF16); RG = dscr("RG", [256, S])
    QS = dscr("QS", [4, 64, S], BF16); KS = dscr("KS", [4, 64, S], BF16); VS = dscr("VS", [S, 256], BF16)
    OH = dscr("OH", [4, 4, 64, S])
    MIXT = dscr("MIXT", [D, S], BF16)
    X1 = dscr("X1", [S, D])
    X2 = dscr("X2", [S, D])

    st = ExitStack()
    with st:
        def sb(name, shape, dt):
            return st.enter_context(nc.sbuf_tensor("sb_" + name, shape, dt))

        ident = sb("ident", [128, 128], BF16)
        identf = sb("identf", [128, 128], F32)
        uincl = sb("uincl", [128, 128], BF16)
        nlower = sb("nlower", [128, 128], BF16)
        onesb = sb("onesb", [128, 128], BF16)
        onesf = sb("onesf", [128, 128], F32)
        ones64 = sb("ones64", [128, 64], F32)
        c_tail = sb("c_tail", [128, 4], F32)
        c_adec = sb("c_adec", [128, 2], F32)
        c_rope = sb("c_rope", [128, 4], F32)
        gqc = sb("gqc", [128, 2], F32)
        gkvc = sb("gkvc", [128, 1], F32)
        gmoc = sb("gmoc", [128, 16], F32)
        bfc = sb("bfc", [4, 1], F32)
        small = sb("small", [128, 64], F32)
        ARENA = 150 * 1024
        arena = sb("arena", [128, ARENA], mybir.dt.uint8)
        NW = 6
        wk32 = [sb("wk32_%d" % i, [128, 512], F32) for i in range(NW)]
        NWB = 12
        wkbf = [sb("wkbf_%d" % i, [128, 512], BF16) for i in range(NWB)]
        xt = [sb("xt_%d" % i, [128, 1024], F32) for i in range(2)]
        yt = [sb("yt_%d" % i, [128, 1024], F32) for i in range(2)]
        hbf = [sb("hbf_%d" % i, [128, 1024], BF16) for i in range(2)]
        gb = [sb("gb_%d" % i, [128, 1024], F32) for i in range(2)]
        PS = [st.enter_context(nc.psum_tensor("ps%d" % i, [128, 512], F32)) for i in range(8)]

        class Carver:
            def __init__(self):
                self.off = 0

            def reset(self):
                self.off = 0

            def take(self, shape, dt):
                esz = {F32: 4, BF16: 2, I32: 4}[dt]
                n = 1
                for d_ in shape[1:]:
                    n *= d_
                nbytes = (n * esz + 63) // 64 * 64
                assert self.off + nbytes <= ARENA, (self.off, nbytes)
                v = arena[0:shape[0], self.off:self.off + n * esz].bitcast(dt)
                self.off += nbytes
                if len(shape) > 2:
                    names = " ".join("d%d" % i for i in range(1, len(shape)))
                    kw = {"d%d" % i: shape[i] for i in range(1, len(shape))}
                    v = v.rearrange("p (%s) -> p %s" % (names, names), **kw)
                return v

        carve = Carver()
        phase_ctr = [0]

        def phase_barrier():
            phase_ctr[0] += 1
            sch.add("pool", lambda e: e.memset(small[:, 63:64], 0.0), reads=["small63"], writes=["ARENA", "small63"])

        AR = ["ARENA"]

        DRAM_TOKS = set(["QF", "QFc", "KF", "fv_dst", "mla_dst", "rv_dst", "sv_dst", "FFL", "CNEG", "QM", "KM", "QR", "KR",
                         "KRT", "RG", "QS", "KS", "OH0", "OH1", "OH2", "OH3", "MIXT", "X1", "X2", "Y",
                         "COS0", "COS1", "SIN0", "SIN1"] + ["KF1_%d" % j for j in range(4)])

        def dma(q, out, in_, reads=(), writes=()):
            r2 = [t for t in reads if t not in DRAM_TOKS] + [t for t in writes if t in DRAM_TOKS]
            w2 = [t for t in writes if t not in DRAM_TOKS] + [t for t in reads if t in DRAM_TOKS]
            sch.add(q, lambda e: e.dma_start(out=out, in_=in_), r2, w2, ndma=1)

        def mm(out, lhsT, rhs, start, stop, reads, writes, sgc=False):
            if sgc:
                sch.add("pe", lambda e: e.matmul(out, lhsT=lhsT, rhs=rhs, start=start, stop=stop, skip_group_check=True),
                        reads, writes)
            else:
                sch.add("pe", lambda e: e.matmul(out, lhsT=lhsT, rhs=rhs, start=start, stop=stop), reads, writes)

        def tr(out, in_, idn, reads, writes):
            sch.add("pe", lambda e: e.transpose(out=out, in_=in_, identity=idn), reads, writes)

        def act(out, in_, func, reads, writes, bias=None, scale=None, accum=None):
            kw = {}
            if bias is not None:
                kw["bias"] = bias
            if scale is not None:
                kw["scale"] = scale
            if accum is not None:
                kw["accum_out"] = accum
            sch.add("act", lambda e: e.activation(out=out, in_=in_, func=func, **kw), reads, writes)

        def tt(eng, out, in0, in1, op, reads, writes):
            sch.add(eng, lambda e: e.tensor_tensor(out=out, in0=in0, in1=in1, op=op), reads, writes)

        def ts(eng, out, in0, s1, op0, reads, writes, s2=None, op1=None):
            if op1 is None:
                sch.add(eng, lambda e: e.tensor_scalar(out=out, in0=in0, scalar1=s1, scalar2=None, op0=op0), reads, writes)
            else:
                sch.add(eng, lambda e: e.tensor_scalar(out=out, in0=in0, scalar1=s1, scalar2=s2, op0=op0, op1=op1),
                        reads, writes)

        def stt(eng, out, in0, scalar, in1, op0, op1, reads, writes):
            sch.add(eng, lambda e: e.scalar_tensor_tensor(out=out, in0=in0, scalar=scalar, in1=in1, op0=op0, op1=op1),
                    reads, writes)

        def cp(eng, out, in_, reads, writes):
            if eng == "act":
                act(out, in_, AF.Copy, reads, writes)
            else:
                sch.add(eng, lambda e: e.tensor_copy(out=out, in_=in_), reads, writes)

        def recip(out, in_, reads, writes):
            sch.add("dve", lambda e: e.reciprocal(out=out, in_=in_), reads, writes)

        def memset(eng, ap, val, writes):
            sch.add(eng, lambda e: e.memset(ap, val), (), writes)

        rr = {"w32": 0, "wbf": 0, "psA": 0, "ev": 0, "psX3": 0}

        def nxt(key, n):
            v = rr[key]
            rr[key] = (v + 1) % n
            return v

        def w32():
            i = nxt("w32", NW)
            return wk32[i], "wk32_%d" % i

        def wbf():
            i = nxt("wbf", NWB)
            return wkbf[i], "wkbf_%d" % i

        def evq():
            return ("act", "dve")[nxt("ev", 2)]

        def rstd_col(col_ap, tok, n_inv):
            act(col_ap, col_ap, AF.Ln, [tok], [tok], bias=EPS, scale=n_inv)
            act(col_ap, col_ap, AF.Exp, [tok], [tok], scale=-0.5)

        dma("sp", ident[:], cst["c_ident"], (), ["ident"])
        dma("sp", uincl[:], cst["c_uincl"], (), ["uincl"])
        dma("sp", nlower[:], cst["c_nlower"], (), ["nlower"])
        dma("sp", c_tail[:], cst["c_tail"], (), ["c_tail"])
        dma("sp", c_adec[:], cst["c_adec"], (), ["c_adec"])
        dma("sp", c_rope[:], cst["c_rope"], (), ["c_rope"])
        memset("pool", onesb[:], 1.0, ["onesb"])
        memset("pool", onesf[:], 1.0, ["onesf"])
        memset("pool", ones64[:], 1.0 / 64.0, ["ones64"])
        memset("pool", small[:], 0.0, ["small63"])
        cp("dve", identf[:], ident[:], ["ident"], ["identf"])
        carve.reset()
        ob = carve.take([128, S], BF16)
        sch.add("pool", lambda e: e.memset(ob, 1.0), AR, ["ob"])
        for h in range(4):
            dma("sp", KF[h, 64:67, :], ob[0:3, :], ["ob"] + AR, ["KF1_%d" % h])
        posi = carve.take([128, S], I32)
        posf = carve.take([128, S], F32)
        ang = carve.take([128, S], F32)
        kf_ = carve.take([128, S], F32)
        ki_ = carve.take([128, S], I32)
        dma("sp", posi, pos.partition_broadcast(128), AR, ["posi"])
        cp("dve", posf, posi, ["posi"] + AR, ["posf"])
        TWO_PI = 2.0 * math.pi
        C1 = 6.28125
        C2 = TWO_PI - C1
        for ti, (COS, SIN) in enumerate(((COSR, SINR), (COSM, SINM))):
            ts("dve", ang, posf, c_rope[:, ti:ti + 1], ALU.mult, ["posf", "c_rope"] + AR, ["ang"])
            for which in (0, 1):
                ts("dve", ki_, ang, 1.0 / TWO_PI, ALU.mult, ["ang"] + AR, ["ki"])
                cp("dve", kf_, ki_, ["ki"] + AR, ["kf"])
                stt("dve", posi.bitcast(F32), kf_, -C1, ang, ALU.mult, ALU.add, ["kf", "ang"] + AR, ["red"])
                red = posi.bitcast(F32)
                stt("dve", red, kf_, -C2, red, ALU.mult, ALU.add, ["kf", "red"] + AR, ["red"])
                if which == 1:
                    ts("dve", red, red, math.pi / 2.0, ALU.add, ["red"] + AR, ["red"])
                    ts("dve", kf_, red, math.pi, ALU.is_gt, ["red"] + AR, ["kf"])
                    stt("dve", red, kf_, -TWO_PI, red, ALU.mult, ALU.add, ["kf", "red"] + AR, ["red"])
                ts("dve", red, red, 3.141592, ALU.min, ["red"] + AR, ["red"], s2=-3.141592, op1=ALU.max)
                act(kf_, red, AF.Sin, ["red"] + AR, ["kf"])
                if which == 0:
                    ts("dve", kf_, kf_, c_rope[:, 2 + ti:3 + ti], ALU.mult, ["kf", "c_rope"] + AR, ["kf"])
                    dma("sp", SIN, kf_, ["kf"] + AR, ["SIN%d" % ti])
                else:
                    dma("sp", COS, kf_, ["kf"] + AR, ["COS%d" % ti])
        phase_barrier()

        for L in range(nlayers):
            if dbg == "setup":
                break
            x_src = x_in if L == 0 else X2
            x_dst = y_out if L == nlayers - 1 else X2
            dma("sp", gqc[:], gq_col[L], (), ["gqc"])
            dma("sp", gkvc[:], gkv_col[L], (), ["gkvc"])
            dma("sp", gmoc[:], gmo_col[L], (), ["gmoc"])
            dma("sp", bfc[:], bf_col[L], (), ["bfc"])

            carve.reset()
            winb = carve.take([128, 8, NINP], BF16)
            wq32 = carve.take([128, 2, 384], F32)
            wqn = carve.take([128, 2, 256], BF16)
            wqr = carve.take([128, 2, 128], BF16)
            wqp = carve.take([128, 2, 128], BF16)
            wkv32 = carve.take([128, 512], F32)
            wkn = carve.take([128, 256], BF16)
            wvv = carve.take([128, 256], BF16)
            hT = [carve.take([128, 8, 512], BF16) for _ in range(2)]
            tabs = [[carve.take([128, 512], F32) for _ in range(4)] for _ in range(2)]
            xt4 = [carve.take([128, 1024], F32) for _ in range(4)]
            cq32 = [carve.take([128, 512], F32) for _ in range(2)]
            sq32 = [carve.take([128, 512], F32) for _ in range(3)]
            cqn = [carve.take([128, 512], BF16) for _ in range(2)]
            ckvn = carve.take([128, 512], BF16)
            rstdb = carve.take([128, 512], F32)
            vstg = [carve.take([128, 4, 256], BF16) for _ in range(3)]
            krtstg = carve.take([128, 4, 256], BF16)
            c_qdummy = None

            for kc in range(8):
                rows = slice(kc * 128, (kc + 1) * 128)
                dma("pool", winb[:, kc, 0:NIN], w_in[L, rows, :], AR, ["winb%d" % kc])
                for nm, nmp, nh, hw in (("rq", "rqp", 4, 32), ("rk", "rkp", 4, 32), ("kr", "krp", 1, 16)):
                    src = winb[:, kc, OFF[nm]:OFF[nm] + nh * 2 * hw].rearrange("p (h t c) -> p h t c", h=nh, t=2)
                    dst = winb[:, kc, OFF[nmp]:OFF[nmp] + nh * 2 * hw].rearrange("p (h t c) -> p h t c", h=nh, t=2)
                    cp("pool", dst[:, :, 0, :], src[:, :, 1, :], ["winb%d" % kc] + AR, ["winbp%d" % kc])
                    cp("pool", dst[:, :, 1, :], src[:, :, 0, :], ["winb%d" % kc] + AR, ["winbp%d" % kc])
            if dbg == "p1a":
                break
            dma("sp", wq32, w_q_up[L].rearrange("(c p) n -> p c n", p=128), AR, ["wq32"])
            dma("sp", wkv32, w_kv_up[L], AR, ["wkv32"])
            for c in range(2):
                src4 = wq32[:, c, :].rearrange("p (h e) -> p h e", h=4)
                gcol = gqc[:, c:c + 1]
                ts("dve", wqn[:, c, :].rearrange("p (h e) -> p h e", h=4), src4[:, :, 0:64], gcol, ALU.mult,
                   ["wq32", "gqc"] + AR, ["wqb"])
                ts("dve", wqr[:, c, :].rearrange("p (h e) -> p h e", h=4), src4[:, :, 64:96], gcol, ALU.mult,
                   ["wq32", "gqc"] + AR, ["wqb"])
                dstp = wqp[:, c, :].rearrange("p (h t e) -> p h t e", h=4, t=2)
                ts("dve", dstp[:, :, 0, :], src4[:, :, 80:96], gcol, ALU.mult, ["wq32", "gqc"] + AR, ["wqb"])
                ts("dve", dstp[:, :, 1, :], src4[:, :, 64:80], gcol, ALU.mult, ["wq32", "gqc"] + AR, ["wqb"])
            kv4 = wkv32.rearrange("p (h e) -> p h e", h=4)
            ts("dve", wkn.rearrange("p (h e) -> p h e", h=4), kv4[:, :, 0:64], gkvc[:, 0:1], ALU.mult,
               ["wkv32", "gkvc"] + AR, ["wkvb"])
            ts("dve", wvv.rearrange("p (h e) -> p h e", h=4), kv4[:, :, 64:128], gkvc[:, 0:1], ALU.mult,
               ["wkv32", "gkvc"] + AR, ["wkvb"])
            dma("sp", gb[0][:], g_pre[L:L + 1, :].partition_broadcast(128), (), ["gb0"])

            WIN_ALL = ["winb%d" % k_ for k_ in range(8)] + ["winbp%d" % k_ for k_ in range(8)]

            def fm_group(hTt, hTtok, col_lo, M, ncols_stride=None):
                b = nxt("psA", 4)
                ps = PS[b]
                for kc in range(8):
                    mm(ps[0:M, :], winb[:, kc, col_lo:col_lo + M], hTt[:, kc, :], kc == 0, kc == 7,
                       ["winb%d" % kc, "winbp%d" % kc, hTtok] + AR, ["ps%d" % b])
                return ps, "ps%d" % b

            def store_rows(stg, stok, dsts, tsl):
                for (r0, r1, dram_rows, wtok) in dsts:
                    dma("sp", dram_rows[:, tsl], stg[r0:r1, :], [stok], [wtok])

            def load_tile_inputs(t):
                tsl_ = slice(t * 512, (t + 1) * 512)
                p_ = t % 2
                for sub in range(4):
                    r0 = t * 512 + sub * 128
                    dma("pool", xt4[sub], x_src[r0:r0 + 128, :], ["X2"] + AR, ["xt4_%d" % sub])
                for j, (SRC, tk) in enumerate(((COSR, "COS0"), (SINR, "SIN0"), (COSM, "COS1"), (SINM, "SIN1"))):
                    dma("pool", tabs[p_][j], SRC[:, tsl_], [tk] + AR, ["tab%d_%d" % (p_, j)])

            def norm_tile(t):
                hTt = hT[t % 2]
                hTtok = "hT%d" % (t % 2)
                p_ = t % 2
                for sub in range(4):
                    xi = sub % 2
                    xs = xt4[sub]
                    xtok = "xt4_%d" % sub
                    col = small[:, sub:sub + 1]
                    ctok = "sm%d" % sub
                    act(hbf[xi][:], xs, AF.Square, [xtok] + AR, ["hbf%d" % xi, ctok], accum=col)
                    rstd_col(col, ctok, 1.0 / D)
                    stt("dve", hbf[xi][:], xs, col, gb[0][:], ALU.mult, ALU.mult,
                        [xtok, ctok, "gb0"] + AR, ["hbf%d" % xi])
                    psT = PS[4][:].bitcast(BF16)
                    for kc in range(8):
                        tr(psT[:, kc * 128:(kc + 1) * 128], hbf[xi][:, kc * 128:(kc + 1) * 128], ident[:],
                           ["hbf%d" % xi, "ident"], ["ps4"])
                    cp("act" if sub % 2 else "dve", hTt[:, :, sub * 128:(sub + 1) * 128],
                       psT.rearrange("p (k c) -> p k c", k=8), ["ps4"] + AR, [hTtok])

            load_tile_inputs(0)
            norm_tile(0)
            for t in range(NT):
                tsl = slice(t * 512, (t + 1) * 512)
                hTt = hT[t % 2]
                hTtok = "hT%d" % (t % 2)
                if t + 1 < NT:
                    load_tile_inputs(t + 1)
                cr, sr, cm, sm = tabs[t % 2]
                crk, srk, cmk, smk = ["tab%d_%d" % (t % 2, j) for j in range(4)]
                for c in range(2):
                    ps, ptok = fm_group(hTt, hTtok, OFF["cq"] + c * 128, 128)
                    act(sq32[c], ps[:], AF.Square, [ptok] + AR, ["sq32_%d" % c])
                    cp("dve", cq32[c], ps[:], [ptok] + AR, ["cq32_%d" % c])
                for c in range(2):
                    mm(PS[6][:], onesf[:], sq32[c], c == 0, c == 1, ["onesf", "sq32_%d" % c] + AR, ["ps6"])
                act(rstdb, PS[6][:], AF.Ln, ["ps6"] + AR, ["rstdb"], bias=EPS, scale=1.0 / 256.0)
                act(rstdb, rstdb, AF.Exp, ["rstdb"] + AR, ["rstdb"], scale=-0.5)
                for c in range(2):
                    tt("pool" if c else "dve", cqn[c], cq32[c], rstdb, ALU.mult, ["cq32_%d" % c, "rstdb"] + AR, ["cqn%d" % c])
                ps, ptok = fm_group(hTt, hTtok, OFF["ckv"], 128)
                act(sq32[2], ps[:], AF.Square, [ptok] + AR, ["sq32_2"])
                cp("dve", cq32[0], ps[:], [ptok] + AR, ["cq32_0"])
                mm(PS[6][:], onesf[:], sq32[2], True, True, ["onesf", "sq32_2"] + AR, ["ps6"])
                act(rstdb, PS[6][:], AF.Ln, ["ps6"] + AR, ["rstdb"], bias=EPS, scale=1.0 / 128.0)
                act(rstdb, rstdb, AF.Exp, ["rstdb"] + AR, ["rstdb"], scale=-0.5)
                tt("dve", ckvn, cq32[0], rstdb, ALU.mult, ["cq32_0", "rstdb"] + AR, ["ckvn"])
                def simple_pair(col_lo, scale, DST, wtok):
                    for hp in range(2):
                        ps, ptok = fm_group(hTt, hTtok, col_lo + hp * 128, 128)
                        stg, stok = wbf()
                        if scale is None:
                            cp(evq(), stg[:], ps[:], [ptok], [stok])
                        else:
                            act(stg[:], ps[:], AF.Copy, [ptok], [stok], scale=scale)
                        store_rows(stg, stok, [(0, 64, DST[2 * hp, 0:64], wtok), (64, 128, DST[2 * hp + 1, 0:64], wtok)], tsl)

                simple_pair(OFF["fq"], 0.125, QF, "QF")
                simple_pair(OFF["fk"], None, KF, "KF")
                simple_pair(OFF["sq"], 0.125, QS, "QS")
                simple_pair(OFF["sk"], None, KS, "KS")
                ps, ptok = fm_group(hTt, hTtok, OFF["ff"], 4)
                stg, stok = w32()
                cp("dve", stg[0:4, :], ps[0:4, :], [ptok], [stok])
                dma("sp", FFL[:, tsl], stg[0:4, :], [stok], ["FFL"])
                for c in range(2):
                    ps, ptok = fm_group(hTt, hTtok, OFF["rg"] + c * 128, 128)
                    stg, stok = w32()
                    act(stg[:], ps[:], AF.Silu, [ptok], [stok])
                    dma("sp", RG[c * 128:(c + 1) * 128, tsl], stg[:], [stok], ["RG"])
                for nm, nmp, DST, wtok, scl in (("rq", "rqp", QR, "QR", 1.0), ("rk", "rkp", KR, "KR", 0.125)):
                    for hp in range(2):
                        psa, pta = fm_group(hTt, hTtok, OFF[nm] + hp * 128, 128)
                        psb, ptb = fm_group(hTt, hTtok, OFF[nmp] + hp * 128, 128)
                        t1, k1 = w32()
                        t2, k2 = w32()
                        stt("dve", t1[:], psa[:], scl, cr, ALU.mult, ALU.mult, [pta, crk] + AR, [k1])
                        stt("dve", t2[:], psb[:], scl, sr, ALU.mult, ALU.mult, [ptb, srk] + AR, [k2])
                        stg, stok = wbf()
                        tt("pool", stg[:], t1[:], t2[:], ALU.add, [k1, k2], [stok])
                        store_rows(stg, stok, [(0, 64, DST[2 * hp], wtok), (64, 128, DST[2 * hp + 1], wtok)], tsl)
                        if nm == "rk":
                            psT = PS[5][:].bitcast(BF16)
                            for sub in range(4):
                                tr(psT[:, sub * 128:(sub + 1) * 128], stg[:, sub * 128:(sub + 1) * 128], ident[:],
                                   [stok, "ident"], ["ps5"])
                            for hh in range(2):
                                h = 2 * hp + hh
                                ts("dve", krtstg[:, :, h * 64:(h + 1) * 64],
                                   psT[:, 0:512].rearrange("p (s a d) -> p s a d", s=4, a=2)[:, :, hh, :],
                                   c_tail[:, h:h + 1], ALU.mult, ["ps5", "c_tail"] + AR, ["krtstg"])
                            if hp == 1:
                                dma("sp", KRT[tsl, :, :].rearrange("(s p) h d -> p s (h d)", p=128), krtstg,
                                    ["krtstg"] + AR, ["KRT"])
                if t + 1 < NT:
                    norm_tile(t + 1)
                qsc = 96.0 ** -0.5
                for hp in range(2):
                    b = nxt("psA", 4)
                    for c in range(2):
                        mm(PS[b][:], wqn[:, c, hp * 128:(hp + 1) * 128], cqn[c], c == 0, c == 1,
                           ["wqb", "cqn%d" % c] + AR, ["ps%d" % b])
                    stg, stok = wbf()
                    act(stg[:], PS[b][:], AF.Copy, ["ps%d" % b], [stok], scale=qsc)
                    store_rows(stg, stok, [(0, 64, QM[2 * hp, 0:64], "QM"), (64, 128, QM[2 * hp + 1, 0:64], "QM")], tsl)
                ba = nxt("psA", 4)
                for c in range(2):
                    mm(PS[ba][:], wqr[:, c, :], cqn[c], c == 0, c == 1, ["wqb", "cqn%d" % c] + AR, ["ps%d" % ba])
                bb = nxt("psA", 4)
                for c in range(2):
                    mm(PS[bb][:], wqp[:, c, :], cqn[c], c == 0, c == 1, ["wqb", "cqn%d" % c] + AR, ["ps%d" % bb])
                t1, k1 = w32()
                t2, k2 = w32()
                stt("dve", t1[:], PS[ba][:], qsc, cm, ALU.mult, ALU.mult, ["ps%d" % ba, cmk] + AR, [k1])
                stt("dve", t2[:], PS[bb][:], qsc, sm, ALU.mult, ALU.mult, ["ps%d" % bb, smk] + AR, [k2])
                stg, stok = wbf()
                tt("pool", stg[:], t1[:], t2[:], ALU.add, [k1, k2], [stok])
                store_rows(stg, stok, [(32 * h, 32 * h + 32, QM[h, 64:96], "QM") for h in range(4)], tsl)
                for hp in range(2):
                    b = nxt("psA", 4)
                    mm(PS[b][:], wkn[:, hp * 128:(hp + 1) * 128], ckvn, True, True, ["wkvb", "ckvn"] + AR, ["ps%d" % b])
                    stg, stok = wbf()
                    cp(evq(), stg[:], PS[b][:], ["ps%d" % b], [stok])
                    store_rows(stg, stok, [(0, 64, KM[2 * hp, 0:64], "KM"), (64, 128, KM[2 * hp + 1, 0:64], "KM")], tsl)
                psa, pta = fm_group(hTt, hTtok, OFF["kr"], 32)
                psb, ptb = fm_group(hTt, hTtok, OFF["krp"], 32)
                t1, k1 = w32()
                t2, k2 = w32()
                tt("dve", t1[0:32, :], psa[0:32, :], cm[0:32, :], ALU.mult, [pta, cmk] + AR, [k1])
                tt("dve", t2[0:32, :], psb[0:32, :], sm[0:32, :], ALU.mult, [ptb, smk] + AR, [k2])
                stg, stok = wbf()
                tt("pool", stg[0:32, :], t1[0:32, :], t2[0:32, :], ALU.add, [k1, k2], [stok])
                store_rows(stg, stok, [(0, 32, KM[h, 64:96], "KM") for h in range(4)], tsl)
                for vi, (nm, DSTv) in enumerate((("fv", None), ("rv", None), ("sv", None), ("mla", None))):
                    vs = vstg[vi % 3]
                    vtok = "vstg%d" % (vi % 3)
                    for sub in range(4):
                        b = 6 + (sub % 2)
                        if nm == "mla":
                            mm(PS[b][:, 0:256], ckvn[:, sub * 128:(sub + 1) * 128], wvv, True, True,
                               ["ckvn", "wkvb"] + AR, ["ps%d" % b])
                        else:
                            for kc in range(8):
                                mm(PS[b][:, 0:256], hTt[:, kc, sub * 128:(sub + 1) * 128],
                                   winb[:, kc, OFF[nm]:OFF[nm] + 256], kc == 0, kc == 7,
                                   ["winb%d" % kc, hTtok] + AR, ["ps%d" % b])
                        cp(evq(), vs[:, sub, :], PS[b][:, 0:256], ["ps%d" % b] + AR, [vtok])
                    V_ = {"fv": VF, "mla": VM, "rv": VR, "sv": VS}[nm]
                    dma("sp", V_[tsl, :].rearrange("(s p) n -> p s n", p=128), vs, [vtok] + AR, [nm + "_dst"])
            phase_barrier()

            if dbg == "p1":
                break
            carve.reset()
            fl = carve.take([4, S], F32)
            ones4 = carve.take([4, S], F32)
            cc = carve.take([4, S], F32)
            r1 = carve.take([4, S], F32)
            chi = carve.take([4, S], BF16)
            cmid = carve.take([4, S], BF16)
            clo = carve.take([4, S], BF16)
            cneg = carve.take([128, NB, 4], F32)
            dma("sp", fl, FFL, ["FFL"] + AR, ["fl"])
            sch.add("pool", lambda e: e.memset(ones4, 1.0), AR, ["ones4"])
            act(fl, fl, AF.Identity, ["fl", "bfc"] + AR, ["fl"], bias=bfc[:, 0:1])
            act(fl, fl, AF.Exp, ["fl"] + AR, ["fl"], scale=-1.0)
            act(fl, fl, AF.Ln, ["fl"] + AR, ["fl"], bias=1.0)
            ts("dve", fl, fl, -1.0, ALU.mult, ["fl"] + AR, ["fl"])
            sch.add("dve", lambda e: e.tensor_tensor_scan(out=cc, data0=ones4, data1=fl, initial=0.0,
                                                          op0=ALU.mult, op1=ALU.add), ["fl", "ones4"] + AR, ["cc"])
            cp("dve", chi, cc, ["cc"] + AR, ["chi"])
            tt("dve", r1, cc, chi, ALU.subtract, ["cc", "chi"] + AR, ["r1"])
            cp("dve", cmid, r1, ["r1"] + AR, ["cmid"])
            tt("dve", r1, r1, cmid, ALU.subtract, ["r1", "cmid"] + AR, ["r1"])
            cp("dve", clo, r1, ["r1"] + AR, ["clo"])
            for h in range(4):
                for i, (src, tk) in enumerate(((chi, "chi"), (cmid, "cmid"), (clo, "clo"))):
                    dma("sp", QF[h, 64 + i:65 + i, :], src[h:h + 1, :], [tk] + AR, ["QFc"])
            for tb in range(NB):
                tr(PS[0][:, tb * 4:(tb + 1) * 4], cc[:, tb * 128:(tb + 1) * 128], identf[0:4, 0:4],
                   ["cc", "identf"] + AR, ["ps0"])
            ts("dve", cneg, PS[0][:, 0:NB * 4].rearrange("p (t h) -> p t h", h=4), -1.0, ALU.mult, ["ps0"] + AR, ["cneg"])
            dma("sp", CNEG.rearrange("(t p) h -> p t h", p=128), cneg, ["cneg"] + AR, ["CNEG"])
            phase_barrier()

            def softmax_attention(g, Qd, Kd, Vd, KD, mask_kind, use_bias):
                carve.reset()
                qT = [carve.take([128, S], BF16) for _ in range(2)]
                kT = [carve.take([128, S], BF16) for _ in range(2)]
                vA = [carve.take([128, NB, 128], BF16) for _ in range(2)]
                cng = carve.take([128, NB, 4], F32)
                msk = carve.take([128, 4, 512], BF16)
                dma("sp", msk, cst["c_masks"][:, mask_kind], AR, ["masks"])
                if use_bias:
                    dma("sp", cng, CNEG.rearrange("(t p) h -> p t h", p=128), ["CNEG"] + AR, ["cng"])
                LA = 3

                def load_head(h):
                    i2 = h % 2
                    dma("sp", qT[i2][0:KD, :], Qd[h], ["QF", "QFc", "QM"] + AR, ["qT%d" % i2])
                    dma("sp", kT[i2][0:KD, :], Kd[h], ["KF", "KM"] + ["KF1_%d" % j for j in range(4)] + AR, ["kT%d" % i2])
                    dma("sp", vA[i2][:, :, 0:64], Vd.rearrange("(t p) (h d) -> p t h d", p=128, h=4)[:, :, h, :],
                        ["fv_dst", "mla_dst"] + AR, ["vA%d" % i2])

                for i2_ in range(2):
                    sch.add("pool", (lambda t_: (lambda e: e.memset(t_[:, :, 64:128], 1.0)))(vA[i2_]), AR, ["vAones%d" % i2_])
                load_head(0)
                for h in range(4):
                    i2 = h % 2
                    if h + 1 < 4:
                        load_head(h + 1)
                    blocks = [(T, kb) for T in range(NT) for kb in range(4 * T + 4)]
                    st_ = {}

                    def stage_a(i):
                        T, kb = blocks[i]
                        qsl = slice(T * 512, (T + 1) * 512)
                        xb = nxt("psA", 4)
                        mm(PS[xb][:], kT[i2][0:KD, kb * 128:(kb + 1) * 128], qT[i2][0:KD, qsl], True, True,
                           ["kT%d" % i2, "qT%d" % i2] + AR, ["ps%d" % xb])
                        p_, ptok = wbf()
                        if use_bias:
                            act(p_[:], PS[xb][:], AF.Exp, ["ps%d" % xb, "cng"] + AR, [ptok], bias=cng[:, kb, h:h + 1])
                        else:
                            act(p_[:], PS[xb][:], AF.Exp, ["ps%d" % xb], [ptok])
                        if kb >= 4 * T:
                            tt("pool", p_[:], p_[:], msk[:, kb - 4 * T, :], ALU.mult, [ptok, "masks"] + AR, [ptok])
                        st_[i] = (p_, ptok)

                    def stage_b(i):
                        T, kb = blocks[i]
                        qsl = slice(T * 512, (T + 1) * 512)
                        nkb = 4 * T + 4
                        ob_ = 4 + (T % 2)
                        p_, ptok = st_.pop(i)
                        mm(PS[ob_][:], vA[i2][:, kb, :], p_[:], kb == 0, kb == nkb - 1,
                           ["vA%d" % i2, "vAones%d" % i2, ptok] + AR, ["ps%d" % ob_])
                        if kb == nkb - 1:
                            den, dtok = w32()
                            act(den[0:64, :], PS[ob_][64:128, :], AF.Copy, ["ps%d" % ob_], [dtok])
                            recip(den[0:64, :], den[0:64, :], [dtok], [dtok])
                            o_, otok = w32()
                            tt("dve", o_[0:64, :], PS[ob_][0:64, :], den[0:64, :], ALU.mult, ["ps%d" % ob_, dtok], [otok])
                            dma("sp", OH[g, h, :, qsl], o_[0:64, :], [otok], ["OH%d" % g])

                    n_ = len(blocks)
                    for i in range(n_ + LA):
                        if i < n_:
                            stage_a(i)
                        if i >= LA:
                            stage_b(i - LA)
                phase_barrier()

            softmax_attention(0, QF, KF, VF, 67, 0, True)
            softmax_attention(1, QM, KM, VM, 96, 1, False)

            carve.reset()
            qT = [carve.take([128, S], BF16) for _ in range(4)]
            kT = [carve.take([128, S], BF16) for _ in range(4)]
            nkT = [carve.take([128, S], BF16) for _ in range(4)]
            vS_ = carve.take([128, NB, 256], BF16)
            msk = carve.take([128, 4, 512], BF16)
            dma("sp", msk, cst["c_masks"][:, 2], AR, ["masks"])
            dma("sp", vS_, VS.rearrange("(t p) n -> p t n", p=128), ["sv_dst"] + AR, ["vS"])
            LA = 2
            NHL = 2 * 2 * (LA + 2)
            hl_pool = [carve.take([128, 512], BF16) for _ in range(NHL)]
            w_pool = [carve.take([128, 512], BF16) for _ in range(6)]
            rr["hl"] = 0
            rr["wp"] = 0

            def hl_tile():
                i_ = nxt("hl", NHL)
                return hl_pool[i_], "hl%d" % i_

            def w_tile():
                i_ = nxt("wp", 6)
                return w_pool[i_], "wp%d" % i_
            ACCB = (3, 6)
            OB = (4, 5)

            def load_head_sb(h):
                dma("sp", qT[h][0:64, :], QS[h], ["QS"] + AR, ["qT%d" % h])
                dma("sp", kT[h][0:64, :], KS[h], ["KS"] + AR, ["kT%d" % h])

            for h in range(4):
                e1, e2 = ("pool", "dve") if h % 2 == 0 else ("dve", "pool")
                sch.add(e1, (lambda t_: (lambda e: e.memset(t_[64:128, :], 0.0)))(qT[h]), AR, ["qTpad%d" % h])
                sch.add(e2, (lambda t_: (lambda e: e.memset(t_[64:128, :], 0.0)))(kT[h]), AR, ["kTpad%d" % h])
                sch.add(e1, (lambda t_: (lambda e: e.memset(t_[64:128, :], 0.0)))(nkT[h]), AR, ["nkTpad%d" % h])
                load_head_sb(h)
            for h in range(4):
                ts("dve" if h % 2 == 0 else "pool", nkT[h][0:64, :], kT[h][0:64, :], -1.0, ALU.mult,
                   ["kT%d" % h] + AR, ["nkT%d" % h])
            for hp in range(2):
                heads = (2 * hp, 2 * hp + 1)
                blocks = [(T, kb) for T in range(NT) for kb in range(4 * T + 3, -1, -1)]
                st_ = {}

                XB = (0, 1, 2, 7)

                def stage_z(i, c):
                    h = heads[c]
                    T, kb = blocks[i]
                    qsl = slice(T * 512, (T + 1) * 512)
                    xb = XB[(2 * i + c) % 4]
                    mm(PS[xb][:], kT[h][:, kb * 128:(kb + 1) * 128], qT[h][:, qsl], True, True,
                       ["kT%d" % h, "qT%d" % h, "kTpad%d" % h, "qTpad%d" % h] + AR, ["ps%d" % xb])

                def stage_a(i, c):
                    h = heads[c]
                    T, kb = blocks[i]
                    diag = kb >= 4 * T
                    xb = XB[(2 * i + c) % 4]
                    e_, etok = w32()
                    act(e_[:], PS[xb][:], AF.Exp, ["ps%d" % xb], [etok])
                    act(e_[:], e_[:], AF.Ln, [etok], [etok], bias=1.0)
                    if diag:
                        tt("pool", e_[:], e_[:], msk[:, kb - 4 * T, :], ALU.mult, [etok, "masks"] + AR, [etok])
                    hi, hitok = hl_tile()
                    lo, lotok = hl_tile()
                    cp("dve", hi[:], e_[:], [etok] + AR, [hitok])
                    tt("dve", lo[:], e_[:], hi[:], ALU.subtract, [etok, hitok] + AR, [lotok])
                    st_[(i, c)] = [hi, hitok, lo, lotok]

                def stage_b1(i, c):
                    h = heads[c]
                    T, kb = blocks[i]
                    qsl = slice(T * 512, (T + 1) * 512)
                    nkb = 4 * T + 4
                    diag = kb >= 4 * T
                    first = kb == nkb - 1
                    hi, hitok, lo, lotok = st_[(i, c)]
                    acc = PS[ACCB[c]]
                    atok = "ps%d" % ACCB[c]
                    mm(acc[:], uincl[:], hi[:], first, False, ["uincl", hitok] + AR, [atok], sgc=True)
                    mm(acc[:], uincl[:], lo[:], False, False, ["uincl", lotok] + AR, [atok], sgc=True)
                    mm(acc[:], kT[h][:, kb * 128:(kb + 1) * 128], qT[h][:, qsl], False, True,
                       ["kT%d" % h, "qT%d" % h, "kTpad%d" % h, "qTpad%d" % h] + AR, [atok], sgc=True)
                    w_, wtok = w_tile()
                    act(w_[:], acc[:], AF.Exp, [atok] + AR, [wtok])
                    if diag:
                        tt("pool", w_[:], w_[:], msk[:, kb - 4 * T, :], ALU.mult, [wtok, "masks"] + AR, [wtok])
                    st_[(i, c)] += [w_, wtok]

                def stage_b2(i, c):
                    h = heads[c]
                    T, kb = blocks[i]
                    qsl = slice(T * 512, (T + 1) * 512)
                    nkb = 4 * T + 4
                    first = kb == nkb - 1
                    last = kb == 0
                    hi, hitok, lo, lotok, w_, wtok = st_.pop((i, c))
                    acc = PS[ACCB[c]]
                    atok = "ps%d" % ACCB[c]
                    if not last:
                        mm(acc[:], nkT[h][:, kb * 128:(kb + 1) * 128], qT[h][:, qsl], False, False,
                           ["nkT%d" % h, "nkTpad%d" % h, "qT%d" % h, "qTpad%d" % h] + AR, [atok], sgc=True)
                        mm(acc[:], nlower[:], hi[:], False, False, ["nlower", hitok] + AR, [atok], sgc=True)
                        mm(acc[:], nlower[:], lo[:], False, True, ["nlower", lotok] + AR, [atok], sgc=True)
                    ob_ = OB[c]
                    mm(PS[ob_][:], vS_[:, kb, hp * 128:(hp + 1) * 128], w_[:], first, last,
                       ["vS", wtok] + AR, ["ps%d" % ob_])
                    if last:
                        o_, otok = w32()
                        rs_ = slice(c * 64, c * 64 + 64)
                        cp("act", o_[rs_, :], PS[ob_][rs_, :], ["ps%d" % ob_], [otok])
                        dma("sp", OH[3, h, :, qsl], o_[rs_, :], [otok], ["OH3"])

                n_ = len(blocks)
                stage_z(0, 0)
                stage_z(0, 1)
                for i in range(n_ + LA):
                    if i >= LA:
                        stage_b1(i - LA, 0)
                        stage_b1(i - LA, 1)
                    if i + 1 < n_:
                        stage_z(i + 1, 0)
                        stage_z(i + 1, 1)
                    if i < n_:
                        stage_a(i, 0)
                        stage_a(i, 1)
                    if i >= LA:
                        stage_b2(i - LA, 0)
                        stage_b2(i - LA, 1)
            phase_barrier()

            carve.reset()
            qR = [carve.take([128, S], BF16) for _ in range(2)]
            kR = [carve.take([128, S], BF16) for _ in range(2)]
            kRT = carve.take([128, NB, 256], BF16)
            vR = carve.take([128, NB, 256], BF16)
            rdec = carve.take([128, 4, 512], F32)
            qdec = carve.take([128, 4, 512], F32)
            KVs = carve.take([128, 2, NCH, 64], F32)
            SPb = carve.take([128, 2, NCH, 64], BF16)
            dma("sp", kRT, KRT.rearrange("(t p) h d -> p t (h d)", p=128), ["KRT"] + AR, ["kRT"])
            dma("sp", vR, VR.rearrange("(t p) n -> p t n", p=128), ["rv_dst"] + AR, ["vR"])
            dma("sp", rdec, cst["c_rdec"], AR, ["rdec"])
            dma("sp", qdec, cst["c_qdec"], AR, ["qdec"])
            for tb in range(NB):
                for j in range(2):
                    ch = 2 * tb + j
                    rs = slice(j * 64, (j + 1) * 64)
                    for hp in range(2):
                        b = nxt("psA", 4)
                        mm(PS[b][:, 0:128], kRT[rs, tb, hp * 128:(hp + 1) * 128], vR[rs, tb, hp * 128:(hp + 1) * 128],
                           True, True, ["kRT", "vR"] + AR, ["ps%d" % b])
                        cp("dve", KVs[0:64, hp, ch, :], PS[b][0:64, 0:64], ["ps%d" % b] + AR, ["KVs"])
                        cp("act", KVs[64:128, hp, ch, :], PS[b][64:128, 64:128], ["ps%d" % b] + AR, ["KVs"])
            for ch in range(1, NCH):
                for hp in range(2):
                    stt("dve", KVs[:, hp, ch, :], KVs[:, hp, ch - 1, :], c_adec[:, hp:hp + 1], KVs[:, hp, ch, :],
                        ALU.mult, ALU.add, ["KVs", "c_adec"] + AR, ["KVs"])
            cp("dve", SPb, KVs, ["KVs"] + AR, ["SPb"])
            for h in range(4):
                i2 = h % 2
                hp = h // 2
                prs = slice(i2 * 64, i2 * 64 + 64)
                dma("sp", qR[i2][prs, :], QR[h], ["QR"] + AR, ["qR%d" % i2])
                dma("sp", kR[i2][prs, :], KR[h], ["KR"] + AR, ["kR%d" % i2])
                for T in range(NT):
                    qsl = slice(T * 512, (T + 1) * 512)
                    xb = nxt("psA", 4)
                    for s4 in range(4):
                        csl = slice(T * 512 + s4 * 128, T * 512 + (s4 + 1) * 128)
                        mm(PS[xb][:, s4 * 128:(s4 + 1) * 128], kR[i2][prs, csl], qR[i2][prs, csl], s4 == 0, s4 == 3,
                           ["kR%d" % i2, "qR%d" % i2] + AR, ["ps%d" % xb])
                    sm_, smtok = wbf()
                    tt("dve", sm_[:], PS[xb][:], rdec[:, h, :], ALU.mult, ["ps%d" % xb, "rdec"] + AR, [smtok])
                    ob_ = 4 + (T % 2)
                    for s4 in range(4):
                        tb = T * 4 + s4
                        mm(PS[ob_][0:64, s4 * 128:(s4 + 1) * 128], vR[:, tb, h * 64:(h + 1) * 64],
                           sm_[:, s4 * 128:(s4 + 1) * 128], s4 == 0, False, ["vR", smtok] + AR, ["ps%d" % ob_])
                    qd_, qdtok = wbf()
                    tt("pool", qd_[prs, :], qR[i2][prs, qsl], qdec[prs, h, :], ALU.mult,
                       ["qR%d" % i2, "qdec"] + AR, [qdtok])
                    for c8 in range(8):
                        ch = T * 8 + c8
                        if ch == 0:
                            continue
                        mm(PS[ob_][0:64, c8 * 64:(c8 + 1) * 64], SPb[prs, hp, ch - 1, :], qd_[prs, c8 * 64:(c8 + 1) * 64],
                           False, c8 == 7, ["SPb", qdtok] + AR, ["ps%d" % ob_])
                    o_, otok = w32()
                    cp("act", o_[0:64, :], PS[ob_][0:64, :], ["ps%d" % ob_], [otok])
                    dma("sp", OH[2, h, :, qsl], o_[0:64, :], [otok], ["OH2"])
            phase_barrier()

            carve.reset()
            oh = [carve.take([64, 512], F32) for _ in range(8)]
            rgt = [carve.take([64, 512], F32) for _ in range(2)]
            for T in range(NT):
                qsl = slice(T * 512, (T + 1) * 512)
                for g in (0, 1, 3):
                    for h in range(4):
                        o_ = oh[(h + 4 * (g % 2)) % 8]
                        otok = "oh%d" % ((h + 4 * (g % 2)) % 8)
                        dma("sp", o_, OH[g, h, :, qsl], ["OH%d" % g] + AR, [otok])
                        s_, stok = w32()
                        act(s_[0:64, :], o_, AF.Square, [otok] + AR, [stok])
                        mm(PS[0][0:64, :], onesf[0:64, 0:64], s_[0:64, :], h == 0, h == 3, ["onesf", stok], ["ps0"])
                    rs_, rtok = w32()
                    act(rs_[0:64, :], PS[0][0:64, :], AF.Ln, ["ps0"], [rtok], bias=EPS, scale=1.0 / 256.0)
                    act(rs_[0:64, :], rs_[0:64, :], AF.Exp, [rtok], [rtok], scale=-0.5)
                    for h in range(4):
                        o_ = oh[(h + 4 * (g % 2)) % 8]
                        otok = "oh%d" % ((h + 4 * (g % 2)) % 8)
                        m_, mtok = wbf()
                        stt("dve", m_[0:64, :], o_, gmoc[0:64, 4 * g + h:4 * g + h + 1], rs_[0:64, :], ALU.mult, ALU.mult,
                            [otok, "gmoc", rtok] + AR, [mtok])
                        dma("pool", MIXT[g * 256 + h * 64:g * 256 + (h + 1) * 64, qsl], m_[0:64, :], [mtok], ["MIXT"])
                g = 2
                for h in range(4):
                    o_ = oh[h]
                    otok = "oh%d" % h
                    dma("sp", o_, OH[2, h, :, qsl], ["OH2"] + AR, [otok])
                    gt = rgt[h % 2]
                    gtok = "rgt%d" % (h % 2)
                    dma("sp", gt, RG[h * 64:(h + 1) * 64, qsl], ["RG"] + AR, [gtok])
                    mm(PS[1][0:64, :], ones64[0:64, :], o_, True, True, ["ones64", otok] + AR, ["ps1"])
                    d_, dtok = w32()
                    tt("dve", d_[0:64, :], o_, PS[1][0:64, :], ALU.subtract, [otok, "ps1"] + AR, [dtok])
                    s_, stok = w32()
                    act(s_[0:64, :], d_[0:64, :], AF.Square, [dtok], [stok])
                    mm(PS[2][0:64, :], ones64[0:64, :], s_[0:64, :], True, True, ["ones64", stok], ["ps2"])
                    act(s_[0:64, :], PS[2][0:64, :], AF.Ln, ["ps2"], [stok], bias=EPS)
                    act(s_[0:64, :], s_[0:64, :], AF.Exp, [stok], [stok], scale=-0.5)
                    stt("dve", d_[0:64, :], d_[0:64, :], gmoc[0:64, 8 + h:9 + h], s_[0:64, :], ALU.mult, ALU.mult,
                        [dtok, "gmoc", stok], [dtok])
                    m_, mtok = wbf()
                    tt("dve", m_[0:64, :], d_[0:64, :], gt, ALU.mult, [dtok, gtok] + AR, [mtok])
                    dma("pool", MIXT[512 + h * 64:512 + (h + 1) * 64, qsl], m_[0:64, :], [mtok], ["MIXT"])
            phase_barrier()

            carve.reset()
            woutb = carve.take([128, 8, D], BF16)
            mixt = [carve.take([128, 8, 512], BF16) for _ in range(2)]
            for kc in range(8):
                dma("pool", woutb[:, kc, :], w_out[L, kc * 128:(kc + 1) * 128, :], AR, ["woutb%d" % kc])
            dma("sp", gb[1][:], g_post[L:L + 1, :].partition_broadcast(128), (), ["gb1"])
            for T in range(NT):
                mt = mixt[T % 2]
                mtok_ = "mixt%d" % (T % 2)
                dma("sp", mt, MIXT.rearrange("(k p) s -> p k s", p=128)[:, :, T * 512:(T + 1) * 512], ["MIXT"] + AR, [mtok_])
                for sub in range(4):
                    r0 = T * 512 + sub * 128
                    xi = sub % 2
                    dma("sp", xt[xi][:], x_src[r0:r0 + 128, :], ["X2"], ["xt%d" % xi])
                    for half in range(2):
                        b = 2 * xi + half
                        for kc in range(8):
                            mm(PS[b][:], mt[:, kc, sub * 128:(sub + 1) * 128], woutb[:, kc, half * 512:(half + 1) * 512],
                               kc == 0, kc == 7, [mtok_, "woutb%d" % kc] + AR, ["ps%d" % b])
                        act(yt[xi][:, half * 512:(half + 1) * 512], PS[b][:], AF.Square, ["ps%d" % b],
                            ["yt%d" % xi, "smc%d" % (8 + 2 * xi + half)], accum=small[:, 8 + 2 * xi + half:9 + 2 * xi + half])
                    col = small[:, 8 + 2 * xi:9 + 2 * xi]
                    ctok = "smc%d" % (8 + 2 * xi)
                    tt("dve", col, col, small[:, 9 + 2 * xi:10 + 2 * xi], ALU.add, [ctok, "smc%d" % (9 + 2 * xi)], [ctok])
                    rstd_col(col, ctok, 1.0 / D)
                    for half in range(2):
                        b = 2 * xi + half
                        hs = slice(half * 512, (half + 1) * 512)
                        stt("dve", yt[xi][:, hs], PS[b][:], col, gb[1][:, hs], ALU.mult, ALU.mult,
                            ["ps%d" % b, ctok, "gb1"], ["yt%d" % xi])
                    tt("dve", yt[xi][:], yt[xi][:], xt[xi][:], ALU.add, ["yt%d" % xi, "xt%d" % xi], ["yt%d" % xi])
                    dma("pool", X1[r0:r0 + 128, :], yt[xi][:], ["yt%d" % xi], ["X1"])
            phase_barrier()

            carve.reset()
            wupb = carve.take([128, 8, DFF], BF16)
            wdnb = carve.take([128, 32, D], BF16)
            TOK = 256
            uT = carve.take([128, 32, TOK], BF16)
            h2T = carve.take([128, 8, TOK], BF16)
            for kc in range(8):
                dma("pool", wupb[:, kc, :], w_up[L, kc * 128:(kc + 1) * 128, :], AR, ["wupb%d" % kc])
            for f4 in range(8):
                dma("pool", wdnb[:, 4 * f4:4 * f4 + 4, :],
                    w_dn[L, f4 * 512:(f4 + 1) * 512, :].rearrange("(f p) n -> p f n", p=128), AR, ["wdnb%d" % f4])
            dma("sp", gb[0][:], g_fpre[L:L + 1, :].partition_broadcast(128), (), ["gb0"])
            dma("sp", gb[1][:], g_fpost[L:L + 1, :].partition_broadcast(128), (), ["gb1"])
            for T in range(S // TOK):
                for sub in range(2):
                    r0 = T * TOK + sub * 128
                    xi = sub
                    dma("sp", xt[xi][:], X1[r0:r0 + 128, :], ["X1"], ["xt%d" % xi])
                    col = small[:, 16 + sub:17 + sub]
                    ctok = "smd%d" % sub
                    act(hbf[xi][:], xt[xi][:], AF.Square, ["xt%d" % xi], ["hbf%d" % xi, ctok], accum=col)
                    rstd_col(col, ctok, 1.0 / D)
                    stt("dve", hbf[xi][:], xt[xi][:], col, gb[0][:], ALU.mult, ALU.mult,
                        ["xt%d" % xi, ctok, "gb0"], ["hbf%d" % xi])
                    psT = PS[4 + sub][:].bitcast(BF16)
                    for kc in range(8):
                        tr(psT[:, kc * 128:(kc + 1) * 128], hbf[xi][:, kc * 128:(kc + 1) * 128], ident[:],
                           ["hbf%d" % xi, "ident"], ["ps%d" % (4 + sub)])
                    cp("act" if sub else "dve", h2T[:, :, sub * 128:(sub + 1) * 128],
                       psT.rearrange("p (k c) -> p k c", k=8), ["ps%d" % (4 + sub)] + AR, ["h2T"])
                for fc in range(32):
                    b = fc % 4
                    for kc in range(8):
                        mm(PS[b][:, 0:TOK], wupb[:, kc, fc * 128:(fc + 1) * 128], h2T[:, kc, :], kc == 0, kc == 7,
                           ["wupb%d" % kc, "h2T"] + AR, ["ps%d" % b])
                    r_, rtok = w32()
                    act(r_[:, 0:TOK], PS[b][:, 0:TOK], AF.Relu, ["ps%d" % b], [rtok])
                    tt("pool" if fc % 2 else "dve", uT[:, fc, :], r_[:, 0:TOK], r_[:, 0:TOK], ALU.mult, [rtok] + AR, ["uT"])
                for sub in range(2):
                    r0 = T * TOK + sub * 128
                    xi = sub
                    for half in range(2):
                        b = 4 + 2 * sub + half
                        for fc in range(32):
                            mm(PS[b][:], uT[:, fc, sub * 128:(sub + 1) * 128], wdnb[:, fc, half * 512:(half + 1) * 512],
                               fc == 0, fc == 31, ["uT", "wdnb%d" % (fc // 4)] + AR, ["ps%d" % b])
                        act(yt[xi][:, half * 512:(half + 1) * 512], PS[b][:], AF.Square, ["ps%d" % b],
                            ["yt%d" % xi, "sme%d" % (2 * sub + half)],
                            accum=small[:, 24 + 2 * sub + half:25 + 2 * sub + half])
                    col = small[:, 24 + 2 * sub:25 + 2 * sub]
                    ctok = "sme%d" % (2 * sub)
                    tt("dve", col, col, small[:, 25 + 2 * sub:26 + 2 * sub], ALU.add, [ctok, "sme%d" % (2 * sub + 1)], [ctok])
                    rstd_col(col, ctok, 1.0 / D)
                    for half in range(2):
                        b = 4 + 2 * sub + half
                        hs = slice(half * 512, (half + 1) * 512)
                        stt("dve", yt[xi][:, hs], PS[b][:], col, gb[1][:, hs], ALU.mult, ALU.mult,
                            ["ps%d" % b, ctok, "gb1"], ["yt%d" % xi])
                    tt("pool", yt[xi][:], yt[xi][:], xt[xi][:], ALU.add, ["yt%d" % xi, "xt%d" % xi], ["yt%d" % xi])
                    dma("sp", x_dst[r0:r0 + 128, :], yt[xi][:], ["yt%d" % xi], ["X2" if x_dst is X2 else "Y"])
            phase_barrier()

        sch.add("sp", lambda e: e.nop(), (), list(set(sch.lastw.keys()) | set(sch.readers.keys())), force=True)
        sch.emit(nc, st)
    return nc, sch


_CACHE = {}


def make_in_maps(inputs, S, n_cores):
    consts = host_consts()
    maps = []
    f = lambda a: np.ascontiguousarray(np.asarray(a, dtype=np.float32))
    gq = f(inputs["g_q_lora"]).reshape(2, 2, 128).transpose(0, 2, 1)
    gkv = f(inputs["g_kv_lora"]).reshape(2, 128, 1)
    gmo = f(inputs["g_mix_out"]).reshape(2, 16, 64).transpose(0, 2, 1)
    gmo = np.concatenate([gmo, gmo], axis=1)
    bfc = f(inputs["b_forget"]).reshape(2, 4, 1)
    shared = dict(w_in=f(inputs["w_in"]), w_q_up=f(inputs["w_q_up"]), w_kv_up=f(inputs["w_kv_up"]),
                  w_out=f(inputs["w_out"]), w_ffn_up=f(inputs["w_ffn_up"]), w_ffn_down=f(inputs["w_ffn_down"]),
                  g_mix_pre=f(inputs["g_mix_pre"]), g_mix_post=f(inputs["g_mix_post"]),
                  g_ffn_pre=f(inputs["g_ffn_pre"]), g_ffn_post=f(inputs["g_ffn_post"]),
                  gq_col=np.ascontiguousarray(gq), gkv_col=np.ascontiguousarray(gkv),
                  gmo_col=np.ascontiguousarray(gmo), bf_col=np.ascontiguousarray(bfc))
    shared.update(consts)
    xs = f(inputs["x"])
    ps = np.asarray(inputs["positions"]).astype(np.int32)
    for c in range(n_cores):
        m = dict(shared)
        m["x"] = np.ascontiguousarray(xs[c])
        m["pos"] = np.ascontiguousarray(ps[c:c + 1])
        maps.append(m)
    return maps


def kernel(**inputs):
    x = np.asarray(inputs["x"])
    B, S, _ = x.shape
    if S not in _CACHE:
        _CACHE[S] = build(S)[0]
    nc = _CACHE[S]
    maps = make_in_maps(inputs, S, B)
    res = run_bass_kernel_spmd(nc, maps, core_ids=list(range(B)))
    return np.stack([np.asarray(r["y"], dtype=np.float32) for r in res.results], axis=0)
```

```python
import math
import numpy as np
import ml_dtypes
from contextlib import ExitStack
import concourse.bass as bass
import concourse.mybir as mybir
from concourse.bass_utils import run_bass_kernel_spmd

F32 = mybir.dt.float32
BF16 = mybir.dt.bfloat16
I32 = mybir.dt.int32
AF = mybir.ActivationFunctionType
ALU = mybir.AluOpType

COMPUTE = ("pe", "act", "dve", "pool", "sp")
N_DMA_SEMS = 48
SAME_ENGINE_SYNC = True

D = 1024
DFF = 4096
NIN = 2980
EPS = 1e-6
OFF = dict(fq=0, fk=256, fv=512, ff=768, cq=772, ckv=1028, kr=1156, rq=1188, rk=1444, rv=1700,
           rg=1956, sq=2212, sk=2468, sv=2724, rqp=2980, rkp=3236, krp=3492)
NINP = 3524


class _Op:
    __slots__ = ("fn", "deps", "raw", "ndma", "signal", "sem", "val", "clock", "queue")

    def __init__(self, fn, deps, ndma, queue, raw=()):
        self.fn = fn
        self.deps = deps
        self.raw = raw
        self.ndma = ndma
        self.signal = False
        self.sem = None
        self.val = 0
        self.clock = None
        self.queue = queue


class Sched:
    def __init__(self):
        self.ops = []
        self.lastw = {}
        self.readers = {}

    def add(self, eng, fn, reads=(), writes=(), ndma=0, force=False):
        import os
        mx = int(os.environ.get("MAX_OPS", "0"))
        if mx and len(self.ops) >= mx and not force:
            return -1
        i = len(self.ops)
        deps = set()
        if any(isinstance(t, str) and t.startswith("ps") for t in reads):
            writes = list(writes) + [t for t in reads if isinstance(t, str) and t.startswith("ps") and t not in writes]
            reads = [t for t in reads if not (isinstance(t, str) and t.startswith("ps"))]
        raw = set()
        for t in reads:
            w = self.lastw.get(t)
            if w is not None:
                deps.add(w)
                raw.add(w)
        for t in writes:
            w = self.lastw.get(t)
            if w is not None:
                deps.add(w)
            r = self.readers.get(t)
            if r:
                deps.update(r)
        for t in reads:
            self.readers.setdefault(t, []).append(i)
        for t in writes:
            self.lastw[t] = i
            self.readers[t] = []
        self.ops.append(_Op(fn, deps, ndma, eng, raw))
        return i

    def emit(self, nc, stack):
        ops = self.ops
        queues = {}
        for i, op in enumerate(ops):
            queues.setdefault(op.queue, []).append(i)
        esem = {q: stack.enter_context(nc.semaphore("s_" + q)) for q in COMPUTE}
        dsems = [stack.enter_context(nc.semaphore("d_%d" % k)) for k in range(N_DMA_SEMS)]
        dcount = [0] * N_DMA_SEMS
        dlast = [None] * N_DMA_SEMS
        N_SW = 8
        kk = {"pool": 0, "sp": 0}
        for i, op in enumerate(ops):
            if op.ndma:
                if op.queue == "pool":
                    k = kk["pool"] % N_SW
                    kk["pool"] += 1
                else:
                    k = N_SW + kk["sp"] % (N_DMA_SEMS - N_SW)
                    kk["sp"] += 1
                op.sem = ("d", k)
                if dlast[k] is not None:
                    op.deps.add(dlast[k])
                dlast[k] = i
                dcount[k] += 16 * op.ndma
                op.val = dcount[k]
                op.signal = True

        def skip(dop, op):
            if dop.ndma or op.ndma or dop.queue != op.queue:
                return False
            return dop.queue == "pe" or not SAME_ENGINE_SYNC

        opidx = {id(o): i for i, o in enumerate(ops)}

        for op in ops:
            for d in op.deps:
                dop = ops[d]
                if dop.ndma or skip(dop, op):
                    continue
                dop.signal = True
        cnt = {q: 0 for q in COMPUTE}
        for op in ops:
            if not op.ndma:
                if op.signal:
                    cnt[op.queue] += 1
                op.sem = ("e", op.queue)
                op.val = cnt[op.queue]
        kn = {q: {} for q in queues}
        for op in ops:
            kq = kn[op.queue]
            for d in op.deps:
                dop = ops[d]
                if kq.get(dop.sem, 0) < dop.val:
                    kq[dop.sem] = dop.val
                if dop.clock:
                    for s, v in dop.clock.items():
                        if kq.get(s, 0) < v:
                            kq[s] = v
            if op.signal:
                c = dict(kq)
                c[op.sem] = op.val
                op.clock = c

        def semobj(s):
            return esem[s[1]] if s[0] == "e" else dsems[s[1]]

        block = stack.enter_context(nc.Block())
        self.nwaits = 0

        def run_queue(q, eng):
            known = {}
            for i in queues[q]:
                op = ops[i]
                need = {}
                for d in op.deps:
                    dop = ops[d]
                    if skip(dop, op):
                        continue
                    if known.get(dop.sem, 0) >= dop.val:
                        continue
                    if need.get(dop.sem, 0) < dop.val:
                        need[dop.sem] = dop.val
                for d in op.deps:
                    dop = ops[d]
                    if skip(dop, op):
                        continue
                    if dop.clock:
                        for s, v in dop.clock.items():
                            if known.get(s, 0) < v:
                                known[s] = v
                for s, v in need.items():
                    eng.wait_ge(semobj(s), v)
                    self.nwaits += 1
                    if known.get(s, 0) < v:
                        known[s] = v
                ins = op.fn(eng)
                if op.ndma:
                    lst = ins if isinstance(ins, (list, tuple)) else [ins]
                    assert len(lst) == op.ndma
                    for x_ in lst:
                        x_.then_inc(semobj(op.sem), 16)
                elif op.signal:
                    ins.then_inc(semobj(op.sem), 1)

        def mk(q):
            return lambda eng: run_queue(q, eng)

        handlers = {"pe": block.tensor, "act": block.scalar, "dve": block.vector,
                    "pool": block.gpsimd, "sp": block.sync}
        for q in queues:
            handlers[q](mk(q))


def host_consts():
    c = {}
    bf = ml_dtypes.bfloat16
    c["c_ident"] = np.eye(128, dtype=np.float32).astype(bf)
    kk = np.arange(128)[:, None, None]
    jj = np.arange(4)[None, :, None]
    qq = np.arange(512)[None, None, :]
    key = 128 * jj + kk
    m = np.zeros((128, 3, 4, 512), np.float32)
    m[:, 0] = (key <= qq)
    m[:, 1] = ((key // 64) <= (qq // 64))
    m[:, 2] = (key < qq)
    c["c_masks"] = m.astype(bf)
    j = np.arange(128)[:, None]
    s = np.arange(128)[None, :]
    c["c_uincl"] = (-(j >= s).astype(np.float32)).astype(bf)
    c["c_nlower"] = (-(j < s).astype(np.float32)).astype(bf)
    h = np.arange(4, dtype=np.float32)
    log_gamma = np.log1p(-np.power(np.float32(2.0), np.float32(-5.0) - h)).astype(np.float32)
    idx = np.arange(64, dtype=np.float32)
    m_ = np.arange(128)
    dec = np.zeros((128, 4, 128), np.float32)
    for hh in range(4):
        dd = np.exp(log_gamma[hh] * np.abs(m_[:, None] - m_[None, :]).astype(np.float32)).astype(np.float32)
        same = (m_[:, None] // 64) == (m_[None, :] // 64)
        dec[:, hh, :] = np.where(same, dd, 0.0)
    c["c_rdec"] = np.tile(dec[:, :, None, :], (1, 1, 4, 1)).reshape(128, 4, 512).astype(np.float32)
    tail = np.exp(log_gamma[None, :] * (63.0 - idx)[:, None]).astype(np.float32)
    c["c_tail"] = np.tile(tail, (2, 1)).astype(np.float32)
    qh = np.exp(log_gamma[None, :] * (idx + 1.0)[:, None]).astype(np.float32)
    qd = np.tile(qh.T[None, :, None, :], (128, 1, 8, 1)).reshape(128, 4, 512)
    c["c_qdec"] = qd.astype(np.float32)
    a_ = np.exp(log_gamma * np.float32(64.0)).astype(np.float32)
    ad = np.zeros((128, 2), np.float32)
    for hp_ in range(2):
        ad[0:64, hp_] = a_[2 * hp_]
        ad[64:128, hp_] = a_[2 * hp_ + 1]
    c["c_adec"] = ad
    r = np.arange(128)
    invf_r = (np.float32(10000.0) ** (-(r % 32).astype(np.float32) / np.float32(32))).astype(np.float32)
    invf_m = (np.float32(10000.0) ** (-(r % 16).astype(np.float32) / np.float32(16))).astype(np.float32)
    sg_r = np.where((r % 64) < 32, -1.0, 1.0).astype(np.float32)
    sg_m = np.where((r % 32) < 16, -1.0, 1.0).astype(np.float32)
    c["c_rope"] = np.stack([invf_r, invf_m, sg_r, sg_m], axis=1).astype(np.float32)
    return c


CONST_SHAPES = dict(c_ident=([128, 128], BF16), c_masks=([128, 3, 4, 512], BF16), c_uincl=([128, 128], BF16), c_nlower=([128, 128], BF16),
                    c_rdec=([128, 4, 512], F32), c_tail=([128, 4], F32), c_qdec=([128, 4, 512], F32),
                    c_adec=([128, 2], F32), c_rope=([128, 4], F32))


def build(S, dbg=False, nlayers=2):
    NT = S // 512
    NB = S // 128
    NCH = S // 64
    nc = bass.Bass("TRN2", target_bir_lowering=False)
    sch = Sched()

    def din(name, shape, dt=F32):
        return nc.dram_tensor(name, shape, dt, kind="ExternalInput").ap()

    def dscr(name, shape, dt=F32):
        return nc.dram_tensor(name, shape, dt, kind=("ExternalOutput" if dbg else "Internal")).ap()

    x_in = din("x", [S, D])
    pos = din("pos", [1, S], I32)
    w_in = din("w_in", [2, D, NIN])
    w_q_up = din("w_q_up", [2, 256, 384])
    w_kv_up = din("w_kv_up", [2, 128, 512])
    w_out = din("w_out", [2, D, D])
    w_up = din("w_ffn_up", [2, D, DFF])
    w_dn = din("w_ffn_down", [2, DFF, D])
    g_pre = din("g_mix_pre", [2, D])
    g_post = din("g_mix_post", [2, D])
    g_fpre = din("g_ffn_pre", [2, D])
    g_fpost = din("g_ffn_post", [2, D])
    gq_col = din("gq_col", [2, 128, 2])
    gkv_col = din("gkv_col", [2, 128, 1])
    gmo_col = din("gmo_col", [2, 128, 16])
    bf_col = din("bf_col", [2, 4, 1])
    cst = {k: din(k, sh, dt) for k, (sh, dt) in CONST_SHAPES.items()}
    y_out = nc.dram_tensor("y", [S, D], F32, kind="ExternalOutput").ap()

    COSR = dscr("COSR", [128, S]); SINR = dscr("SINR", [128, S])
    COSM = dscr("COSM", [128, S]); SINM = dscr("SINM", [128, S])
    QF = dscr("QF", [4, 67, S], BF16); KF = dscr("KF", [4, 67, S], BF16)
    VF = dscr("VF", [S, 256], BF16); FFL = dscr("FFL", [4, S])
    CNEG = dscr("CNEG", [S, 4])
    QM = dscr("QM", [4, 96, S], BF16); KM = dscr("KM", [4, 96, S], BF16); VM = dscr("VM", [S, 256], BF16)
    QR = dscr("QR", [4, 64, S], BF16); KR = dscr("KR", [4, 64, S], BF16)
    KRT = dscr("KRT", [S, 4, 64], BF16); VR = dscr("VR", [S, 256], BF16); RG = dscr("RG", [256, S])
    QS = dscr("QS", [4, 64, S], BF16); KS = dscr("KS", [4, 64, S], BF16); VS = dscr("VS", [S, 256], BF16)
    OH = dscr("OH", [4, 4, 64, S])
    MIXT = dscr("MIXT", [D, S], BF16)
    X1 = dscr("X1", [S, D])
    X2 = dscr("X2", [S, D])

    st = ExitStack()
    with st:
        def sb(name, shape, dt):
            return st.enter_context(nc.sbuf_tensor("sb_" + name, shape, dt))

        ident = sb("ident", [128, 128], BF16)
        identf = sb("identf", [128, 128], F32)
        uincl = sb("uincl", [128, 128], BF16)
        nlower = sb("nlower", [128, 128], BF16)
        onesb = sb("onesb", [128, 128], BF16)
        onesf = sb("onesf", [128, 128], F32)
        ones64 = sb("ones64", [128, 64], F32)
        c_tail = sb("c_tail", [128, 4], F32)
        c_adec = sb("c_adec", [128, 2], F32)
        c_rope = sb("c_rope", [128, 4], F32)
        gqc = sb("gqc", [128, 2], F32)
        gkvc = sb("gkvc", [128, 1], F32)
        gmoc = sb("gmoc", [128, 16], F32)
        bfc = sb("bfc", [4, 1], F32)
        small = sb("small", [128, 64], F32)
        ARENA = 150 * 1024
        arena = sb("arena", [128, ARENA], mybir.dt.uint8)
        NW = 5
        wk32 = [sb("wk32_%d" % i, [128, 512], F32) for i in range(NW)]
        NWB = 8
        wkbf = [sb("wkbf_%d" % i, [128, 512], BF16) for i in range(NWB)]
        xt = [sb("xt_%d" % i, [128, 1024], F32) for i in range(4)]
        yt = [sb("yt_%d" % i, [128, 1024], F32) for i in range(2)]
        hbf = [sb("hbf_%d" % i, [128, 1024], BF16) for i in range(2)]
        gb = [sb("gb_%d" % i, [128, 1024], F32) for i in range(2)]
        PS = [st.enter_context(nc.psum_tensor("ps%d" % i, [128, 512], F32)) for i in range(8)]

        class Carver:
            def __init__(self):
                self.off = 0

            def reset(self):
                self.off = 0

            def take(self, shape, dt):
                esz = {F32: 4, BF16: 2, I32: 4}[dt]
                n = 1
                for d_ in shape[1:]:
                    n *= d_
                nbytes = (n * esz + 63) // 64 * 64
                assert self.off + nbytes <= ARENA, (self.off, nbytes)
                v = arena[0:shape[0], self.off:self.off + n * esz].bitcast(dt)
                self.off += nbytes
                if len(shape) > 2:
                    names = " ".join("d%d" % i for i in range(1, len(shape)))
                    kw = {"d%d" % i: shape[i] for i in range(1, len(shape))}
                    v = v.rearrange("p (%s) -> p %s" % (names, names), **kw)
                return v

        carve = Carver()
        phase_ctr = [0]

        def phase_barrier():
            phase_ctr[0] += 1
            sch.add("pool", lambda e: e.memset(small[:, 63:64], 0.0), reads=["small63"], writes=["ARENA", "small63"])

        AR = ["ARENA"]

        DRAM_TOKS = set(["QF", "QFc", "KF", "fv_dst", "mla_dst", "rv_dst", "sv_dst", "FFL", "CNEG", "QM", "KM", "QR", "KR",
                         "KRT", "RG", "QS", "KS", "OH0", "OH1", "OH2", "OH3", "MIXT", "X1", "X2", "Y",
                         "COS0", "COS1", "SIN0", "SIN1"] + ["KF1_%d" % j for j in range(4)])

        def dma(q, out, in_, reads=(), writes=()):
            r2 = [t for t in reads if t not in DRAM_TOKS] + [t for t in writes if t in DRAM_TOKS]
            w2 = [t for t in writes if t not in DRAM_TOKS] + [t for t in reads if t in DRAM_TOKS]
            sch.add(q, lambda e: e.dma_start(out=out, in_=in_), r2, w2, ndma=1)

        def mm(out, lhsT, rhs, start, stop, reads, writes, sgc=False):
            if sgc:
                sch.add("pe", lambda e: e.matmul(out, lhsT=lhsT, rhs=rhs, start=start, stop=stop, skip_group_check=True),
                        reads, writes)
            else:
                sch.add("pe", lambda e: e.matmul(out, lhsT=lhsT, rhs=rhs, start=start, stop=stop), reads, writes)

        def tr(out, in_, idn, reads, writes):
            sch.add("pe", lambda e: e.transpose(out=out, in_=in_, identity=idn), reads, writes)

        def act(out, in_, func, reads, writes, bias=None, scale=None, accum=None):
            kw = {}
            if bias is not None:
                kw["bias"] = bias
            if scale is not None:
                kw["scale"] = scale
            if accum is not None:
                kw["accum_out"] = accum
            sch.add("act", lambda e: e.activation(out=out, in_=in_, func=func, **kw), reads, writes)

        def tt(eng, out, in0, in1, op, reads, writes):
            sch.add(eng, lambda e: e.tensor_tensor(out=out, in0=in0, in1=in1, op=op), reads, writes)

        def ts(eng, out, in0, s1, op0, reads, writes, s2=None, op1=None):
            if op1 is None:
                sch.add(eng, lambda e: e.tensor_scalar(out=out, in0=in0, scalar1=s1, scalar2=None, op0=op0), reads, writes)
            else:
                sch.add(eng, lambda e: e.tensor_scalar(out=out, in0=in0, scalar1=s1, scalar2=s2, op0=op0, op1=op1),
                        reads, writes)

        def stt(eng, out, in0, scalar, in1, op0, op1, reads, writes):
            sch.add(eng, lambda e: e.scalar_tensor_tensor(out=out, in0=in0, scalar=scalar, in1=in1, op0=op0, op1=op1),
                    reads, writes)

        def cp(eng, out, in_, reads, writes):
            if eng == "act":
                act(out, in_, AF.Copy, reads, writes)
            else:
                sch.add(eng, lambda e: e.tensor_copy(out=out, in_=in_), reads, writes)

        def recip(out, in_, reads, writes):
            sch.add("dve", lambda e: e.reciprocal(out=out, in_=in_), reads, writes)

        def memset(eng, ap, val, writes):
            sch.add(eng, lambda e: e.memset(ap, val), (), writes)

        rr = {"w32": 0, "wbf": 0, "psA": 0, "ev": 0, "psX3": 0}

        def nxt(key, n):
            v = rr[key]
            rr[key] = (v + 1) % n
            return v

        def w32():
            i = nxt("w32", NW)
            return wk32[i], "wk32_%d" % i

        def wbf():
            i = nxt("wbf", NWB)
            return wkbf[i], "wkbf_%d" % i

        def evq():
            return ("act", "dve")[nxt("ev", 2)]

        def rstd_col(col_ap, tok, n_inv):
            act(col_ap, col_ap, AF.Ln, [tok], [tok], bias=EPS, scale=n_inv)
            act(col_ap, col_ap, AF.Exp, [tok], [tok], scale=-0.5)

        dma("sp", ident[:], cst["c_ident"], (), ["ident"])
        dma("sp", uincl[:], cst["c_uincl"], (), ["uincl"])
        dma("sp", nlower[:], cst["c_nlower"], (), ["nlower"])
        dma("sp", c_tail[:], cst["c_tail"], (), ["c_tail"])
        dma("sp", c_adec[:], cst["c_adec"], (), ["c_adec"])
        dma("sp", c_rope[:], cst["c_rope"], (), ["c_rope"])
        memset("pool", onesb[:], 1.0, ["onesb"])
        memset("pool", onesf[:], 1.0, ["onesf"])
        memset("pool", ones64[:], 1.0 / 64.0, ["ones64"])
        memset("pool", small[:], 0.0, ["small63"])
        cp("dve", identf[:], ident[:], ["ident"], ["identf"])
        carve.reset()
        ob = carve.take([128, S], BF16)
        sch.add("pool", lambda e: e.memset(ob, 1.0), AR, ["ob"])
        for h in range(4):
            dma("sp", KF[h, 64:67, :], ob[0:3, :], ["ob"] + AR, ["KF1_%d" % h])
        posi = carve.take([128, S], I32)
        posf = carve.take([128, S], F32)
        ang = carve.take([128, S], F32)
        kf_ = carve.take([128, S], F32)
        ki_ = carve.take([128, S], I32)
        dma("sp", posi, pos.partition_broadcast(128), AR, ["posi"])
        cp("dve", posf, posi, ["posi"] + AR, ["posf"])
        TWO_PI = 2.0 * math.pi
        C1 = 6.28125
        C2 = TWO_PI - C1
        for ti, (COS, SIN) in enumerate(((COSR, SINR), (COSM, SINM))):
            ts("dve", ang, posf, c_rope[:, ti:ti + 1], ALU.mult, ["posf", "c_rope"] + AR, ["ang"])
            for which in (0, 1):
                ts("dve", ki_, ang, 1.0 / TWO_PI, ALU.mult, ["ang"] + AR, ["ki"])
                cp("dve", kf_, ki_, ["ki"] + AR, ["kf"])
                stt("dve", posi.bitcast(F32), kf_, -C1, ang, ALU.mult, ALU.add, ["kf", "ang"] + AR, ["red"])
                red = posi.bitcast(F32)
                stt("dve", red, kf_, -C2, red, ALU.mult, ALU.add, ["kf", "red"] + AR, ["red"])
                if which == 1:
                    ts("dve", red, red, math.pi / 2.0, ALU.add, ["red"] + AR, ["red"])
                    ts("dve", kf_, red, math.pi, ALU.is_gt, ["red"] + AR, ["kf"])
                    stt("dve", red, kf_, -TWO_PI, red, ALU.mult, ALU.add, ["kf", "red"] + AR, ["red"])
                ts("dve", red, red, 3.141592, ALU.min, ["red"] + AR, ["red"], s2=-3.141592, op1=ALU.max)
                act(kf_, red, AF.Sin, ["red"] + AR, ["kf"])
                if which == 0:
                    ts("dve", kf_, kf_, c_rope[:, 2 + ti:3 + ti], ALU.mult, ["kf", "c_rope"] + AR, ["kf"])
                    dma("sp", SIN, kf_, ["kf"] + AR, ["SIN%d" % ti])
                else:
                    dma("sp", COS, kf_, ["kf"] + AR, ["COS%d" % ti])
        phase_barrier()

        for L in range(nlayers):
            if dbg == "setup":
                break
            x_src = x_in if L == 0 else X2
            x_dst = y_out if L == nlayers - 1 else X2
            dma("sp", gqc[:], gq_col[L], (), ["gqc"])
            dma("sp", gkvc[:], gkv_col[L], (), ["gkvc"])
            dma("sp", gmoc[:], gmo_col[L], (), ["gmoc"])
            dma("sp", bfc[:], bf_col[L], (), ["bfc"])

            carve.reset()
            winb = carve.take([128, 8, NINP], BF16)
            wq32 = carve.take([128, 2, 384], F32)
            wqn = carve.take([128, 2, 256], BF16)
            wqr = carve.take([128, 2, 128], BF16)
            wqp = carve.take([128, 2, 128], BF16)
            wkv32 = carve.take([128, 512], F32)
            wkn = carve.take([128, 256], BF16)
            wvv = carve.take([128, 256], BF16)
            hT = [carve.take([128, 8, 512], BF16) for _ in range(2)]
            tabs = [[carve.take([128, 512], F32) for _ in range(4)] for _ in range(2)]
            xt4 = [carve.take([128, 1024], F32) for _ in range(4)]
            cq32 = [carve.take([128, 512], F32) for _ in range(2)]
            sq32 = [carve.take([128, 512], F32) for _ in range(3)]
            cqn = [carve.take([128, 512], BF16) for _ in range(2)]
            ckvn = carve.take([128, 512], BF16)
            rstdb = carve.take([128, 512], F32)
            vstg = [carve.take([128, 4, 256], BF16) for _ in range(3)]
            krtstg = carve.take([128, 4, 256], BF16)
            c_qdummy = None

            for kc in range(8):
                rows = slice(kc * 128, (kc + 1) * 128)
                dma("pool", winb[:, kc, 0:NIN], w_in[L, rows, :], AR, ["winb%d" % kc])
                for nm, nmp, nh, hw in (("rq", "rqp", 4, 32), ("rk", "rkp", 4, 32), ("kr", "krp", 1, 16)):
                    src = winb[:, kc, OFF[nm]:OFF[nm] + nh * 2 * hw].rearrange("p (h t c) -> p h t c", h=nh, t=2)
                    dst = winb[:, kc, OFF[nmp]:OFF[nmp] + nh * 2 * hw].rearrange("p (h t c) -> p h t c", h=nh, t=2)
                    cp("pool", dst[:, :, 0, :], src[:, :, 1, :], ["winb%d" % kc] + AR, ["winbp%d" % kc])
                    cp("pool", dst[:, :, 1, :], src[:, :, 0, :], ["winb%d" % kc] + AR, ["winbp%d" % kc])
            if dbg == "p1a":
                break
            dma("sp", wq32, w_q_up[L].rearrange("(c p) n -> p c n", p=128), AR, ["wq32"])
            dma("sp", wkv32, w_kv_up[L], AR, ["wkv32"])
            for c in range(2):
                src4 = wq32[:, c, :].rearrange("p (h e) -> p h e", h=4)
                gcol = gqc[:, c:c + 1]
                ts("dve", wqn[:, c, :].rearrange("p (h e) -> p h e", h=4), src4[:, :, 0:64], gcol, ALU.mult,
                   ["wq32", "gqc"] + AR, ["wqb"])
                ts("dve", wqr[:, c, :].rearrange("p (h e) -> p h e", h=4), src4[:, :, 64:96], gcol, ALU.mult,
                   ["wq32", "gqc"] + AR, ["wqb"])
                dstp = wqp[:, c, :].rearrange("p (h t e) -> p h t e", h=4, t=2)
                ts("dve", dstp[:, :, 0, :], src4[:, :, 80:96], gcol, ALU.mult, ["wq32", "gqc"] + AR, ["wqb"])
                ts("dve", dstp[:, :, 1, :], src4[:, :, 64:80], gcol, ALU.mult, ["wq32", "gqc"] + AR, ["wqb"])
            kv4 = wkv32.rearrange("p (h e) -> p h e", h=4)
            ts("dve", wkn.rearrange("p (h e) -> p h e", h=4), kv4[:, :, 0:64], gkvc[:, 0:1], ALU.mult,
               ["wkv32", "gkvc"] + AR, ["wkvb"])
            ts("dve", wvv.rearrange("p (h e) -> p h e", h=4), kv4[:, :, 64:128], gkvc[:, 0:1], ALU.mult,
               ["wkv32", "gkvc"] + AR, ["wkvb"])
            dma("sp", gb[0][:], g_pre[L:L + 1, :].partition_broadcast(128), (), ["gb0"])

            WIN_ALL = ["winb%d" % k_ for k_ in range(8)] + ["winbp%d" % k_ for k_ in range(8)]

            def fm_group(hTt, hTtok, col_lo, M, ncols_stride=None):
                b = nxt("psA", 4)
                ps = PS[b]
                for kc in range(8):
                    mm(ps[0:M, :], winb[:, kc, col_lo:col_lo + M], hTt[:, kc, :], kc == 0, kc == 7,
                       ["winb%d" % kc, "winbp%d" % kc, hTtok] + AR, ["ps%d" % b])
                return ps, "ps%d" % b

            def store_rows(stg, stok, dsts, tsl):
                for (r0, r1, dram_rows, wtok) in dsts:
                    dma("sp", dram_rows[:, tsl], stg[r0:r1, :], [stok], [wtok])

            def load_tile_inputs(t):
                tsl_ = slice(t * 512, (t + 1) * 512)
                p_ = t % 2
                for sub in range(4):
                    r0 = t * 512 + sub * 128
                    dma("pool", xt4[sub], x_src[r0:r0 + 128, :], ["X2"] + AR, ["xt4_%d" % sub])
                for j, (SRC, tk) in enumerate(((COSR, "COS0"), (SINR, "SIN0"), (COSM, "COS1"), (SINM, "SIN1"))):
                    dma("pool", tabs[p_][j], SRC[:, tsl_], [tk] + AR, ["tab%d_%d" % (p_, j)])

            def norm_tile(t):
                hTt = hT[t % 2]
                hTtok = "hT%d" % (t % 2)
                p_ = t % 2
                for sub in range(4):
                    xi = sub % 2
                    xs = xt4[sub]
                    xtok = "xt4_%d" % sub
                    col = small[:, sub:sub + 1]
                    ctok = "sm%d" % sub
                    act(hbf[xi][:], xs, AF.Square, [xtok] + AR, ["hbf%d" % xi, ctok], accum=col)
                    rstd_col(col, ctok, 1.0 / D)
                    stt("dve", hbf[xi][:], xs, col, gb[0][:], ALU.mult, ALU.mult,
                        [xtok, ctok, "gb0"] + AR, ["hbf%d" % xi])
                    psT = PS[4][:].bitcast(BF16)
                    for kc in range(8):
                        tr(psT[:, kc * 128:(kc + 1) * 128], hbf[xi][:, kc * 128:(kc + 1) * 128], ident[:],
                           ["hbf%d" % xi, "ident"], ["ps4"])
                    cp("act" if sub % 2 else "dve", hTt[:, :, sub * 128:(sub + 1) * 128],
                       psT.rearrange("p (k c) -> p k c", k=8), ["ps4"] + AR, [hTtok])

            load_tile_inputs(0)
            norm_tile(0)
            for t in range(NT):
                tsl = slice(t * 512, (t + 1) * 512)
                hTt = hT[t % 2]
                hTtok = "hT%d" % (t % 2)
                if t + 1 < NT:
                    load_tile_inputs(t + 1)
                cr, sr, cm, sm = tabs[t % 2]
                crk, srk, cmk, smk = ["tab%d_%d" % (t % 2, j) for j in range(4)]
                for c in range(2):
                    ps, ptok = fm_group(hTt, hTtok, OFF["cq"] + c * 128, 128)
                    act(sq32[c], ps[:], AF.Square, [ptok] + AR, ["sq32_%d" % c])
                    cp("dve", cq32[c], ps[:], [ptok] + AR, ["cq32_%d" % c])
                for c in range(2):
                    mm(PS[6][:], onesf[:], sq32[c], c == 0, c == 1, ["onesf", "sq32_%d" % c] + AR, ["ps6"])
                act(rstdb, PS[6][:], AF.Ln, ["ps6"] + AR, ["rstdb"], bias=EPS, scale=1.0 / 256.0)
                act(rstdb, rstdb, AF.Exp, ["rstdb"] + AR, ["rstdb"], scale=-0.5)
                for c in range(2):
                    tt("pool" if c else "dve", cqn[c], cq32[c], rstdb, ALU.mult, ["cq32_%d" % c, "rstdb"] + AR, ["cqn%d" % c])
                ps, ptok = fm_group(hTt, hTtok, OFF["ckv"], 128)
                act(sq32[2], ps[:], AF.Square, [ptok] + AR, ["sq32_2"])
                cp("dve", cq32[0], ps[:], [ptok] + AR, ["cq32_0"])
                mm(PS[6][:], onesf[:], sq32[2], True, True, ["onesf", "sq32_2"] + AR, ["ps6"])
                act(rstdb, PS[6][:], AF.Ln, ["ps6"] + AR, ["rstdb"], bias=EPS, scale=1.0 / 128.0)
                act(rstdb, rstdb, AF.Exp, ["rstdb"] + AR, ["rstdb"], scale=-0.5)
                tt("dve", ckvn, cq32[0], rstdb, ALU.mult, ["cq32_0", "rstdb"] + AR, ["ckvn"])
                def simple_pair(col_lo, scale, DST, wtok):
                    for hp in range(2):
                        ps, ptok = fm_group(hTt, hTtok, col_lo + hp * 128, 128)
                        stg, stok = wbf()
                        if scale is None:
                            cp(evq(), stg[:], ps[:], [ptok], [stok])
                        else:
                            act(stg[:], ps[:], AF.Copy, [ptok], [stok], scale=scale)
                        store_rows(stg, stok, [(0, 64, DST[2 * hp, 0:64], wtok), (64, 128, DST[2 * hp + 1, 0:64], wtok)], tsl)

                simple_pair(OFF["fq"], 0.125, QF, "QF")
                simple_pair(OFF["fk"], None, KF, "KF")
                simple_pair(OFF["sq"], 0.125, QS, "QS")
                simple_pair(OFF["sk"], None, KS, "KS")
                ps, ptok = fm_group(hTt, hTtok, OFF["ff"], 4)
                stg, stok = w32()
                cp("dve", stg[0:4, :], ps[0:4, :], [ptok], [stok])
                dma("sp", FFL[:, tsl], stg[0:4, :], [stok], ["FFL"])
                for c in range(2):
                    ps, ptok = fm_group(hTt, hTtok, OFF["rg"] + c * 128, 128)
                    stg, stok = w32()
                    act(stg[:], ps[:], AF.Silu, [ptok], [stok])
                    dma("sp", RG[c * 128:(c + 1) * 128, tsl], stg[:], [stok], ["RG"])
                for nm, nmp, DST, wtok, scl in (("rq", "rqp", QR, "QR", 1.0), ("rk", "rkp", KR, "KR", 0.125)):
                    for hp in range(2):
                        psa, pta = fm_group(hTt, hTtok, OFF[nm] + hp * 128, 128)
                        psb, ptb = fm_group(hTt, hTtok, OFF[nmp] + hp * 128, 128)
                        t1, k1 = w32()
                        t2, k2 = w32()
                        stt("dve", t1[:], psa[:], scl, cr, ALU.mult, ALU.mult, [pta, crk] + AR, [k1])
                        stt("dve", t2[:], psb[:], scl, sr, ALU.mult, ALU.mult, [ptb, srk] + AR, [k2])
                        stg, stok = wbf()
                        tt("pool", stg[:], t1[:], t2[:], ALU.add, [k1, k2], [stok])
                        store_rows(stg, stok, [(0, 64, DST[2 * hp], wtok), (64, 128, DST[2 * hp + 1], wtok)], tsl)
                        if nm == "rk":
                            psT = PS[5][:].bitcast(BF16)
                            for sub in range(4):
                                tr(psT[:, sub * 128:(sub + 1) * 128], stg[:, sub * 128:(sub + 1) * 128], ident[:],
                                   [stok, "ident"], ["ps5"])
                            for hh in range(2):
                                h = 2 * hp + hh
                                ts("dve", krtstg[:, :, h * 64:(h + 1) * 64],
                                   psT[:, 0:512].rearrange("p (s a d) -> p s a d", s=4, a=2)[:, :, hh, :],
                                   c_tail[:, h:h + 1], ALU.mult, ["ps5", "c_tail"] + AR, ["krtstg"])
                            if hp == 1:
                                dma("sp", KRT[tsl, :, :].rearrange("(s p) h d -> p s (h d)", p=128), krtstg,
                                    ["krtstg"] + AR, ["KRT"])
                if t + 1 < NT:
                    norm_tile(t + 1)
                qsc = 96.0 ** -0.5
                for hp in range(2):
                    b = nxt("psA", 4)
                    for c in range(2):
                        mm(PS[b][:], wqn[:, c, hp * 128:(hp + 1) * 128], cqn[c], c == 0, c == 1,
                           ["wqb", "cqn%d" % c] + AR, ["ps%d" % b])
                    stg, stok = wbf()
                    act(stg[:], PS[b][:], AF.Copy, ["ps%d" % b], [stok], scale=qsc)
                    store_rows(stg, stok, [(0, 64, QM[2 * hp, 0:64], "QM"), (64, 128, QM[2 * hp + 1, 0:64], "QM")], tsl)
                ba = nxt("psA", 4)
                for c in range(2):
                    mm(PS[ba][:], wqr[:, c, :], cqn[c], c == 0, c == 1, ["wqb", "cqn%d" % c] + AR, ["ps%d" % ba])
                bb = nxt("psA", 4)
                for c in range(2):
                    mm(PS[bb][:], wqp[:, c, :], cqn[c], c == 0, c == 1, ["wqb", "cqn%d" % c] + AR, ["ps%d" % bb])
                t1, k1 = w32()
                t2, k2 = w32()
                stt("dve", t1[:], PS[ba][:], qsc, cm, ALU.mult, ALU.mult, ["ps%d" % ba, cmk] + AR, [k1])
                stt("dve", t2[:], PS[bb][:], qsc, sm, ALU.mult, ALU.mult, ["ps%d" % bb, smk] + AR, [k2])
                stg, stok = wbf()
                tt("pool", stg[:], t1[:], t2[:], ALU.add, [k1, k2], [stok])
                store_rows(stg, stok, [(32 * h, 32 * h + 32, QM[h, 64:96], "QM") for h in range(4)], tsl)
                for hp in range(2):
                    b = nxt("psA", 4)
                    mm(PS[b][:], wkn[:, hp * 128:(hp + 1) * 128], ckvn, True, True, ["wkvb", "ckvn"] + AR, ["ps%d" % b])
                    stg, stok = wbf()
                    cp(evq(), stg[:], PS[b][:], ["ps%d" % b], [stok])
                    store_rows(stg, stok, [(0, 64, KM[2 * hp, 0:64], "KM"), (64, 128, KM[2 * hp + 1, 0:64], "KM")], tsl)
                psa, pta = fm_group(hTt, hTtok, OFF["kr"], 32)
                psb, ptb = fm_group(hTt, hTtok, OFF["krp"], 32)
                t1, k1 = w32()
                t2, k2 = w32()
                tt("dve", t1[0:32, :], psa[0:32, :], cm[0:32, :], ALU.mult, [pta, cmk] + AR, [k1])
                tt("dve", t2[0:32, :], psb[0:32, :], sm[0:32, :], ALU.mult, [ptb, smk] + AR, [k2])
                stg, stok = wbf()
                tt("pool", stg[0:32, :], t1[0:32, :], t2[0:32, :], ALU.add, [k1, k2], [stok])
                store_rows(stg, stok, [(0, 32, KM[h, 64:96], "KM") for h in range(4)], tsl)
                for vi, (nm, DSTv) in enumerate((("fv", None), ("rv", None), ("sv", None), ("mla", None))):
                    vs = vstg[vi % 3]
                    vtok = "vstg%d" % (vi % 3)
                    for sub in range(4):
                        b = 6 + (sub % 2)
                        if nm == "mla":
                            mm(PS[b][:, 0:256], ckvn[:, sub * 128:(sub + 1) * 128], wvv, True, True,
                               ["ckvn", "wkvb"] + AR, ["ps%d" % b])
                        else:
                            for kc in range(8):
                                mm(PS[b][:, 0:256], hTt[:, kc, sub * 128:(sub + 1) * 128],
                                   winb[:, kc, OFF[nm]:OFF[nm] + 256], kc == 0, kc == 7,
                                   ["winb%d" % kc, hTtok] + AR, ["ps%d" % b])
                        cp(evq(), vs[:, sub, :], PS[b][:, 0:256], ["ps%d" % b] + AR, [vtok])
                    V_ = {"fv": VF, "mla": VM, "rv": VR, "sv": VS}[nm]
                    dma("sp", V_[tsl, :].rearrange("(s p) n -> p s n", p=128), vs, [vtok] + AR, [nm + "_dst"])
            phase_barrier()

            if dbg == "p1":
                break
            carve.reset()
            fl = carve.take([4, S], F32)
            ones4 = carve.take([4, S], F32)
            cc = carve.take([4, S], F32)
            r1 = carve.take([4, S], F32)
            chi = carve.take([4, S], BF16)
            cmid = carve.take([4, S], BF16)
            clo = carve.take([4, S], BF16)
            cneg = carve.take([128, NB, 4], F32)
            dma("sp", fl, FFL, ["FFL"] + AR, ["fl"])
            sch.add("pool", lambda e: e.memset(ones4, 1.0), AR, ["ones4"])
            act(fl, fl, AF.Identity, ["fl", "bfc"] + AR, ["fl"], bias=bfc[:, 0:1])
            act(fl, fl, AF.Exp, ["fl"] + AR, ["fl"], scale=-1.0)
            act(fl, fl, AF.Ln, ["fl"] + AR, ["fl"], bias=1.0)
            ts("dve", fl, fl, -1.0, ALU.mult, ["fl"] + AR, ["fl"])
            sch.add("dve", lambda e: e.tensor_tensor_scan(out=cc, data0=ones4, data1=fl, initial=0.0,
                                                          op0=ALU.mult, op1=ALU.add), ["fl", "ones4"] + AR, ["cc"])
            cp("dve", chi, cc, ["cc"] + AR, ["chi"])
            tt("dve", r1, cc, chi, ALU.subtract, ["cc", "chi"] + AR, ["r1"])
            cp("dve", cmid, r1, ["r1"] + AR, ["cmid"])
            tt("dve", r1, r1, cmid, ALU.subtract, ["r1", "cmid"] + AR, ["r1"])
            cp("dve", clo, r1, ["r1"] + AR, ["clo"])
            for h in range(4):
                for i, (src, tk) in enumerate(((chi, "chi"), (cmid, "cmid"), (clo, "clo"))):
                    dma("sp", QF[h, 64 + i:65 + i, :], src[h:h + 1, :], [tk] + AR, ["QFc"])
            for tb in range(NB):
                tr(PS[0][:, tb * 4:(tb + 1) * 4], cc[:, tb * 128:(tb + 1) * 128], identf[0:4, 0:4],
                   ["cc", "identf"] + AR, ["ps0"])
            ts("dve", cneg, PS[0][:, 0:NB * 4].rearrange("p (t h) -> p t h", h=4), -1.0, ALU.mult, ["ps0"] + AR, ["cneg"])
            dma("sp", CNEG.rearrange("(t p) h -> p t h", p=128), cneg, ["cneg"] + AR, ["CNEG"])
            phase_barrier()

            def softmax_attention(g, Qd, Kd, Vd, KD, mask_kind, use_bias):
                carve.reset()
                qT = [carve.take([128, S], BF16) for _ in range(2)]
                kT = [carve.take([128, S], BF16) for _ in range(2)]
                vA = [carve.take([128, NB, 128], BF16) for _ in range(2)]
                cng = carve.take([128, NB, 4], F32)
                msk = carve.take([128, 4, 512], BF16)
                dma("sp", msk, cst["c_masks"][:, mask_kind], AR, ["masks"])
                if use_bias:
                    dma("sp", cng, CNEG.rearrange("(t p) h -> p t h", p=128), ["CNEG"] + AR, ["cng"])
                LA = 3

                def load_head(h):
                    i2 = h % 2
                    dma("sp", qT[i2][0:KD, :], Qd[h], ["QF", "QFc", "QM"] + AR, ["qT%d" % i2])
                    dma("sp", kT[i2][0:KD, :], Kd[h], ["KF", "KM"] + ["KF1_%d" % j for j in range(4)] + AR, ["kT%d" % i2])
                    dma("sp", vA[i2][:, :, 0:64], Vd.rearrange("(t p) (h d) -> p t h d", p=128, h=4)[:, :, h, :],
                        ["fv_dst", "mla_dst"] + AR, ["vA%d" % i2])

                for i2_ in range(2):
                    sch.add("pool", (lambda t_: (lambda e: e.memset(t_[:, :, 64:128], 1.0)))(vA[i2_]), AR, ["vAones%d" % i2_])
                load_head(0)
                for h in range(4):
                    i2 = h % 2
                    if h + 1 < 4:
                        load_head(h + 1)
                    blocks = [(T, kb) for T in range(NT) for kb in range(4 * T + 4)]
                    st_ = {}

                    def stage_a(i):
                        T, kb = blocks[i]
                        qsl = slice(T * 512, (T + 1) * 512)
                        xb = nxt("psA", 4)
                        mm(PS[xb][:], kT[i2][0:KD, kb * 128:(kb + 1) * 128], qT[i2][0:KD, qsl], True, True,
                           ["kT%d" % i2, "qT%d" % i2] + AR, ["ps%d" % xb])
                        p_, ptok = wbf()
                        if use_bias:
                            act(p_[:], PS[xb][:], AF.Exp, ["ps%d" % xb, "cng"] + AR, [ptok], bias=cng[:, kb, h:h + 1])
                        else:
                            act(p_[:], PS[xb][:], AF.Exp, ["ps%d" % xb], [ptok])
                        if kb >= 4 * T:
                            tt("pool", p_[:], p_[:], msk[:, kb - 4 * T, :], ALU.mult, [ptok, "masks"] + AR, [ptok])
                        st_[i] = (p_, ptok)

                    def stage_b(i):
                        T, kb = blocks[i]
                        qsl = slice(T * 512, (T + 1) * 512)
                        nkb = 4 * T + 4
                        ob_ = 4 + (T % 2)
                        p_, ptok = st_.pop(i)
                        mm(PS[ob_][:], vA[i2][:, kb, :], p_[:], kb == 0, kb == nkb - 1,
                           ["vA%d" % i2, "vAones%d" % i2, ptok] + AR, ["ps%d" % ob_])
                        if kb == nkb - 1:
                            den, dtok = w32()
                            act(den[0:64, :], PS[ob_][64:128, :], AF.Copy, ["ps%d" % ob_], [dtok])
                            recip(den[0:64, :], den[0:64, :], [dtok], [dtok])
                            o_, otok = w32()
                            tt("dve", o_[0:64, :], PS[ob_][0:64, :], den[0:64, :], ALU.mult, ["ps%d" % ob_, dtok], [otok])
                            dma("sp", OH[g, h, :, qsl], o_[0:64, :], [otok], ["OH%d" % g])

                    n_ = len(blocks)
                    for i in range(n_ + LA):
                        if i < n_:
                            stage_a(i)
                        if i >= LA:
                            stage_b(i - LA)
                phase_barrier()

            softmax_attention(0, QF, KF, VF, 67, 0, True)
            softmax_attention(1, QM, KM, VM, 96, 1, False)

            carve.reset()
            qT = [carve.take([128, S], BF16) for _ in range(4)]
            kT = [carve.take([128, S], BF16) for _ in range(4)]
            nkT = [carve.take([128, S], BF16) for _ in range(4)]
            vS_ = carve.take([128, NB, 256], BF16)
            msk = carve.take([128, 4, 512], BF16)
            dma("sp", msk, cst["c_masks"][:, 2], AR, ["masks"])
            dma("sp", vS_, VS.rearrange("(t p) n -> p t n", p=128), ["sv_dst"] + AR, ["vS"])
            LA = 2
            NHL = 2 * 2 * (LA + 2)
            hl_pool = [carve.take([128, 512], BF16) for _ in range(NHL)]
            w_pool = [carve.take([128, 512], BF16) for _ in range(6)]
            rr["hl"] = 0
            rr["wp"] = 0

            def hl_tile():
                i_ = nxt("hl", NHL)
                return hl_pool[i_], "hl%d" % i_

            def w_tile():
                i_ = nxt("wp", 6)
                return w_pool[i_], "wp%d" % i_
            ACCB = (3, 6)
            OB = (4, 5)

            def load_head_sb(h):
                dma("sp", qT[h][0:64, :], QS[h], ["QS"] + AR, ["qT%d" % h])
                dma("sp", kT[h][0:64, :], KS[h], ["KS"] + AR, ["kT%d" % h])

            for h in range(4):
                e1, e2 = ("pool", "dve") if h % 2 == 0 else ("dve", "pool")
                sch.add(e1, (lambda t_: (lambda e: e.memset(t_[64:128, :], 0.0)))(qT[h]), AR, ["qTpad%d" % h])
                sch.add(e2, (lambda t_: (lambda e: e.memset(t_[64:128, :], 0.0)))(kT[h]), AR, ["kTpad%d" % h])
                sch.add(e1, (lambda t_: (lambda e: e.memset(t_[64:128, :], 0.0)))(nkT[h]), AR, ["nkTpad%d" % h])
                load_head_sb(h)
            for h in range(4):
                ts("dve" if h % 2 == 0 else "pool", nkT[h][0:64, :], kT[h][0:64, :], -1.0, ALU.mult,
                   ["kT%d" % h] + AR, ["nkT%d" % h])
            for hp in range(2):
                heads = (2 * hp, 2 * hp + 1)
                blocks = [(T, kb) for T in range(NT) for kb in range(4 * T + 3, -1, -1)]
                st_ = {}

                XB = (0, 1, 2, 7)

                def stage_z(i, c):
                    h = heads[c]
                    T, kb = blocks[i]
                    qsl = slice(T * 512, (T + 1) * 512)
                    xb = XB[(2 * i + c) % 4]
                    mm(PS[xb][:], kT[h][:, kb * 128:(kb + 1) * 128], qT[h][:, qsl], True, True,
                       ["kT%d" % h, "qT%d" % h, "kTpad%d" % h, "qTpad%d" % h] + AR, ["ps%d" % xb])

                def stage_a(i, c):
                    h = heads[c]
                    T, kb = blocks[i]
                    diag = kb >= 4 * T
                    xb = XB[(2 * i + c) % 4]
                    e_, etok = w32()
                    act(e_[:], PS[xb][:], AF.Exp, ["ps%d" % xb], [etok])
                    act(e_[:], e_[:], AF.Ln, [etok], [etok], bias=1.0)
                    if diag:
                        tt("pool", e_[:], e_[:], msk[:, kb - 4 * T, :], ALU.mult, [etok, "masks"] + AR, [etok])
                    hi, hitok = hl_tile()
                    lo, lotok = hl_tile()
                    cp("dve", hi[:], e_[:], [etok] + AR, [hitok])
                    tt("dve", lo[:], e_[:], hi[:], ALU.subtract, [etok, hitok] + AR, [lotok])
                    st_[(i, c)] = [hi, hitok, lo, lotok]

                def stage_b1(i, c):
                    h = heads[c]
                    T, kb = blocks[i]
                    qsl = slice(T * 512, (T + 1) * 512)
                    nkb = 4 * T + 4
                    diag = kb >= 4 * T
                    first = kb == nkb - 1
                    hi, hitok, lo, lotok = st_[(i, c)]
                    acc = PS[ACCB[c]]
                    atok = "ps%d" % ACCB[c]
                    mm(acc[:], uincl[:], hi[:], first, False, ["uincl", hitok] + AR, [atok], sgc=True)
                    mm(acc[:], uincl[:], lo[:], False, False, ["uincl", lotok] + AR, [atok], sgc=True)
                    mm(acc[:], kT[h][:, kb * 128:(kb + 1) * 128], qT[h][:, qsl], False, True,
                       ["kT%d" % h, "qT%d" % h, "kTpad%d" % h, "qTpad%d" % h] + AR, [atok], sgc=True)
                    w_, wtok = w_tile()
                    act(w_[:], acc[:], AF.Exp, [atok] + AR, [wtok])
                    if diag:
                        tt("pool", w_[:], w_[:], msk[:, kb - 4 * T, :], ALU.mult, [wtok, "masks"] + AR, [wtok])
                    st_[(i, c)] += [w_, wtok]

                def stage_b2(i, c):
                    h = heads[c]
                    T, kb = blocks[i]
                    qsl = slice(T * 512, (T + 1) * 512)
                    nkb = 4 * T + 4
                    first = kb == nkb - 1
                    last = kb == 0
                    hi, hitok, lo, lotok, w_, wtok = st_.pop((i, c))
                    acc = PS[ACCB[c]]
                    atok = "ps%d" % ACCB[c]
                    if not last:
                        mm(acc[:], nkT[h][:, kb * 128:(kb + 1) * 128], qT[h][:, qsl], False, False,
                           ["nkT%d" % h, "nkTpad%d" % h, "qT%d" % h, "qTpad%d" % h] + AR, [atok], sgc=True)
                        mm(acc[:], nlower[:], hi[:], False, False, ["nlower", hitok] + AR, [atok], sgc=True)
                        mm(acc[:], nlower[:], lo[:], False, True, ["nlower", lotok] + AR, [atok], sgc=True)
                    ob_ = OB[c]
                    mm(PS[ob_][:], vS_[:, kb, hp * 128:(hp + 1) * 128], w_[:], first, last,
                       ["vS", wtok] + AR, ["ps%d" % ob_])
                    if last:
                        o_, otok = w32()
                        rs_ = slice(c * 64, c * 64 + 64)
                        cp("act", o_[rs_, :], PS[ob_][rs_, :], ["ps%d" % ob_], [otok])
                        dma("sp", OH[3, h, :, qsl], o_[rs_, :], [otok], ["OH3"])

                n_ = len(blocks)
                stage_z(0, 0)
                stage_z(0, 1)
                for i in range(n_ + LA):
                    if i >= LA:
                        stage_b1(i - LA, 0)
                        stage_b1(i - LA, 1)
                    if i + 1 < n_:
                        stage_z(i + 1, 0)
                        stage_z(i + 1, 1)
                    if i < n_:
                        stage_a(i, 0)
                        stage_a(i, 1)
                    if i >= LA:
                        stage_b2(i - LA, 0)
                        stage_b2(i - LA, 1)
            phase_barrier()

            carve.reset()
            qR = [carve.take([128, S], BF16) for _ in range(2)]
            kR = [carve.take([128, S], BF16) for _ in range(2)]
            kRT = carve.take([128, NB, 256], BF16)
            vR = carve.take([128, NB, 256], BF16)
            rdec = carve.take([128, 4, 512], F32)
            qdec = carve.take([128, 4, 512], F32)
            KVs = carve.take([128, 2, NCH, 64], F32)
            SPb = carve.take([128, 2, NCH, 64], BF16)
            dma("sp", kRT, KRT.rearrange("(t p) h d -> p t (h d)", p=128), ["KRT"] + AR, ["kRT"])
            dma("sp", vR, VR.rearrange("(t p) n -> p t n", p=128), ["rv_dst"] + AR, ["vR"])
            dma("sp", rdec, cst["c_rdec"], AR, ["rdec"])
            dma("sp", qdec, cst["c_qdec"], AR, ["qdec"])
            for tb in range(NB):
                for j in range(2):
                    ch = 2 * tb + j
                    rs = slice(j * 64, (j + 1) * 64)
                    for hp in range(2):
                        b = nxt("psA", 4)
                        mm(PS[b][:, 0:128], kRT[rs, tb, hp * 128:(hp + 1) * 128], vR[rs, tb, hp * 128:(hp + 1) * 128],
                           True, True, ["kRT", "vR"] + AR, ["ps%d" % b])
                        cp("dve", KVs[0:64, hp, ch, :], PS[b][0:64, 0:64], ["ps%d" % b] + AR, ["KVs"])
                        cp("act", KVs[64:128, hp, ch, :], PS[b][64:128, 64:128], ["ps%d" % b] + AR, ["KVs"])
            for ch in range(1, NCH):
                for hp in range(2):
                    stt("dve", KVs[:, hp, ch, :], KVs[:, hp, ch - 1, :], c_adec[:, hp:hp + 1], KVs[:, hp, ch, :],
                        ALU.mult, ALU.add, ["KVs", "c_adec"] + AR, ["KVs"])
            cp("dve", SPb, KVs, ["KVs"] + AR, ["SPb"])
            for h in range(4):
                i2 = h % 2
                hp = h // 2
                prs = slice(i2 * 64, i2 * 64 + 64)
                dma("sp", qR[i2][prs, :], QR[h], ["QR"] + AR, ["qR%d" % i2])
                dma("sp", kR[i2][prs, :], KR[h], ["KR"] + AR, ["kR%d" % i2])
                for T in range(NT):
                    qsl = slice(T * 512, (T + 1) * 512)
                    xb = nxt("psA", 4)
                    for s4 in range(4):
                        csl = slice(T * 512 + s4 * 128, T * 512 + (s4 + 1) * 128)
                        mm(PS[xb][:, s4 * 128:(s4 + 1) * 128], kR[i2][prs, csl], qR[i2][prs, csl], s4 == 0, s4 == 3,
                           ["kR%d" % i2, "qR%d" % i2] + AR, ["ps%d" % xb])
                    sm_, smtok = wbf()
                    tt("dve", sm_[:], PS[xb][:], rdec[:, h, :], ALU.mult, ["ps%d" % xb, "rdec"] + AR, [smtok])
                    ob_ = 4 + (T % 2)
                    for s4 in range(4):
                        tb = T * 4 + s4
                        mm(PS[ob_][0:64, s4 * 128:(s4 + 1) * 128], vR[:, tb, h * 64:(h + 1) * 64],
                           sm_[:, s4 * 128:(s4 + 1) * 128], s4 == 0, False, ["vR", smtok] + AR, ["ps%d" % ob_])
                    qd_, qdtok = wbf()
                    tt("pool", qd_[prs, :], qR[i2][prs, qsl], qdec[prs, h, :], ALU.mult,
                       ["qR%d" % i2, "qdec"] + AR, [qdtok])
                    for c8 in range(8):
                        ch = T * 8 + c8
                        if ch == 0:
                            continue
                        mm(PS[ob_][0:64, c8 * 64:(c8 + 1) * 64], SPb[prs, hp, ch - 1, :], qd_[prs, c8 * 64:(c8 + 1) * 64],
                           False, c8 == 7, ["SPb", qdtok] + AR, ["ps%d" % ob_])
                    o_, otok = w32()
                    cp("act", o_[0:64, :], PS[ob_][0:64, :], ["ps%d" % ob_], [otok])
                    dma("sp", OH[2, h, :, qsl], o_[0:64, :], [otok], ["OH2"])
            phase_barrier()

            carve.reset()
            oh = [carve.take([64, 512], F32) for _ in range(8)]
            rgt = [carve.take([64, 512], F32) for _ in range(2)]
            for T in range(NT):
                qsl = slice(T * 512, (T + 1) * 512)
                for g in (0, 1, 3):
                    for h in range(4):
                        o_ = oh[(h + 4 * (g % 2)) % 8]
                        otok = "oh%d" % ((h + 4 * (g % 2)) % 8)
                        dma("sp", o_, OH[g, h, :, qsl], ["OH%d" % g] + AR, [otok])
                        s_, stok = w32()
                        act(s_[0:64, :], o_, AF.Square, [otok] + AR, [stok])
                        mm(PS[0][0:64, :], onesf[0:64, 0:64], s_[0:64, :], h == 0, h == 3, ["onesf", stok], ["ps0"])
                    rs_, rtok = w32()
                    act(rs_[0:64, :], PS[0][0:64, :], AF.Ln, ["ps0"], [rtok], bias=EPS, scale=1.0 / 256.0)
                    act(rs_[0:64, :], rs_[0:64, :], AF.Exp, [rtok], [rtok], scale=-0.5)
                    for h in range(4):
                        o_ = oh[(h + 4 * (g % 2)) % 8]
                        otok = "oh%d" % ((h + 4 * (g % 2)) % 8)
                        m_, mtok = wbf()
                        stt("dve", m_[0:64, :], o_, gmoc[0:64, 4 * g + h:4 * g + h + 1], rs_[0:64, :], ALU.mult, ALU.mult,
                            [otok, "gmoc", rtok] + AR, [mtok])
                        dma("pool", MIXT[g * 256 + h * 64:g * 256 + (h + 1) * 64, qsl], m_[0:64, :], [mtok], ["MIXT"])
                g = 2
                for h in range(4):
                    o_ = oh[h]
                    otok = "oh%d" % h
                    dma("sp", o_, OH[2, h, :, qsl], ["OH2"] + AR, [otok])
                    gt = rgt[h % 2]
                    gtok = "rgt%d" % (h % 2)
                    dma("sp", gt, RG[h * 64:(h + 1) * 64, qsl], ["RG"] + AR, [gtok])
                    mm(PS[1][0:64, :], ones64[0:64, :], o_, True, True, ["ones64", otok] + AR, ["ps1"])
                    d_, dtok = w32()
                    tt("dve", d_[0:64, :], o_, PS[1][0:64, :], ALU.subtract, [otok, "ps1"] + AR, [dtok])
                    s_, stok = w32()
                    act(s_[0:64, :], d_[0:64, :], AF.Square, [dtok], [stok])
                    mm(PS[2][0:64, :], ones64[0:64, :], s_[0:64, :], True, True, ["ones64", stok], ["ps2"])
                    act(s_[0:64, :], PS[2][0:64, :], AF.Ln, ["ps2"], [stok], bias=EPS)
                    act(s_[0:64, :], s_[0:64, :], AF.Exp, [stok], [stok], scale=-0.5)
                    stt("dve", d_[0:64, :], d_[0:64, :], gmoc[0:64, 8 + h:9 + h], s_[0:64, :], ALU.mult, ALU.mult,
                        [dtok, "gmoc", stok], [dtok])
                    m_, mtok = wbf()
                    tt("dve", m_[0:64, :], d_[0:64, :], gt, ALU.mult, [dtok, gtok] + AR, [mtok])
                    dma("pool", MIXT[512 + h * 64:512 + (h + 1) * 64, qsl], m_[0:64, :], [mtok], ["MIXT"])
            phase_barrier()

            carve.reset()
            woutb = carve.take([128, 8, D], BF16)
            mixt = [carve.take([128, 8, 512], BF16) for _ in range(2)]
            for kc in range(8):
                dma("pool", woutb[:, kc, :], w_out[L, kc * 128:(kc + 1) * 128, :], AR, ["woutb%d" % kc])
            dma("sp", gb[1][:], g_post[L:L + 1, :].partition_broadcast(128), (), ["gb1"])
            for T in range(NT):
                mt = mixt[T % 2]
                mtok_ = "mixt%d" % (T % 2)
                dma("sp", mt, MIXT.rearrange("(k p) s -> p k s", p=128)[:, :, T * 512:(T + 1) * 512], ["MIXT"] + AR, [mtok_])
                for sub in range(4):
                    r0 = T * 512 + sub * 128
                    xi = sub % 2
                    dma("sp", xt[xi][:], x_src[r0:r0 + 128, :], ["X2"], ["xt%d" % xi])
                    for half in range(2):
                        b = 2 * xi + half
                        for kc in range(8):
                            mm(PS[b][:], mt[:, kc, sub * 128:(sub + 1) * 128], woutb[:, kc, half * 512:(half + 1) * 512],
                               kc == 0, kc == 7, [mtok_, "woutb%d" % kc] + AR, ["ps%d" % b])
                        act(yt[xi][:, half * 512:(half + 1) * 512], PS[b][:], AF.Square, ["ps%d" % b],
                            ["yt%d" % xi, "smc%d" % (8 + 2 * xi + half)], accum=small[:, 8 + 2 * xi + half:9 + 2 * xi + half])
                    col = small[:, 8 + 2 * xi:9 + 2 * xi]
                    ctok = "smc%d" % (8 + 2 * xi)
                    tt("dve", col, col, small[:, 9 + 2 * xi:10 + 2 * xi], ALU.add, [ctok, "smc%d" % (9 + 2 * xi)], [ctok])
                    rstd_col(col, ctok, 1.0 / D)
                    for half in range(2):
                        b = 2 * xi + half
                        hs = slice(half * 512, (half + 1) * 512)
                        stt("dve", yt[xi][:, hs], PS[b][:], col, gb[1][:, hs], ALU.mult, ALU.mult,
                            ["ps%d" % b, ctok, "gb1"], ["yt%d" % xi])
                    tt("dve", yt[xi][:], yt[xi][:], xt[xi][:], ALU.add, ["yt%d" % xi, "xt%d" % xi], ["yt%d" % xi])
                    dma("pool", X1[r0:r0 + 128, :], yt[xi][:], ["yt%d" % xi], ["X1"])
            phase_barrier()

            carve.reset()
            wupb = carve.take([128, 8, DFF], BF16)
            wdnb = carve.take([128, 32, D], BF16)
            TOK = 256
            uT = carve.take([128, 32, TOK], BF16)
            h2T = carve.take([128, 8, TOK], BF16)
            for kc in range(8):
                dma("pool", wupb[:, kc, :], w_up[L, kc * 128:(kc + 1) * 128, :], AR, ["wupb%d" % kc])
            for f4 in range(8):
                dma("pool", wdnb[:, 4 * f4:4 * f4 + 4, :],
                    w_dn[L, f4 * 512:(f4 + 1) * 512, :].rearrange("(f p) n -> p f n", p=128), AR, ["wdnb%d" % f4])
            dma("sp", gb[0][:], g_fpre[L:L + 1, :].partition_broadcast(128), (), ["gb0"])
            dma("sp", gb[1][:], g_fpost[L:L + 1, :].partition_broadcast(128), (), ["gb1"])
            NTT = S // TOK

            def f_load_x(T):
                for sub in range(2):
                    r0 = T * TOK + sub * 128
                    xq = (T % 2) * 2 + sub
                    dma("sp", xt[xq][:], X1[r0:r0 + 128, :], ["X1"], ["xt%d" % xq])

            def f_norm_pre(T, sub):
                xq = (T % 2) * 2 + sub
                col = small[:, 16 + sub:17 + sub]
                ctok = "smd%d" % sub
                act(hbf[sub][:], xt[xq][:], AF.Square, ["xt%d" % xq], ["hbf%d" % sub, ctok], accum=col)
                rstd_col(col, ctok, 1.0 / D)
                stt("dve", hbf[sub][:], xt[xq][:], col, gb[0][:], ALU.mult, ALU.mult,
                    ["xt%d" % xq, ctok, "gb0"], ["hbf%d" % sub])

            def f_tr(T, sub):
                psT = PS[sub][:].bitcast(BF16)
                for kc in range(8):
                    tr(psT[:, kc * 128:(kc + 1) * 128], hbf[sub][:, kc * 128:(kc + 1) * 128], ident[:],
                       ["hbf%d" % sub, "ident"], ["ps%d" % sub])
                cp("act" if sub else "dve", h2T[:, :, sub * 128:(sub + 1) * 128],
                   psT.rearrange("p (k c) -> p k c", k=8), ["ps%d" % sub] + AR, ["h2T"])

            def f_up(T):
                for fc in range(32):
                    b = fc % 4
                    for kc in range(8):
                        mm(PS[b][:, 0:TOK], wupb[:, kc, fc * 128:(fc + 1) * 128], h2T[:, kc, :], kc == 0, kc == 7,
                           ["wupb%d" % kc, "h2T"] + AR, ["ps%d" % b])
                    r_, rtok = w32()
                    act(r_[:, 0:TOK], PS[b][:, 0:TOK], AF.Relu, ["ps%d" % b], [rtok])
                    tt("pool" if fc % 2 else "dve", uT[:, fc, :], r_[:, 0:TOK], r_[:, 0:TOK], ALU.mult, [rtok] + AR, ["uT"])

            def f_down(T, sub):
                for half in range(2):
                    b = 4 + 2 * sub + half
                    for fc in range(32):
                        mm(PS[b][:], uT[:, fc, sub * 128:(sub + 1) * 128], wdnb[:, fc, half * 512:(half + 1) * 512],
                           fc == 0, fc == 31, ["uT", "wdnb%d" % (fc // 4)] + AR, ["ps%d" % b])

            def f_post(T, sub):
                r0 = T * TOK + sub * 128
                xq = (T % 2) * 2 + sub
                for half in range(2):
                    b = 4 + 2 * sub + half
                    act(yt[sub][:, half * 512:(half + 1) * 512], PS[b][:], AF.Square, ["ps%d" % b],
                        ["yt%d" % sub, "sme%d" % (2 * sub + half)],
                        accum=small[:, 24 + 2 * sub + half:25 + 2 * sub + half])
                col = small[:, 24 + 2 * sub:25 + 2 * sub]
                ctok = "sme%d" % (2 * sub)
                tt("dve", col, col, small[:, 25 + 2 * sub:26 + 2 * sub], ALU.add, [ctok, "sme%d" % (2 * sub + 1)], [ctok])
                rstd_col(col, ctok, 1.0 / D)
                for half in range(2):
                    b = 4 + 2 * sub + half
                    hs = slice(half * 512, (half + 1) * 512)
                    stt("dve", yt[sub][:, hs], PS[b][:], col, gb[1][:, hs], ALU.mult, ALU.mult,
                        ["ps%d" % b, ctok, "gb1"], ["yt%d" % sub])
                tt("pool", yt[sub][:], yt[sub][:], xt[xq][:], ALU.add, ["yt%d" % sub, "xt%d" % xq], ["yt%d" % sub])
                dma("sp", x_dst[r0:r0 + 128, :], yt[sub][:], ["yt%d" % sub], ["X2" if x_dst is X2 else "Y"])

            f_load_x(0)
            for sub in range(2):
                f_norm_pre(0, sub)
                f_tr(0, sub)
            for T in range(NTT):
                nxt_ = T + 1 < NTT
                if nxt_:
                    f_load_x(T + 1)
                f_up(T)
                f_down(T, 0)
                f_post(T, 0)
                if nxt_:
                    f_norm_pre(T + 1, 0)
                f_down(T, 1)
                if nxt_:
                    f_norm_pre(T + 1, 1)
                    f_tr(T + 1, 0)
                    f_tr(T + 1, 1)
                f_post(T, 1)
            phase_barrier()

        sch.add("sp", lambda e: e.nop(), (), list(set(sch.lastw.keys()) | set(sch.readers.keys())), force=True)
        sch.emit(nc, st)
    return nc, sch


_CACHE = {}


def make_in_maps(inputs, S, n_cores):
    consts = host_consts()
    maps = []
    f = lambda a: np.ascontiguousarray(np.asarray(a, dtype=np.float32))
    gq = f(inputs["g_q_lora"]).reshape(2, 2, 128).transpose(0, 2, 1)
    gkv = f(inputs["g_kv_lora"]).reshape(2, 128, 1)
    gmo = f(inputs["g_mix_out"]).reshape(2, 16, 64).transpose(0, 2, 1)
    gmo = np.concatenate([gmo, gmo], axis=1)
    bfc = f(inputs["b_forget"]).reshape(2, 4, 1)
    shared = dict(w_in=f(inputs["w_in"]), w_q_up=f(inputs["w_q_up"]), w_kv_up=f(inputs["w_kv_up"]),
                  w_out=f(inputs["w_out"]), w_ffn_up=f(inputs["w_ffn_up"]), w_ffn_down=f(inputs["w_ffn_down"]),
                  g_mix_pre=f(inputs["g_mix_pre"]), g_mix_post=f(inputs["g_mix_post"]),
                  g_ffn_pre=f(inputs["g_ffn_pre"]), g_ffn_post=f(inputs["g_ffn_post"]),
                  gq_col=np.ascontiguousarray(gq), gkv_col=np.ascontiguousarray(gkv),
                  gmo_col=np.ascontiguousarray(gmo), bf_col=np.ascontiguousarray(bfc))
    shared.update(consts)
    xs = f(inputs["x"])
    ps = np.asarray(inputs["positions"]).astype(np.int32)
    for c in range(n_cores):
        m = dict(shared)
        m["x"] = np.ascontiguousarray(xs[c])
        m["pos"] = np.ascontiguousarray(ps[c:c + 1])
        maps.append(m)
    return maps


def kernel(**inputs):
    x = np.asarray(inputs["x"])
    B, S, _ = x.shape
    if S not in _CACHE:
        _CACHE[S] = build(S)[0]
    nc = _CACHE[S]
    maps = make_in_maps(inputs, S, B)
    res = run_bass_kernel_spmd(nc, maps, core_ids=list(range(B)))
    return np.stack([np.asarray(r["y"], dtype=np.float32) for r in res.results], axis=0)
```

```python
import math
import numpy as np
import ml_dtypes
from contextlib import ExitStack
import concourse.bass as bass
import concourse.mybir as mybir
from concourse.bass_utils import run_bass_kernel_spmd

F32 = mybir.dt.float32
BF16 = mybir.dt.bfloat16
I32 = mybir.dt.int32
AF = mybir.ActivationFunctionType
ALU = mybir.AluOpType

COMPUTE = ("pe", "act", "dve", "pool", "sp")
N_DMA_SEMS = 48
SAME_ENGINE_SYNC = True

D = 1024
DFF = 4096
NIN = 2980
EPS = 1e-6
OFF = dict(fq=0, fk=256, fv=512, ff=768, cq=772, ckv=1028, kr=1156, rq=1188, rk=1444, rv=1700,
           rg=1956, sq=2212, sk=2468, sv=2724, rqp=2980, rkp=3236, krp=3492)
NINP = 3524


class _Op:
    __slots__ = ("fn", "deps", "raw", "ndma", "signal", "sem", "val", "clock", "queue")

    def __init__(self, fn, deps, ndma, queue, raw=()):
        self.fn = fn
        self.deps = deps
        self.raw = raw
        self.ndma = ndma
        self.signal = False
        self.sem = None
        self.val = 0
        self.clock = None
        self.queue = queue


class Sched:
    def __init__(self):
        self.ops = []
        self.lastw = {}
        self.readers = {}

    def add(self, eng, fn, reads=(), writes=(), ndma=0, force=False):
        import os
        mx = int(os.environ.get("MAX_OPS", "0"))
        if mx and len(self.ops) >= mx and not force:
            return -1
        i = len(self.ops)
        deps = set()
        if any(isinstance(t, str) and t.startswith("ps") for t in reads):
            writes = list(writes) + [t for t in reads if isinstance(t, str) and t.startswith("ps") and t not in writes]
            reads = [t for t in reads if not (isinstance(t, str) and t.startswith("ps"))]
        raw = set()
        for t in reads:
            w = self.lastw.get(t)
            if w is not None:
                deps.add(w)
                raw.add(w)
        for t in writes:
            w = self.lastw.get(t)
            if w is not None:
                deps.add(w)
            r = self.readers.get(t)
            if r:
                deps.update(r)
        for t in reads:
            self.readers.setdefault(t, []).append(i)
        for t in writes:
            self.lastw[t] = i
            self.readers[t] = []
        self.ops.append(_Op(fn, deps, ndma, eng, raw))
        return i

    def emit(self, nc, stack):
        ops = self.ops
        queues = {}
        for i, op in enumerate(ops):
            queues.setdefault(op.queue, []).append(i)
        esem = {q: stack.enter_context(nc.semaphore("s_" + q)) for q in COMPUTE}
        dsems = [stack.enter_context(nc.semaphore("d_%d" % k)) for k in range(N_DMA_SEMS)]
        dcount = [0] * N_DMA_SEMS
        dlast = [None] * N_DMA_SEMS
        N_SW = 8
        kk = {"pool": 0, "sp": 0}
        for i, op in enumerate(ops):
            if op.ndma:
                if op.queue == "pool":
                    k = kk["pool"] % N_SW
                    kk["pool"] += 1
                else:
                    k = N_SW + kk["sp"] % (N_DMA_SEMS - N_SW)
                    kk["sp"] += 1
                op.sem = ("d", k)
                if dlast[k] is not None:
                    op.deps.add(dlast[k])
                dlast[k] = i
                dcount[k] += 16 * op.ndma
                op.val = dcount[k]
                op.signal = True

        def skip(dop, op):
            if dop.ndma or op.ndma or dop.queue != op.queue:
                return False
            return dop.queue == "pe" or not SAME_ENGINE_SYNC

        opidx = {id(o): i for i, o in enumerate(ops)}

        for op in ops:
            for d in op.deps:
                dop = ops[d]
                if dop.ndma or skip(dop, op):
                    continue
                dop.signal = True
        cnt = {q: 0 for q in COMPUTE}
        for op in ops:
            if not op.ndma:
                if op.signal:
                    cnt[op.queue] += 1
                op.sem = ("e", op.queue)
                op.val = cnt[op.queue]
        kn = {q: {} for q in queues}
        for op in ops:
            kq = kn[op.queue]
            for d in op.deps:
                dop = ops[d]
                if kq.get(dop.sem, 0) < dop.val:
                    kq[dop.sem] = dop.val
                if dop.clock:
                    for s, v in dop.clock.items():
                        if kq.get(s, 0) < v:
                            kq[s] = v
            if op.signal:
                c = dict(kq)
                c[op.sem] = op.val
                op.clock = c

        def semobj(s):
            return esem[s[1]] if s[0] == "e" else dsems[s[1]]

        block = stack.enter_context(nc.Block())
        self.nwaits = 0

        def run_queue(q, eng):
            known = {}
            for i in queues[q]:
                op = ops[i]
                need = {}
                for d in op.deps:
                    dop = ops[d]
                    if skip(dop, op):
                        continue
                    if known.get(dop.sem, 0) >= dop.val:
                        continue
                    if need.get(dop.sem, 0) < dop.val:
                        need[dop.sem] = dop.val
                for d in op.deps:
                    dop = ops[d]
                    if skip(dop, op):
                        continue
                    if dop.clock:
                        for s, v in dop.clock.items():
                            if known.get(s, 0) < v:
                                known[s] = v
                for s, v in need.items():
                    eng.wait_ge(semobj(s), v)
                    self.nwaits += 1
                    if known.get(s, 0) < v:
                        known[s] = v
                ins = op.fn(eng)
                if op.ndma:
                    lst = ins if isinstance(ins, (list, tuple)) else [ins]
                    assert len(lst) == op.ndma
                    for x_ in lst:
                        x_.then_inc(semobj(op.sem), 16)
                elif op.signal:
                    ins.then_inc(semobj(op.sem), 1)

        def mk(q):
            return lambda eng: run_queue(q, eng)

        handlers = {"pe": block.tensor, "act": block.scalar, "dve": block.vector,
                    "pool": block.gpsimd, "sp": block.sync}
        for q in queues:
            handlers[q](mk(q))


def host_consts():
    c = {}
    bf = ml_dtypes.bfloat16
    c["c_ident"] = np.eye(128, dtype=np.float32).astype(bf)
    kk = np.arange(128)[:, None, None]
    jj = np.arange(4)[None, :, None]
    qq = np.arange(512)[None, None, :]
    key = 128 * jj + kk
    m = np.zeros((128, 3, 4, 512), np.float32)
    m[:, 0] = (key <= qq)
    m[:, 1] = ((key // 64) <= (qq // 64))
    m[:, 2] = (key < qq)
    c["c_masks"] = m.astype(bf)
    j = np.arange(128)[:, None]
    s = np.arange(128)[None, :]
    c["c_uincl"] = (-(j >= s).astype(np.float32)).astype(bf)
    c["c_nlower"] = (-(j < s).astype(np.float32)).astype(bf)
    h = np.arange(4, dtype=np.float32)
    log_gamma = np.log1p(-np.power(np.float32(2.0), np.float32(-5.0) - h)).astype(np.float32)
    idx = np.arange(64, dtype=np.float32)
    m_ = np.arange(128)
    dec = np.zeros((128, 4, 128), np.float32)
    for hh in range(4):
        dd = np.exp(log_gamma[hh] * np.abs(m_[:, None] - m_[None, :]).astype(np.float32)).astype(np.float32)
        same = (m_[:, None] // 64) == (m_[None, :] // 64)
        dec[:, hh, :] = np.where(same, dd, 0.0)
    c["c_rdec"] = np.tile(dec[:, :, None, :], (1, 1, 4, 1)).reshape(128, 4, 512).astype(np.float32)
    tail = np.exp(log_gamma[None, :] * (63.0 - idx)[:, None]).astype(np.float32)
    c["c_tail"] = np.tile(tail, (2, 1)).astype(np.float32)
    qh = np.exp(log_gamma[None, :] * (idx + 1.0)[:, None]).astype(np.float32)
    qd = np.tile(qh.T[None, :, None, :], (128, 1, 8, 1)).reshape(128, 4, 512)
    c["c_qdec"] = qd.astype(np.float32)
    a_ = np.exp(log_gamma * np.float32(64.0)).astype(np.float32)
    ad = np.zeros((128, 2), np.float32)
    for hp_ in range(2):
        ad[0:64, hp_] = a_[2 * hp_]
        ad[64:128, hp_] = a_[2 * hp_ + 1]
    c["c_adec"] = ad
    r = np.arange(128)
    invf_r = (np.float32(10000.0) ** (-(r % 32).astype(np.float32) / np.float32(32))).astype(np.float32)
    invf_m = (np.float32(10000.0) ** (-(r % 16).astype(np.float32) / np.float32(16))).astype(np.float32)
    sg_r = np.where((r % 64) < 32, -1.0, 1.0).astype(np.float32)
    sg_m = np.where((r % 32) < 16, -1.0, 1.0).astype(np.float32)
    c["c_rope"] = np.stack([invf_r, invf_m, sg_r, sg_m], axis=1).astype(np.float32)
    return c


CONST_SHAPES = dict(c_ident=([128, 128], BF16), c_masks=([128, 3, 4, 512], BF16), c_uincl=([128, 128], BF16), c_nlower=([128, 128], BF16),
                    c_rdec=([128, 4, 512], F32), c_tail=([128, 4], F32), c_qdec=([128, 4, 512], F32),
                    c_adec=([128, 2], F32), c_rope=([128, 4], F32))


def build(S, dbg=False, nlayers=2):
    NT = S // 512
    NB = S // 128
    NCH = S // 64
    nc = bass.Bass("TRN2", target_bir_lowering=False)
    sch = Sched()

    def din(name, shape, dt=F32):
        return nc.dram_tensor(name, shape, dt, kind="ExternalInput").ap()

    def dscr(name, shape, dt=F32):
        return nc.dram_tensor(name, shape, dt, kind=("ExternalOutput" if dbg else "Internal")).ap()

    x_in = din("x", [S, D])
    pos = din("pos", [1, S], I32)
    w_in = din("w_in", [2, D, NIN])
    w_q_up = din("w_q_up", [2, 256, 384])
    w_kv_up = din("w_kv_up", [2, 128, 512])
    w_out = din("w_out", [2, D, D])
    w_up = din("w_ffn_up", [2, D, DFF])
    w_dn = din("w_ffn_down", [2, DFF, D])
    g_pre = din("g_mix_pre", [2, D])
    g_post = din("g_mix_post", [2, D])
    g_fpre = din("g_ffn_pre", [2, D])
    g_fpost = din("g_ffn_post", [2, D])
    gq_col = din("gq_col", [2, 128, 2])
    gkv_col = din("gkv_col", [2, 128, 1])
    gmo_col = din("gmo_col", [2, 128, 16])
    bf_col = din("bf_col", [2, 4, 1])
    cst = {k: din(k, sh, dt) for k, (sh, dt) in CONST_SHAPES.items()}
    y_out = nc.dram_tensor("y", [S, D], F32, kind="ExternalOutput").ap()

    COSR = dscr("COSR", [128, S]); SINR = dscr("SINR", [128, S])
    COSM = dscr("COSM", [128, S]); SINM = dscr("SINM", [128, S])
    QF = dscr("QF", [4, 67, S], BF16); KF = dscr("KF", [4, 67, S], BF16)
    VF = dscr("VF", [S, 256], BF16); FFL = dscr("FFL", [4, S])
    CNEG = dscr("CNEG", [S, 4])
    QM = dscr("QM", [4, 96, S], BF16); KM = dscr("KM", [4, 96, S], BF16); VM = dscr("VM", [S, 256], BF16)
    QR = dscr("QR", [4, 64, S], BF16); KR = dscr("KR", [4, 64, S], BF16)
    KRT = dscr("KRT", [S, 4, 64], BF16); VR = dscr("VR", [S, 256], BF16); RG = dscr("RG", [256, S])
    QS = dscr("QS", [4, 64, S], BF16); KS = dscr("KS", [4, 64, S], BF16); VS = dscr("VS", [S, 256], BF16)
    OH = dscr("OH", [4, 4, 64, S])
    MIXT = dscr("MIXT", [D, S], BF16)
    X1 = dscr("X1", [S, D])
    X2 = dscr("X2", [S, D])

    st = ExitStack()
    with st:
        def sb(name, shape, dt):
            return st.enter_context(nc.sbuf_tensor("sb_" + name, shape, dt))

        ident = sb("ident", [128, 128], BF16)
        identf = sb("identf", [128, 128], F32)
        uincl = sb("uincl", [128, 128], BF16)
        nlower = sb("nlower", [128, 128], BF16)
        onesb = sb("onesb", [128, 128], BF16)
        onesf = sb("onesf", [128, 128], F32)
        ones64 = sb("ones64", [128, 64], F32)
        c_tail = sb("c_tail", [128, 4], F32)
        c_adec = sb("c_adec", [128, 2], F32)
        c_rope = sb("c_rope", [128, 4], F32)
        gqc = sb("gqc", [128, 2], F32)
        gkvc = sb("gkvc", [128, 1], F32)
        gmoc = sb("gmoc", [128, 16], F32)
        bfc = sb("bfc", [4, 1], F32)
        small = sb("small", [128, 64], F32)
        ARENA = 150 * 1024
        arena = sb("arena", [128, ARENA], mybir.dt.uint8)
        NW = 5
        wk32 = [sb("wk32_%d" % i, [128, 512], F32) for i in range(NW)]
        NWB = 8
        wkbf = [sb("wkbf_%d" % i, [128, 512], BF16) for i in range(NWB)]
        xt = [sb("xt_%d" % i, [128, 1024], F32) for i in range(4)]
        yt = [sb("yt_%d" % i, [128, 1024], F32) for i in range(2)]
        hbf = [sb("hbf_%d" % i, [128, 1024], BF16) for i in range(2)]
        gb = [sb("gb_%d" % i, [128, 1024], F32) for i in range(2)]
        PS = [st.enter_context(nc.psum_tensor("ps%d" % i, [128, 512], F32)) for i in range(8)]

        class Carver:
            def __init__(self):
                self.off = 0

            def reset(self):
                self.off = 0

            def take(self, shape, dt):
                esz = {F32: 4, BF16: 2, I32: 4}[dt]
                n = 1
                for d_ in shape[1:]:
                    n *= d_
                nbytes = (n * esz + 63) // 64 * 64
                assert self.off + nbytes <= ARENA, (self.off, nbytes)
                v = arena[0:shape[0], self.off:self.off + n * esz].bitcast(dt)
                self.off += nbytes
                if len(shape) > 2:
                    names = " ".join("d%d" % i for i in range(1, len(shape)))
                    kw = {"d%d" % i: shape[i] for i in range(1, len(shape))}
                    v = v.rearrange("p (%s) -> p %s" % (names, names), **kw)
                return v

        carve = Carver()
        phase_ctr = [0]

        def phase_barrier():
            phase_ctr[0] += 1
            sch.add("pool", lambda e: e.memset(small[:, 63:64], 0.0), reads=["small63"], writes=["ARENA", "small63"])

        AR = ["ARENA"]

        DRAM_TOKS = set(["QF", "QFc", "KF", "fv_dst", "mla_dst", "rv_dst", "sv_dst", "FFL", "CNEG", "QM", "KM", "QR", "KR",
                         "KRT", "RG", "QS", "KS", "OH0", "OH1", "OH2", "OH3", "MIXT", "X1", "X2", "Y",
                         "COS0", "COS1", "SIN0", "SIN1"] + ["KF1_%d" % j for j in range(4)])

        def dma(q, out, in_, reads=(), writes=()):
            r2 = [t for t in reads if t not in DRAM_TOKS] + [t for t in writes if t in DRAM_TOKS]
            w2 = [t for t in writes if t not in DRAM_TOKS] + [t for t in reads if t in DRAM_TOKS]
            sch.add(q, lambda e: e.dma_start(out=out, in_=in_), r2, w2, ndma=1)

        def mm(out, lhsT, rhs, start, stop, reads, writes, sgc=False):
            if sgc:
                sch.add("pe", lambda e: e.matmul(out, lhsT=lhsT, rhs=rhs, start=start, stop=stop, skip_group_check=True),
                        reads, writes)
            else:
                sch.add("pe", lambda e: e.matmul(out, lhsT=lhsT, rhs=rhs, start=start, stop=stop), reads, writes)

        def tr(out, in_, idn, reads, writes):
            sch.add("pe", lambda e: e.transpose(out=out, in_=in_, identity=idn), reads, writes)

        def act(out, in_, func, reads, writes, bias=None, scale=None, accum=None):
            kw = {}
            if bias is not None:
                kw["bias"] = bias
            if scale is not None:
                kw["scale"] = scale
            if accum is not None:
                kw["accum_out"] = accum
            sch.add("act", lambda e: e.activation(out=out, in_=in_, func=func, **kw), reads, writes)

        def tt(eng, out, in0, in1, op, reads, writes):
            sch.add(eng, lambda e: e.tensor_tensor(out=out, in0=in0, in1=in1, op=op), reads, writes)

        def ts(eng, out, in0, s1, op0, reads, writes, s2=None, op1=None):
            if op1 is None:
                sch.add(eng, lambda e: e.tensor_scalar(out=out, in0=in0, scalar1=s1, scalar2=None, op0=op0), reads, writes)
            else:
                sch.add(eng, lambda e: e.tensor_scalar(out=out, in0=in0, scalar1=s1, scalar2=s2, op0=op0, op1=op1),
                        reads, writes)

        def stt(eng, out, in0, scalar, in1, op0, op1, reads, writes):
            sch.add(eng, lambda e: e.scalar_tensor_tensor(out=out, in0=in0, scalar=scalar, in1=in1, op0=op0, op1=op1),
                    reads, writes)

        def cp(eng, out, in_, reads, writes):
            if eng == "act":
                act(out, in_, AF.Copy, reads, writes)
            else:
                sch.add(eng, lambda e: e.tensor_copy(out=out, in_=in_), reads, writes)

        def recip(out, in_, reads, writes):
            sch.add("dve", lambda e: e.reciprocal(out=out, in_=in_), reads, writes)

        def memset(eng, ap, val, writes):
            sch.add(eng, lambda e: e.memset(ap, val), (), writes)

        rr = {"w32": 0, "wbf": 0, "psA": 0, "ev": 0, "psX3": 0}

        def nxt(key, n):
            v = rr[key]
            rr[key] = (v + 1) % n
            return v

        def w32():
            i = nxt("w32", NW)
            return wk32[i], "wk32_%d" % i

        def wbf():
            i = nxt("wbf", NWB)
            return wkbf[i], "wkbf_%d" % i

        def evq():
            return ("act", "dve")[nxt("ev", 2)]

        def rstd_col(col_ap, tok, n_inv):
            act(col_ap, col_ap, AF.Ln, [tok], [tok], bias=EPS, scale=n_inv)
            act(col_ap, col_ap, AF.Exp, [tok], [tok], scale=-0.5)

        dma("sp", ident[:], cst["c_ident"], (), ["ident"])
        dma("sp", uincl[:], cst["c_uincl"], (), ["uincl"])
        dma("sp", nlower[:], cst["c_nlower"], (), ["nlower"])
        dma("sp", c_tail[:], cst["c_tail"], (), ["c_tail"])
        dma("sp", c_adec[:], cst["c_adec"], (), ["c_adec"])
        dma("sp", c_rope[:], cst["c_rope"], (), ["c_rope"])
        memset("pool", onesb[:], 1.0, ["onesb"])
        memset("pool", onesf[:], 1.0, ["onesf"])
        memset("pool", ones64[:], 1.0 / 64.0, ["ones64"])
        memset("pool", small[:], 0.0, ["small63"])
        cp("dve", identf[:], ident[:], ["ident"], ["identf"])
        carve.reset()
        ob = carve.take([128, S], BF16)
        sch.add("pool", lambda e: e.memset(ob, 1.0), AR, ["ob"])
        for h in range(4):
            dma("sp", KF[h, 64:67, :], ob[0:3, :], ["ob"] + AR, ["KF1_%d" % h])
        posi = carve.take([128, S], I32)
        posf = carve.take([128, S], F32)
        ang = carve.take([128, S], F32)
        kf_ = carve.take([128, S], F32)
        ki_ = carve.take([128, S], I32)
        dma("sp", posi, pos.partition_broadcast(128), AR, ["posi"])
        cp("dve", posf, posi, ["posi"] + AR, ["posf"])
        TWO_PI = 2.0 * math.pi
        C1 = 6.28125
        C2 = TWO_PI - C1
        for ti, (COS, SIN) in enumerate(((COSR, SINR), (COSM, SINM))):
            ts("dve", ang, posf, c_rope[:, ti:ti + 1], ALU.mult, ["posf", "c_rope"] + AR, ["ang"])
            for which in (0, 1):
                ts("dve", ki_, ang, 1.0 / TWO_PI, ALU.mult, ["ang"] + AR, ["ki"])
                cp("dve", kf_, ki_, ["ki"] + AR, ["kf"])
                stt("dve", posi.bitcast(F32), kf_, -C1, ang, ALU.mult, ALU.add, ["kf", "ang"] + AR, ["red"])
                red = posi.bitcast(F32)
                stt("dve", red, kf_, -C2, red, ALU.mult, ALU.add, ["kf", "red"] + AR, ["red"])
                if which == 1:
                    ts("dve", red, red, math.pi / 2.0, ALU.add, ["red"] + AR, ["red"])
                    ts("dve", kf_, red, math.pi, ALU.is_gt, ["red"] + AR, ["kf"])
                    stt("dve", red, kf_, -TWO_PI, red, ALU.mult, ALU.add, ["kf", "red"] + AR, ["red"])
                ts("dve", red, red, 3.141592, ALU.min, ["red"] + AR, ["red"], s2=-3.141592, op1=ALU.max)
                act(kf_, red, AF.Sin, ["red"] + AR, ["kf"])
                if which == 0:
                    ts("dve", kf_, kf_, c_rope[:, 2 + ti:3 + ti], ALU.mult, ["kf", "c_rope"] + AR, ["kf"])
                    dma("sp", SIN, kf_, ["kf"] + AR, ["SIN%d" % ti])
                else:
                    dma("sp", COS, kf_, ["kf"] + AR, ["COS%d" % ti])
        phase_barrier()

        for L in range(nlayers):
            if dbg == "setup":
                break
            x_src = x_in if L == 0 else X2
            x_dst = y_out if L == nlayers - 1 else X2
            dma("sp", gqc[:], gq_col[L], (), ["gqc"])
            dma("sp", gkvc[:], gkv_col[L], (), ["gkvc"])
            dma("sp", gmoc[:], gmo_col[L], (), ["gmoc"])
            dma("sp", bfc[:], bf_col[L], (), ["bfc"])

            carve.reset()
            winb = carve.take([128, 8, NINP], BF16)
            wq32 = carve.take([128, 2, 384], F32)
            wqn = carve.take([128, 2, 256], BF16)
            wqr = carve.take([128, 2, 128], BF16)
            wqp = carve.take([128, 2, 128], BF16)
            wkv32 = carve.take([128, 512], F32)
            wkn = carve.take([128, 256], BF16)
            wvv = carve.take([128, 256], BF16)
            hT = [carve.take([128, 8, 512], BF16) for _ in range(2)]
            tabs = [[carve.take([128, 512], F32) for _ in range(4)] for _ in range(2)]
            xt4 = [carve.take([128, 1024], F32) for _ in range(4)]
            cq32 = [carve.take([128, 512], F32) for _ in range(2)]
            sq32 = [carve.take([128, 512], F32) for _ in range(3)]
            cqn = [carve.take([128, 512], BF16) for _ in range(2)]
            ckvn = carve.take([128, 512], BF16)
            rstdb = carve.take([128, 512], F32)
            vstg = [carve.take([128, 4, 256], BF16) for _ in range(3)]
            krtstg = carve.take([128, 4, 256], BF16)
            c_qdummy = None

            for kc in range(8):
                rows = slice(kc * 128, (kc + 1) * 128)
                dma("pool", winb[:, kc, 0:NIN], w_in[L, rows, :], AR, ["winb%d" % kc])
                for nm, nmp, nh, hw in (("rq", "rqp", 4, 32), ("rk", "rkp", 4, 32), ("kr", "krp", 1, 16)):
                    src = winb[:, kc, OFF[nm]:OFF[nm] + nh * 2 * hw].rearrange("p (h t c) -> p h t c", h=nh, t=2)
                    dst = winb[:, kc, OFF[nmp]:OFF[nmp] + nh * 2 * hw].rearrange("p (h t c) -> p h t c", h=nh, t=2)
                    cp("pool", dst[:, :, 0, :], src[:, :, 1, :], ["winb%d" % kc] + AR, ["winbp%d" % kc])
                    cp("pool", dst[:, :, 1, :], src[:, :, 0, :], ["winb%d" % kc] + AR, ["winbp%d" % kc])
            if dbg == "p1a":
                break
            dma("sp", wq32, w_q_up[L].rearrange("(c p) n -> p c n", p=128), AR, ["wq32"])
            dma("sp", wkv32, w_kv_up[L], AR, ["wkv32"])
            for c in range(2):
                src4 = wq32[:, c, :].rearrange("p (h e) -> p h e", h=4)
                gcol = gqc[:, c:c + 1]
                ts("dve", wqn[:, c, :].rearrange("p (h e) -> p h e", h=4), src4[:, :, 0:64], gcol, ALU.mult,
                   ["wq32", "gqc"] + AR, ["wqb"])
                ts("dve", wqr[:, c, :].rearrange("p (h e) -> p h e", h=4), src4[:, :, 64:96], gcol, ALU.mult,
                   ["wq32", "gqc"] + AR, ["wqb"])
                dstp = wqp[:, c, :].rearrange("p (h t e) -> p h t e", h=4, t=2)
                ts("dve", dstp[:, :, 0, :], src4[:, :, 80:96], gcol, ALU.mult, ["wq32", "gqc"] + AR, ["wqb"])
                ts("dve", dstp[:, :, 1, :], src4[:, :, 64:80], gcol, ALU.mult, ["wq32", "gqc"] + AR, ["wqb"])
            kv4 = wkv32.rearrange("p (h e) -> p h e", h=4)
            ts("dve", wkn.rearrange("p (h e) -> p h e", h=4), kv4[:, :, 0:64], gkvc[:, 0:1], ALU.mult,
               ["wkv32", "gkvc"] + AR, ["wkvb"])
            ts("dve", wvv.rearrange("p (h e) -> p h e", h=4), kv4[:, :, 64:128], gkvc[:, 0:1], ALU.mult,
               ["wkv32", "gkvc"] + AR, ["wkvb"])
            dma("sp", gb[0][:], g_pre[L:L + 1, :].partition_broadcast(128), (), ["gb0"])

            WIN_ALL = ["winb%d" % k_ for k_ in range(8)] + ["winbp%d" % k_ for k_ in range(8)]

            def fm_group(hTt, hTtok, col_lo, M, ncols_stride=None):
                b = nxt("psA", 4)
                ps = PS[b]
                for kc in range(8):
                    mm(ps[0:M, :], winb[:, kc, col_lo:col_lo + M], hTt[:, kc, :], kc == 0, kc == 7,
                       ["winb%d" % kc, "winbp%d" % kc, hTtok] + AR, ["ps%d" % b])
                return ps, "ps%d" % b

            def store_rows(stg, stok, dsts, tsl):
                for (r0, r1, dram_rows, wtok) in dsts:
                    dma("sp", dram_rows[:, tsl], stg[r0:r1, :], [stok], [wtok])

            def load_tile_inputs(t):
                tsl_ = slice(t * 512, (t + 1) * 512)
                p_ = t % 2
                for sub in range(4):
                    r0 = t * 512 + sub * 128
                    dma("pool", xt4[sub], x_src[r0:r0 + 128, :], ["X2"] + AR, ["xt4_%d" % sub])
                for j, (SRC, tk) in enumerate(((COSR, "COS0"), (SINR, "SIN0"), (COSM, "COS1"), (SINM, "SIN1"))):
                    dma("pool", tabs[p_][j], SRC[:, tsl_], [tk] + AR, ["tab%d_%d" % (p_, j)])

            def norm_tile(t):
                hTt = hT[t % 2]
                hTtok = "hT%d" % (t % 2)
                p_ = t % 2
                for sub in range(4):
                    xi = sub % 2
                    xs = xt4[sub]
                    xtok = "xt4_%d" % sub
                    col = small[:, sub:sub + 1]
                    ctok = "sm%d" % sub
                    act(hbf[xi][:], xs, AF.Square, [xtok] + AR, ["hbf%d" % xi, ctok], accum=col)
                    rstd_col(col, ctok, 1.0 / D)
                    stt("dve", hbf[xi][:], xs, col, gb[0][:], ALU.mult, ALU.mult,
                        [xtok, ctok, "gb0"] + AR, ["hbf%d" % xi])
                    psT = PS[4][:].bitcast(BF16)
                    for kc in range(8):
                        tr(psT[:, kc * 128:(kc + 1) * 128], hbf[xi][:, kc * 128:(kc + 1) * 128], ident[:],
                           ["hbf%d" % xi, "ident"], ["ps4"])
                    cp("act" if sub % 2 else "dve", hTt[:, :, sub * 128:(sub + 1) * 128],
                       psT.rearrange("p (k c) -> p k c", k=8), ["ps4"] + AR, [hTtok])

            load_tile_inputs(0)
            norm_tile(0)
            for t in range(NT):
                tsl = slice(t * 512, (t + 1) * 512)
                hTt = hT[t % 2]
                hTtok = "hT%d" % (t % 2)
                if t + 1 < NT:
                    load_tile_inputs(t + 1)
                cr, sr, cm, sm = tabs[t % 2]
                crk, srk, cmk, smk = ["tab%d_%d" % (t % 2, j) for j in range(4)]
                for c in range(2):
                    ps, ptok = fm_group(hTt, hTtok, OFF["cq"] + c * 128, 128)
                    act(sq32[c], ps[:], AF.Square, [ptok] + AR, ["sq32_%d" % c])
                    cp("dve", cq32[c], ps[:], [ptok] + AR, ["cq32_%d" % c])
                for c in range(2):
                    mm(PS[6][:], onesf[:], sq32[c], c == 0, c == 1, ["onesf", "sq32_%d" % c] + AR, ["ps6"])
                act(rstdb, PS[6][:], AF.Ln, ["ps6"] + AR, ["rstdb"], bias=EPS, scale=1.0 / 256.0)
                act(rstdb, rstdb, AF.Exp, ["rstdb"] + AR, ["rstdb"], scale=-0.5)
                for c in range(2):
                    tt("pool" if c else "dve", cqn[c], cq32[c], rstdb, ALU.mult, ["cq32_%d" % c, "rstdb"] + AR, ["cqn%d" % c])
                ps, ptok = fm_group(hTt, hTtok, OFF["ckv"], 128)
                act(sq32[2], ps[:], AF.Square, [ptok] + AR, ["sq32_2"])
                cp("dve", cq32[0], ps[:], [ptok] + AR, ["cq32_0"])
                mm(PS[6][:], onesf[:], sq32[2], True, True, ["onesf", "sq32_2"] + AR, ["ps6"])
                act(rstdb, PS[6][:], AF.Ln, ["ps6"] + AR, ["rstdb"], bias=EPS, scale=1.0 / 128.0)
                act(rstdb, rstdb, AF.Exp, ["rstdb"] + AR, ["rstdb"], scale=-0.5)
                tt("dve", ckvn, cq32[0], rstdb, ALU.mult, ["cq32_0", "rstdb"] + AR, ["ckvn"])
                def simple_pair(col_lo, scale, DST, wtok):
                    for hp in range(2):
                        ps, ptok = fm_group(hTt, hTtok, col_lo + hp * 128, 128)
                        stg, stok = wbf()
                        if scale is None:
                            cp(evq(), stg[:], ps[:], [ptok], [stok])
                        else:
                            act(stg[:], ps[:], AF.Copy, [ptok], [stok], scale=scale)
                        store_rows(stg, stok, [(0, 64, DST[2 * hp, 0:64], wtok), (64, 128, DST[2 * hp + 1, 0:64], wtok)], tsl)

                simple_pair(OFF["fq"], 0.125, QF, "QF")
                simple_pair(OFF["fk"], None, KF, "KF")
                simple_pair(OFF["sq"], 0.125, QS, "QS")
                simple_pair(OFF["sk"], None, KS, "KS")
                ps, ptok = fm_group(hTt, hTtok, OFF["ff"], 4)
                stg, stok = w32()
                cp("dve", stg[0:4, :], ps[0:4, :], [ptok], [stok])
                dma("sp", FFL[:, tsl], stg[0:4, :], [stok], ["FFL"])
                for c in range(2):
                    ps, ptok = fm_group(hTt, hTtok, OFF["rg"] + c * 128, 128)
                    stg, stok = w32()
                    act(stg[:], ps[:], AF.Silu, [ptok], [stok])
                    dma("sp", RG[c * 128:(c + 1) * 128, tsl], stg[:], [stok], ["RG"])
                for nm, nmp, DST, wtok, scl in (("rq", "rqp", QR, "QR", 1.0), ("rk", "rkp", KR, "KR", 0.125)):
                    for hp in range(2):
                        psa, pta = fm_group(hTt, hTtok, OFF[nm] + hp * 128, 128)
                        psb, ptb = fm_group(hTt, hTtok, OFF[nmp] + hp * 128, 128)
                        t1, k1 = w32()
                        t2, k2 = w32()
                        stt("dve", t1[:], psa[:], scl, cr, ALU.mult, ALU.mult, [pta, crk] + AR, [k1])
                        stt("dve", t2[:], psb[:], scl, sr, ALU.mult, ALU.mult, [ptb, srk] + AR, [k2])
                        stg, stok = wbf()
                        tt("pool", stg[:], t1[:], t2[:], ALU.add, [k1, k2], [stok])
                        store_rows(stg, stok, [(0, 64, DST[2 * hp], wtok), (64, 128, DST[2 * hp + 1], wtok)], tsl)
                        if nm == "rk":
                            psT = PS[5][:].bitcast(BF16)
                            for sub in range(4):
                                tr(psT[:, sub * 128:(sub + 1) * 128], stg[:, sub * 128:(sub + 1) * 128], ident[:],
                                   [stok, "ident"], ["ps5"])
                            for hh in range(2):
                                h = 2 * hp + hh
                                ts("dve", krtstg[:, :, h * 64:(h + 1) * 64],
                                   psT[:, 0:512].rearrange("p (s a d) -> p s a d", s=4, a=2)[:, :, hh, :],
                                   c_tail[:, h:h + 1], ALU.mult, ["ps5", "c_tail"] + AR, ["krtstg"])
                            if hp == 1:
                                dma("sp", KRT[tsl, :, :].rearrange("(s p) h d -> p s (h d)", p=128), krtstg,
                                    ["krtstg"] + AR, ["KRT"])
                if t + 1 < NT:
                    norm_tile(t + 1)
                qsc = 96.0 ** -0.5
                for hp in range(2):
                    b = nxt("psA", 4)
                    for c in range(2):
                        mm(PS[b][:], wqn[:, c, hp * 128:(hp + 1) * 128], cqn[c], c == 0, c == 1,
                           ["wqb", "cqn%d" % c] + AR, ["ps%d" % b])
                    stg, stok = wbf()
                    act(stg[:], PS[b][:], AF.Copy, ["ps%d" % b], [stok], scale=qsc)
                    store_rows(stg, stok, [(0, 64, QM[2 * hp, 0:64], "QM"), (64, 128, QM[2 * hp + 1, 0:64], "QM")], tsl)
                ba = nxt("psA", 4)
                for c in range(2):
                    mm(PS[ba][:], wqr[:, c, :], cqn[c], c == 0, c == 1, ["wqb", "cqn%d" % c] + AR, ["ps%d" % ba])
                bb = nxt("psA", 4)
                for c in range(2):
                    mm(PS[bb][:], wqp[:, c, :], cqn[c], c == 0, c == 1, ["wqb", "cqn%d" % c] + AR, ["ps%d" % bb])
                t1, k1 = w32()
                t2, k2 = w32()
                stt("dve", t1[:], PS[ba][:], qsc, cm, ALU.mult, ALU.mult, ["ps%d" % ba, cmk] + AR, [k1])
                stt("dve", t2[:], PS[bb][:], qsc, sm, ALU.mult, ALU.mult, ["ps%d" % bb, smk] + AR, [k2])
                stg, stok = wbf()
                tt("pool", stg[:], t1[:], t2[:], ALU.add, [k1, k2], [stok])
                store_rows(stg, stok, [(32 * h, 32 * h + 32, QM[h, 64:96], "QM") for h in range(4)], tsl)
                for hp in range(2):
                    b = nxt("psA", 4)
                    mm(PS[b][:], wkn[:, hp * 128:(hp + 1) * 128], ckvn, True, True, ["wkvb", "ckvn"] + AR, ["ps%d" % b])
                    stg, stok = wbf()
                    cp(evq(), stg[:], PS[b][:], ["ps%d" % b], [stok])
                    store_rows(stg, stok, [(0, 64, KM[2 * hp, 0:64], "KM"), (64, 128, KM[2 * hp + 1, 0:64], "KM")], tsl)
                psa, pta = fm_group(hTt, hTtok, OFF["kr"], 32)
                psb, ptb = fm_group(hTt, hTtok, OFF["krp"], 32)
                t1, k1 = w32()
                t2, k2 = w32()
                tt("dve", t1[0:32, :], psa[0:32, :], cm[0:32, :], ALU.mult, [pta, cmk] + AR, [k1])
                tt("dve", t2[0:32, :], psb[0:32, :], sm[0:32, :], ALU.mult, [ptb, smk] + AR, [k2])
                stg, stok = wbf()
                tt("pool", stg[0:32, :], t1[0:32, :], t2[0:32, :], ALU.add, [k1, k2], [stok])
                store_rows(stg, stok, [(0, 32, KM[h, 64:96], "KM") for h in range(4)], tsl)
                for vi, (nm, DSTv) in enumerate((("fv", None), ("rv", None), ("sv", None), ("mla", None))):
                    vs = vstg[vi % 3]
                    vtok = "vstg%d" % (vi % 3)
                    for sub in range(4):
                        b = 6 + (sub % 2)
                        if nm == "mla":
                            mm(PS[b][:, 0:256], ckvn[:, sub * 128:(sub + 1) * 128], wvv, True, True,
                               ["ckvn", "wkvb"] + AR, ["ps%d" % b])
                        else:
                            for kc in range(8):
                                mm(PS[b][:, 0:256], hTt[:, kc, sub * 128:(sub + 1) * 128],
                                   winb[:, kc, OFF[nm]:OFF[nm] + 256], kc == 0, kc == 7,
                                   ["winb%d" % kc, hTtok] + AR, ["ps%d" % b])
                        cp(evq(), vs[:, sub, :], PS[b][:, 0:256], ["ps%d" % b] + AR, [vtok])
                    V_ = {"fv": VF, "mla": VM, "rv": VR, "sv": VS}[nm]
                    dma("sp", V_[tsl, :].rearrange("(s p) n -> p s n", p=128), vs, [vtok] + AR, [nm + "_dst"])
            phase_barrier()

            if dbg == "p1":
                break
            carve.reset()
            fl = carve.take([4, S], F32)
            ones4 = carve.take([4, S], F32)
            cc = carve.take([4, S], F32)
            r1 = carve.take([4, S], F32)
            chi = carve.take([4, S], BF16)
            cmid = carve.take([4, S], BF16)
            clo = carve.take([4, S], BF16)
            cneg = carve.take([128, NB, 4], F32)
            dma("sp", fl, FFL, ["FFL"] + AR, ["fl"])
            sch.add("pool", lambda e: e.memset(ones4, 1.0), AR, ["ones4"])
            act(fl, fl, AF.Identity, ["fl", "bfc"] + AR, ["fl"], bias=bfc[:, 0:1])
            act(fl, fl, AF.Exp, ["fl"] + AR, ["fl"], scale=-1.0)
            act(fl, fl, AF.Ln, ["fl"] + AR, ["fl"], bias=1.0)
            ts("dve", fl, fl, -1.0, ALU.mult, ["fl"] + AR, ["fl"])
            sch.add("dve", lambda e: e.tensor_tensor_scan(out=cc, data0=ones4, data1=fl, initial=0.0,
                                                          op0=ALU.mult, op1=ALU.add), ["fl", "ones4"] + AR, ["cc"])
            cp("dve", chi, cc, ["cc"] + AR, ["chi"])
            tt("dve", r1, cc, chi, ALU.subtract, ["cc", "chi"] + AR, ["r1"])
            cp("dve", cmid, r1, ["r1"] + AR, ["cmid"])
            tt("dve", r1, r1, cmid, ALU.subtract, ["r1", "cmid"] + AR, ["r1"])
            cp("dve", clo, r1, ["r1"] + AR, ["clo"])
            for h in range(4):
                for i, (src, tk) in enumerate(((chi, "chi"), (cmid, "cmid"), (clo, "clo"))):
                    dma("sp", QF[h, 64 + i:65 + i, :], src[h:h + 1, :], [tk] + AR, ["QFc"])
            for tb in range(NB):
                tr(PS[0][:, tb * 4:(tb + 1) * 4], cc[:, tb * 128:(tb + 1) * 128], identf[0:4, 0:4],
                   ["cc", "identf"] + AR, ["ps0"])
            ts("dve", cneg, PS[0][:, 0:NB * 4].rearrange("p (t h) -> p t h", h=4), -1.0, ALU.mult, ["ps0"] + AR, ["cneg"])
            dma("sp", CNEG.rearrange("(t p) h -> p t h", p=128), cneg, ["cneg"] + AR, ["CNEG"])
            phase_barrier()

            def softmax_attention(g, Qd, Kd, Vd, KD, mask_kind, use_bias):
                carve.reset()
                qT = [carve.take([128, S], BF16) for _ in range(2)]
                kT = [carve.take([128, S], BF16) for _ in range(2)]
                vA = [carve.take([128, NB, 128], BF16) for _ in range(2)]
                cng = carve.take([128, NB, 4], F32)
                msk = carve.take([128, 4, 512], BF16)
                dma("sp", msk, cst["c_masks"][:, mask_kind], AR, ["masks"])
                if use_bias:
                    dma("sp", cng, CNEG.rearrange("(t p) h -> p t h", p=128), ["CNEG"] + AR, ["cng"])
                LA = 3

                def load_head(h):
                    i2 = h % 2
                    dma("sp", qT[i2][0:KD, :], Qd[h], ["QF", "QFc", "QM"] + AR, ["qT%d" % i2])
                    dma("sp", kT[i2][0:KD, :], Kd[h], ["KF", "KM"] + ["KF1_%d" % j for j in range(4)] + AR, ["kT%d" % i2])
                    dma("sp", vA[i2][:, :, 0:64], Vd.rearrange("(t p) (h d) -> p t h d", p=128, h=4)[:, :, h, :],
                        ["fv_dst", "mla_dst"] + AR, ["vA%d" % i2])

                for i2_ in range(2):
                    sch.add("pool", (lambda t_: (lambda e: e.memset(t_[:, :, 64:128], 1.0)))(vA[i2_]), AR, ["vAones%d" % i2_])
                load_head(0)
                for h in range(4):
                    i2 = h % 2
                    if h + 1 < 4:
                        load_head(h + 1)
                    blocks = [(T, kb) for T in range(NT) for kb in range(4 * T + 4)]
                    st_ = {}

                    def stage_a(i):
                        T, kb = blocks[i]
                        qsl = slice(T * 512, (T + 1) * 512)
                        xb = nxt("psA", 4)
                        mm(PS[xb][:], kT[i2][0:KD, kb * 128:(kb + 1) * 128], qT[i2][0:KD, qsl], True, True,
                           ["kT%d" % i2, "qT%d" % i2] + AR, ["ps%d" % xb])
                        p_, ptok = wbf()
                        if use_bias:
                            act(p_[:], PS[xb][:], AF.Exp, ["ps%d" % xb, "cng"] + AR, [ptok], bias=cng[:, kb, h:h + 1])
                        else:
                            act(p_[:], PS[xb][:], AF.Exp, ["ps%d" % xb], [ptok])
                        if kb >= 4 * T:
                            tt("pool", p_[:], p_[:], msk[:, kb - 4 * T, :], ALU.mult, [ptok, "masks"] + AR, [ptok])
                        st_[i] = (p_, ptok)

                    def stage_b(i):
                        T, kb = blocks[i]
                        qsl = slice(T * 512, (T + 1) * 512)
                        nkb = 4 * T + 4
                        ob_ = 4 + (T % 2)
                        p_, ptok = st_.pop(i)
                        mm(PS[ob_][:], vA[i2][:, kb, :], p_[:], kb == 0, kb == nkb - 1,
                           ["vA%d" % i2, "vAones%d" % i2, ptok] + AR, ["ps%d" % ob_])
                        if kb == nkb - 1:
                            den, dtok = w32()
                            act(den[0:64, :], PS[ob_][64:128, :], AF.Copy, ["ps%d" % ob_], [dtok])
                            recip(den[0:64, :], den[0:64, :], [dtok], [dtok])
                            o_, otok = w32()
                            tt("dve", o_[0:64, :], PS[ob_][0:64, :], den[0:64, :], ALU.mult, ["ps%d" % ob_, dtok], [otok])
                            dma("sp", OH[g, h, :, qsl], o_[0:64, :], [otok], ["OH%d" % g])

                    n_ = len(blocks)
                    for i in range(n_ + LA):
                        if i < n_:
                            stage_a(i)
                        if i >= LA:
                            stage_b(i - LA)
                phase_barrier()

            softmax_attention(0, QF, KF, VF, 67, 0, True)
            softmax_attention(1, QM, KM, VM, 96, 1, False)

            carve.reset()
            qT = [carve.take([128, S], BF16) for _ in range(4)]
            kT = [carve.take([128, S], BF16) for _ in range(4)]
            nkT = [carve.take([128, S], BF16) for _ in range(4)]
            vS_ = carve.take([128, NB, 256], BF16)
            msk = carve.take([128, 4, 512], BF16)
            dma("sp", msk, cst["c_masks"][:, 2], AR, ["masks"])
            dma("sp", vS_, VS.rearrange("(t p) n -> p t n", p=128), ["sv_dst"] + AR, ["vS"])
            LA = 2
            NHL = 2 * 2 * (LA + 2)
            hl_pool = [carve.take([128, 512], BF16) for _ in range(NHL)]
            w_pool = [carve.take([128, 512], BF16) for _ in range(6)]
            rr["hl"] = 0
            rr["wp"] = 0

            def hl_tile():
                i_ = nxt("hl", NHL)
                return hl_pool[i_], "hl%d" % i_

            def w_tile():
                i_ = nxt("wp", 6)
                return w_pool[i_], "wp%d" % i_
            ACCB = (3, 6)
            OB = (4, 5)

            def load_head_sb(h):
                dma("sp", qT[h][0:64, :], QS[h], ["QS"] + AR, ["qT%d" % h])
                dma("sp", kT[h][0:64, :], KS[h], ["KS"] + AR, ["kT%d" % h])

            for h in range(4):
                e1, e2 = ("pool", "dve") if h % 2 == 0 else ("dve", "pool")
                sch.add(e1, (lambda t_: (lambda e: e.memset(t_[64:128, :], 0.0)))(qT[h]), AR, ["qTpad%d" % h])
                sch.add(e2, (lambda t_: (lambda e: e.memset(t_[64:128, :], 0.0)))(kT[h]), AR, ["kTpad%d" % h])
                sch.add(e1, (lambda t_: (lambda e: e.memset(t_[64:128, :], 0.0)))(nkT[h]), AR, ["nkTpad%d" % h])
                load_head_sb(h)
            for h in range(4):
                ts("dve" if h % 2 == 0 else "pool", nkT[h][0:64, :], kT[h][0:64, :], -1.0, ALU.mult,
                   ["kT%d" % h] + AR, ["nkT%d" % h])
            for hp in range(2):
                heads = (2 * hp, 2 * hp + 1)
                blocks = [(T, kb) for T in range(NT) for kb in range(4 * T + 3, -1, -1)]
                st_ = {}

                XB = (0, 1, 2, 7)

                def stage_z(i, c):
                    h = heads[c]
                    T, kb = blocks[i]
                    qsl = slice(T * 512, (T + 1) * 512)
                    xb = XB[(2 * i + c) % 4]
                    mm(PS[xb][:], kT[h][:, kb * 128:(kb + 1) * 128], qT[h][:, qsl], True, True,
                       ["kT%d" % h, "qT%d" % h, "kTpad%d" % h, "qTpad%d" % h] + AR, ["ps%d" % xb])

                def stage_a(i, c):
                    h = heads[c]
                    T, kb = blocks[i]
                    diag = kb >= 4 * T
                    xb = XB[(2 * i + c) % 4]
                    e_, etok = w32()
                    act(e_[:], PS[xb][:], AF.Exp, ["ps%d" % xb], [etok])
                    act(e_[:], e_[:], AF.Ln, [etok], [etok], bias=1.0)
                    if diag:
                        tt("pool", e_[:], e_[:], msk[:, kb - 4 * T, :], ALU.mult, [etok, "masks"] + AR, [etok])
                    hi, hitok = hl_tile()
                    lo, lotok = hl_tile()
                    cp("dve", hi[:], e_[:], [etok] + AR, [hitok])
                    tt("dve", lo[:], e_[:], hi[:], ALU.subtract, [etok, hitok] + AR, [lotok])
                    st_[(i, c)] = [hi, hitok, lo, lotok]

                def stage_b1(i, c):
                    h = heads[c]
                    T, kb = blocks[i]
                    qsl = slice(T * 512, (T + 1) * 512)
                    nkb = 4 * T + 4
                    diag = kb >= 4 * T
                    first = kb == nkb - 1
                    hi, hitok, lo, lotok = st_[(i, c)]
                    acc = PS[ACCB[c]]
                    atok = "ps%d" % ACCB[c]
                    mm(acc[:], uincl[:], hi[:], first, False, ["uincl", hitok] + AR, [atok], sgc=True)
                    mm(acc[:], uincl[:], lo[:], False, False, ["uincl", lotok] + AR, [atok], sgc=True)
                    mm(acc[:], kT[h][:, kb * 128:(kb + 1) * 128], qT[h][:, qsl], False, True,
                       ["kT%d" % h, "qT%d" % h, "kTpad%d" % h, "qTpad%d" % h] + AR, [atok], sgc=True)
                    w_, wtok = w_tile()
                    act(w_[:], acc[:], AF.Exp, [atok] + AR, [wtok])
                    if diag:
                        tt("pool", w_[:], w_[:], msk[:, kb - 4 * T, :], ALU.mult, [wtok, "masks"] + AR, [wtok])
                    st_[(i, c)] += [w_, wtok]

                def stage_b2(i, c):
                    h = heads[c]
                    T, kb = blocks[i]
                    qsl = slice(T * 512, (T + 1) * 512)
                    nkb = 4 * T + 4
                    first = kb == nkb - 1
                    last = kb == 0
                    hi, hitok, lo, lotok, w_, wtok = st_.pop((i, c))
                    acc = PS[ACCB[c]]
                    atok = "ps%d" % ACCB[c]
                    if not last:
                        mm(acc[:], nkT[h][:, kb * 128:(kb + 1) * 128], qT[h][:, qsl], False, False,
                           ["nkT%d" % h, "nkTpad%d" % h, "qT%d" % h, "qTpad%d" % h] + AR, [atok], sgc=True)
                        mm(acc[:], nlower[:], hi[:], False, False, ["nlower", hitok] + AR, [atok], sgc=True)
                        mm(acc[:], nlower[:], lo[:], False, True, ["nlower", lotok] + AR, [atok], sgc=True)
                    ob_ = OB[c]
                    mm(PS[ob_][:], vS_[:, kb, hp * 128:(hp + 1) * 128], w_[:], first, last,
                       ["vS", wtok] + AR, ["ps%d" % ob_])
                    if last:
                        o_, otok = w32()
                        rs_ = slice(c * 64, c * 64 + 64)
                        cp("act", o_[rs_, :], PS[ob_][rs_, :], ["ps%d" % ob_], [otok])
                        dma("sp", OH[3, h, :, qsl], o_[rs_, :], [otok], ["OH3"])

                n_ = len(blocks)
                stage_z(0, 0)
                stage_z(0, 1)
                for i in range(n_ + LA):
                    if i >= LA:
                        stage_b1(i - LA, 0)
                        stage_b1(i - LA, 1)
                    if i + 1 < n_:
                        stage_z(i + 1, 0)
                        stage_z(i + 1, 1)
                    if i < n_:
                        stage_a(i, 0)
                        stage_a(i, 1)
                    if i >= LA:
                        stage_b2(i - LA, 0)
                        stage_b2(i - LA, 1)
            phase_barrier()

            carve.reset()
            qR = [carve.take([128, S], BF16) for _ in range(2)]
            kR = [carve.take([128, S], BF16) for _ in range(2)]
            kRT = carve.take([128, NB, 256], BF16)
            vR = carve.take([128, NB, 256], BF16)
            rdec = carve.take([128, 4, 512], F32)
            qdec = carve.take([128, 4, 512], F32)
            KVs = carve.take([128, 2, NCH, 64], F32)
            SPb = carve.take([128, 2, NCH, 64], BF16)
            dma("sp", kRT, KRT.rearrange("(t p) h d -> p t (h d)", p=128), ["KRT"] + AR, ["kRT"])
            dma("sp", vR, VR.rearrange("(t p) n -> p t n", p=128), ["rv_dst"] + AR, ["vR"])
            dma("sp", rdec, cst["c_rdec"], AR, ["rdec"])
            dma("sp", qdec, cst["c_qdec"], AR, ["qdec"])
            for tb in range(NB):
                for j in range(2):
                    ch = 2 * tb + j
                    rs = slice(j * 64, (j + 1) * 64)
                    for hp in range(2):
                        b = nxt("psA", 4)
                        mm(PS[b][:, 0:128], kRT[rs, tb, hp * 128:(hp + 1) * 128], vR[rs, tb, hp * 128:(hp + 1) * 128],
                           True, True, ["kRT", "vR"] + AR, ["ps%d" % b])
                        cp("dve", KVs[0:64, hp, ch, :], PS[b][0:64, 0:64], ["ps%d" % b] + AR, ["KVs"])
                        cp("act", KVs[64:128, hp, ch, :], PS[b][64:128, 64:128], ["ps%d" % b] + AR, ["KVs"])
            for ch in range(1, NCH):
                for hp in range(2):
                    stt("dve", KVs[:, hp, ch, :], KVs[:, hp, ch - 1, :], c_adec[:, hp:hp + 1], KVs[:, hp, ch, :],
                        ALU.mult, ALU.add, ["KVs", "c_adec"] + AR, ["KVs"])
            cp("dve", SPb, KVs, ["KVs"] + AR, ["SPb"])
            for h in range(4):
                i2 = h % 2
                hp = h // 2
                prs = slice(i2 * 64, i2 * 64 + 64)
                dma("sp", qR[i2][prs, :], QR[h], ["QR"] + AR, ["qR%d" % i2])
                dma("sp", kR[i2][prs, :], KR[h], ["KR"] + AR, ["kR%d" % i2])
                for T in range(NT):
                    qsl = slice(T * 512, (T + 1) * 512)
                    xb = nxt("psA", 4)
                    for s4 in range(4):
                        csl = slice(T * 512 + s4 * 128, T * 512 + (s4 + 1) * 128)
                        mm(PS[xb][:, s4 * 128:(s4 + 1) * 128], kR[i2][prs, csl], qR[i2][prs, csl], s4 == 0, s4 == 3,
                           ["kR%d" % i2, "qR%d" % i2] + AR, ["ps%d" % xb])
                    sm_, smtok = wbf()
                    tt("dve", sm_[:], PS[xb][:], rdec[:, h, :], ALU.mult, ["ps%d" % xb, "rdec"] + AR, [smtok])
                    ob_ = 4 + (T % 2)
                    for s4 in range(4):
                        tb = T * 4 + s4
                        mm(PS[ob_][0:64, s4 * 128:(s4 + 1) * 128], vR[:, tb, h * 64:(h + 1) * 64],
                           sm_[:, s4 * 128:(s4 + 1) * 128], s4 == 0, False, ["vR", smtok] + AR, ["ps%d" % ob_])
                    qd_, qdtok = wbf()
                    tt("pool", qd_[prs, :], qR[i2][prs, qsl], qdec[prs, h, :], ALU.mult,
                       ["qR%d" % i2, "qdec"] + AR, [qdtok])
                    for c8 in range(8):
                        ch = T * 8 + c8
                        if ch == 0:
                            continue
                        mm(PS[ob_][0:64, c8 * 64:(c8 + 1) * 64], SPb[prs, hp, ch - 1, :], qd_[prs, c8 * 64:(c8 + 1) * 64],
                           False, c8 == 7, ["SPb", qdtok] + AR, ["ps%d" % ob_])
                    o_, otok = w32()
                    cp("act", o_[0:64, :], PS[ob_][0:64, :], ["ps%d" % ob_], [otok])
                    dma("sp", OH[2, h, :, qsl], o_[0:64, :], [otok], ["OH2"])
            phase_barrier()

            carve.reset()
            oh = [carve.take([64, 512], F32) for _ in range(8)]
            sqt = [carve.take([64, 512], F32) for _ in range(8)]
            rst = [carve.take([64, 512], F32) for _ in range(2)]
            ohr = [carve.take([64, 512], F32) for _ in range(3)]
            rgt = [carve.take([64, 512], F32) for _ in range(3)]
            dt_ = [carve.take([64, 512], F32) for _ in range(3)]
            st2 = [carve.take([64, 512], F32) for _ in range(3)]
            mo = [carve.take([64, 512], BF16) for _ in range(8)]
            rr["mo"] = 0
            units = []
            for T in range(NT):
                units += [(T, "rms", g) for g in (0, 1, 3)] + [(T, "ret", h) for h in range(4)]
            krms = [0]
            kret = [0]
            ust = {}

            def gn_a(k):
                T, kind, idx = units[k]
                qsl = slice(T * 512, (T + 1) * 512)
                if kind == "rms":
                    g = idx
                    p_ = krms[0] % 2
                    krms[0] += 1
                    bank = (0, 3)[p_]
                    for h in range(4):
                        o_ = oh[4 * p_ + h]
                        otok = "oh%d" % (4 * p_ + h)
                        dma("sp", o_, OH[g, h, :, qsl], ["OH%d" % g] + AR, [otok])
                        act(sqt[4 * p_ + h], o_, AF.Square, [otok] + AR, ["sqt%d" % (4 * p_ + h)])
                        mm(PS[bank][0:64, :], onesf[0:64, 0:64], sqt[4 * p_ + h], h == 0, h == 3,
                           ["onesf", "sqt%d" % (4 * p_ + h)] + AR, ["ps%d" % bank])
                    ust[k] = (p_, bank)
                else:
                    h = idx
                    p_ = kret[0] % 3
                    q_ = kret[0] % 2
                    kret[0] += 1
                    b1, b2 = ((1, 2), (4, 5))[q_]
                    o_ = ohr[p_]
                    otok = "ohr%d" % p_
                    dma("sp", o_, OH[2, h, :, qsl], ["OH2"] + AR, [otok])
                    dma("sp", rgt[p_], RG[h * 64:(h + 1) * 64, qsl], ["RG"] + AR, ["rgt%d" % p_])
                    mm(PS[b1][0:64, :], ones64[0:64, :], o_, True, True, ["ones64", otok] + AR, ["ps%d" % b1])
                    tt("dve", dt_[p_], o_, PS[b1][0:64, :], ALU.subtract, [otok, "ps%d" % b1] + AR, ["dt%d" % p_])
                    act(st2[p_], dt_[p_], AF.Square, ["dt%d" % p_] + AR, ["st2_%d" % p_])
                    mm(PS[b2][0:64, :], ones64[0:64, :], st2[p_], True, True, ["ones64", "st2_%d" % p_] + AR, ["ps%d" % b2])
                    ust[k] = (p_, b2)

            def gn_b(k):
                T, kind, idx = units[k]
                qsl = slice(T * 512, (T + 1) * 512)
                p_, bank = ust.pop(k)
                if kind == "rms":
                    g = idx
                    rs_ = rst[p_]
                    rtok = "rst%d" % p_
                    act(rs_, PS[bank][0:64, :], AF.Ln, ["ps%d" % bank] + AR, [rtok], bias=EPS, scale=1.0 / 256.0)
                    act(rs_, rs_, AF.Exp, [rtok] + AR, [rtok], scale=-0.5)
                    for h in range(4):
                        o_ = oh[4 * p_ + h]
                        otok = "oh%d" % (4 * p_ + h)
                        mi = nxt("mo", 8)
                        stt("dve", mo[mi], o_, gmoc[0:64, 4 * g + h:4 * g + h + 1], rs_, ALU.mult, ALU.mult,
                            [otok, "gmoc", rtok] + AR, ["mo%d" % mi])
                        dma("pool", MIXT[g * 256 + h * 64:g * 256 + (h + 1) * 64, qsl], mo[mi], ["mo%d" % mi] + AR, ["MIXT"])
                else:
                    h = idx
                    s_ = st2[p_]
                    stok = "st2_%d" % p_
                    act(s_, PS[bank][0:64, :], AF.Ln, ["ps%d" % bank] + AR, [stok], bias=EPS)
                    act(s_, s_, AF.Exp, [stok] + AR, [stok], scale=-0.5)
                    stt("dve", dt_[p_], dt_[p_], gmoc[0:64, 8 + h:9 + h], s_, ALU.mult, ALU.mult,
                        ["dt%d" % p_, "gmoc", stok] + AR, ["dt%d" % p_])
                    mi = nxt("mo", 8)
                    tt("dve", mo[mi], dt_[p_], rgt[p_], ALU.mult, ["dt%d" % p_, "rgt%d" % p_] + AR, ["mo%d" % mi])
                    dma("pool", MIXT[512 + h * 64:512 + (h + 1) * 64, qsl], mo[mi], ["mo%d" % mi] + AR, ["MIXT"])

            for k in range(len(units) + 1):
                if k < len(units):
                    gn_a(k)
                if k >= 1:
                    gn_b(k - 1)
            phase_barrier()

            carve.reset()
            woutb = carve.take([128, 8, D], BF16)
            mixt = [carve.take([128, 8, 512], BF16) for _ in range(2)]
            for kc in range(8):
                dma("pool", woutb[:, kc, :], w_out[L, kc * 128:(kc + 1) * 128, :], AR, ["woutb%d" % kc])
            dma("sp", gb[1][:], g_post[L:L + 1, :].partition_broadcast(128), (), ["gb1"])
            for T in range(NT):
                mt = mixt[T % 2]
                mtok_ = "mixt%d" % (T % 2)
                dma("sp", mt, MIXT.rearrange("(k p) s -> p k s", p=128)[:, :, T * 512:(T + 1) * 512], ["MIXT"] + AR, [mtok_])
                for sub in range(4):
                    r0 = T * 512 + sub * 128
                    xi = sub % 2
                    dma("sp", xt[sub][:], x_src[r0:r0 + 128, :], ["X2"], ["xt%d" % sub])
                    for half in range(2):
                        b = 2 * xi + half
                        for kc in range(8):
                            mm(PS[b][:], mt[:, kc, sub * 128:(sub + 1) * 128], woutb[:, kc, half * 512:(half + 1) * 512],
                               kc == 0, kc == 7, [mtok_, "woutb%d" % kc] + AR, ["ps%d" % b])
                        act(yt[xi][:, half * 512:(half + 1) * 512], PS[b][:], AF.Square, ["ps%d" % b],
                            ["yt%d" % xi, "smc%d" % (8 + 2 * xi + half)], accum=small[:, 8 + 2 * xi + half:9 + 2 * xi + half])
                    col = small[:, 8 + 2 * xi:9 + 2 * xi]
                    ctok = "smc%d" % (8 + 2 * xi)
                    tt("dve", col, col, small[:, 9 + 2 * xi:10 + 2 * xi], ALU.add, [ctok, "smc%d" % (9 + 2 * xi)], [ctok])
                    rstd_col(col, ctok, 1.0 / D)
                    for half in range(2):
                        b = 2 * xi + half
                        hs = slice(half * 512, (half + 1) * 512)
                        stt("dve", yt[xi][:, hs], PS[b][:], col, gb[1][:, hs], ALU.mult, ALU.mult,
                            ["ps%d" % b, ctok, "gb1"], ["yt%d" % xi])
                    tt("dve", yt[xi][:], yt[xi][:], xt[sub][:], ALU.add, ["yt%d" % xi, "xt%d" % sub], ["yt%d" % xi])
                    dma("pool", X1[r0:r0 + 128, :], yt[xi][:], ["yt%d" % xi], ["X1"])
            phase_barrier()

            carve.reset()
            wupb = carve.take([128, 8, DFF], BF16)
            wdnb = carve.take([128, 32, D], BF16)
            TOK = 256
            uT = carve.take([128, 32, TOK], BF16)
            h2T = carve.take([128, 8, TOK], BF16)
            for kc in range(8):
                dma("pool", wupb[:, kc, :], w_up[L, kc * 128:(kc + 1) * 128, :], AR, ["wupb%d" % kc])
            for f4 in range(8):
                dma("pool", wdnb[:, 4 * f4:4 * f4 + 4, :],
                    w_dn[L, f4 * 512:(f4 + 1) * 512, :].rearrange("(f p) n -> p f n", p=128), AR, ["wdnb%d" % f4])
            dma("sp", gb[0][:], g_fpre[L:L + 1, :].partition_broadcast(128), (), ["gb0"])
            dma("sp", gb[1][:], g_fpost[L:L + 1, :].partition_broadcast(128), (), ["gb1"])
            NTT = S // TOK

            def f_load_x(T):
                for sub in range(2):
                    r0 = T * TOK + sub * 128
                    xq = (T % 2) * 2 + sub
                    dma("sp", xt[xq][:], X1[r0:r0 + 128, :], ["X1"], ["xt%d" % xq])

            def f_norm_pre(T, sub):
                xq = (T % 2) * 2 + sub
                col = small[:, 16 + sub:17 + sub]
                ctok = "smd%d" % sub
                act(hbf[sub][:], xt[xq][:], AF.Square, ["xt%d" % xq], ["hbf%d" % sub, ctok], accum=col)
                rstd_col(col, ctok, 1.0 / D)
                stt("dve", hbf[sub][:], xt[xq][:], col, gb[0][:], ALU.mult, ALU.mult,
                    ["xt%d" % xq, ctok, "gb0"], ["hbf%d" % sub])

            def f_tr(T, sub):
                psT = PS[sub][:].bitcast(BF16)
                for kc in range(8):
                    tr(psT[:, kc * 128:(kc + 1) * 128], hbf[sub][:, kc * 128:(kc + 1) * 128], ident[:],
                       ["hbf%d" % sub, "ident"], ["ps%d" % sub])
                cp("act" if sub else "dve", h2T[:, :, sub * 128:(sub + 1) * 128],
                   psT.rearrange("p (k c) -> p k c", k=8), ["ps%d" % sub] + AR, ["h2T"])

            def f_up(T):
                for fc in range(32):
                    b = fc % 4
                    for kc in range(8):
                        mm(PS[b][:, 0:TOK], wupb[:, kc, fc * 128:(fc + 1) * 128], h2T[:, kc, :], kc == 0, kc == 7,
                           ["wupb%d" % kc, "h2T"] + AR, ["ps%d" % b])
                    r_, rtok = w32()
                    act(r_[:, 0:TOK], PS[b][:, 0:TOK], AF.Relu, ["ps%d" % b], [rtok])
                    tt("pool" if fc % 2 else "dve", uT[:, fc, :], r_[:, 0:TOK], r_[:, 0:TOK], ALU.mult, [rtok] + AR, ["uT"])

            def f_down(T, sub):
                for half in range(2):
                    b = 4 + 2 * sub + half
                    for fc in range(32):
                        mm(PS[b][:], uT[:, fc, sub * 128:(sub + 1) * 128], wdnb[:, fc, half * 512:(half + 1) * 512],
                           fc == 0, fc == 31, ["uT", "wdnb%d" % (fc // 4)] + AR, ["ps%d" % b])

            def f_post(T, sub):
                r0 = T * TOK + sub * 128
                xq = (T % 2) * 2 + sub
                for half in range(2):
                    b = 4 + 2 * sub + half
                    act(yt[sub][:, half * 512:(half + 1) * 512], PS[b][:], AF.Square, ["ps%d" % b],
                        ["yt%d" % sub, "sme%d" % (2 * sub + half)],
                        accum=small[:, 24 + 2 * sub + half:25 + 2 * sub + half])
                col = small[:, 24 + 2 * sub:25 + 2 * sub]
                ctok = "sme%d" % (2 * sub)
                tt("dve", col, col, small[:, 25 + 2 * sub:26 + 2 * sub], ALU.add, [ctok, "sme%d" % (2 * sub + 1)], [ctok])
                rstd_col(col, ctok, 1.0 / D)
                for half in range(2):
                    b = 4 + 2 * sub + half
                    hs = slice(half * 512, (half + 1) * 512)
                    stt("dve", yt[sub][:, hs], PS[b][:], col, gb[1][:, hs], ALU.mult, ALU.mult,
                        ["ps%d" % b, ctok, "gb1"], ["yt%d" % sub])
                tt("pool", yt[sub][:], yt[sub][:], xt[xq][:], ALU.add, ["yt%d" % sub, "xt%d" % xq], ["yt%d" % sub])
                dma("sp", x_dst[r0:r0 + 128, :], yt[sub][:], ["yt%d" % sub], ["X2" if x_dst is X2 else "Y"])

            f_load_x(0)
            for sub in range(2):
                f_norm_pre(0, sub)
                f_tr(0, sub)
            for T in range(NTT):
                nxt_ = T + 1 < NTT
                if nxt_:
                    f_load_x(T + 1)
                f_up(T)
                f_down(T, 0)
                f_post(T, 0)
                if nxt_:
                    f_norm_pre(T + 1, 0)
                f_down(T, 1)
                if nxt_:
                    f_norm_pre(T + 1, 1)
                    f_tr(T + 1, 0)
                    f_tr(T + 1, 1)
                f_post(T, 1)
            phase_barrier()

        sch.add("sp", lambda e: e.nop(), (), list(set(sch.lastw.keys()) | set(sch.readers.keys())), force=True)
        sch.emit(nc, st)
    return nc, sch


_CACHE = {}


def make_in_maps(inputs, S, n_cores):
    consts = host_consts()
    maps = []
    f = lambda a: np.ascontiguousarray(np.asarray(a, dtype=np.float32))
    gq = f(inputs["g_q_lora"]).reshape(2, 2, 128).transpose(0, 2, 1)
    gkv = f(inputs["g_kv_lora"]).reshape(2, 128, 1)
    gmo = f(inputs["g_mix_out"]).reshape(2, 16, 64).transpose(0, 2, 1)
    gmo = np.concatenate([gmo, gmo], axis=1)
    bfc = f(inputs["b_forget"]).reshape(2, 4, 1)
    shared = dict(w_in=f(inputs["w_in"]), w_q_up=f(inputs["w_q_up"]), w_kv_up=f(inputs["w_kv_up"]),
                  w_out=f(inputs["w_out"]), w_ffn_up=f(inputs["w_ffn_up"]), w_ffn_down=f(inputs["w_ffn_down"]),
                  g_mix_pre=f(inputs["g_mix_pre"]), g_mix_post=f(inputs["g_mix_post"]),
                  g_ffn_pre=f(inputs["g_ffn_pre"]), g_ffn_post=f(inputs["g_ffn_post"]),
                  gq_col=np.ascontiguousarray(gq), gkv_col=np.ascontiguousarray(gkv),
                  gmo_col=np.ascontiguousarray(gmo), bf_col=np.ascontiguousarray(bfc))
    shared.update(consts)
    xs = f(inputs["x"])
    ps = np.asarray(inputs["positions"]).astype(np.int32)
    for c in range(n_cores):
        m = dict(shared)
        m["x"] = np.ascontiguousarray(xs[c])
        m["pos"] = np.ascontiguousarray(ps[c:c + 1])
        maps.append(m)
    return maps


def kernel(**inputs):
    x = np.asarray(inputs["x"])
    B, S, _ = x.shape
    if S not in _CACHE:
        _CACHE[S] = build(S)[0]
    nc = _CACHE[S]
    maps = make_in_maps(inputs, S, B)
    res = run_bass_kernel_spmd(nc, maps, core_ids=list(range(B)))
    return np.stack([np.asarray(r["y"], dtype=np.float32) for r in res.results], axis=0)
```

```python
import math
import numpy as np
import ml_dtypes
from contextlib import ExitStack
import concourse.bass as bass
import concourse.mybir as mybir
from concourse.bass_utils import run_bass_kernel_spmd

F32 = mybir.dt.float32
BF16 = mybir.dt.bfloat16
I32 = mybir.dt.int32
AF = mybir.ActivationFunctionType
ALU = mybir.AluOpType

COMPUTE = ("pe", "act", "dve", "pool", "sp")
N_DMA_SEMS = 48
SAME_ENGINE_SYNC = True

D = 1024
DFF = 4096
NIN = 2980
EPS = 1e-6
OFF = dict(fq=0, fk=256, fv=512, ff=768, cq=772, ckv=1028, kr=1156, rq=1188, rk=1444, rv=1700,
           rg=1956, sq=2212, sk=2468, sv=2724, rqp=2980, rkp=3236, krp=3492)
NINP = 3524


class _Op:
    __slots__ = ("fn", "deps", "raw", "ndma", "signal", "sem", "val", "clock", "queue")

    def __init__(self, fn, deps, ndma, queue, raw=()):
        self.fn = fn
        self.deps = deps
        self.raw = raw
        self.ndma = ndma
        self.signal = False
        self.sem = None
        self.val = 0
        self.clock = None
        self.queue = queue


class Sched:
    def __init__(self):
        self.ops = []
        self.lastw = {}
        self.readers = {}

    def add(self, eng, fn, reads=(), writes=(), ndma=0, force=False):
        import os
        mx = int(os.environ.get("MAX_OPS", "0"))
        if mx and len(self.ops) >= mx and not force:
            return -1
        i = len(self.ops)
        deps = set()
        if any(isinstance(t, str) and t.startswith("ps") for t in reads):
            writes = list(writes) + [t for t in reads if isinstance(t, str) and t.startswith("ps") and t not in writes]
            reads = [t for t in reads if not (isinstance(t, str) and t.startswith("ps"))]
        raw = set()
        for t in reads:
            w = self.lastw.get(t)
            if w is not None:
                deps.add(w)
                raw.add(w)
        for t in writes:
            w = self.lastw.get(t)
            if w is not None:
                deps.add(w)
            r = self.readers.get(t)
            if r:
                deps.update(r)
        for t in reads:
            self.readers.setdefault(t, []).append(i)
        for t in writes:
            self.lastw[t] = i
            self.readers[t] = []
        self.ops.append(_Op(fn, deps, ndma, eng, raw))
        return i

    def emit(self, nc, stack):
        ops = self.ops
        queues = {}
        for i, op in enumerate(ops):
            queues.setdefault(op.queue, []).append(i)
        esem = {q: stack.enter_context(nc.semaphore("s_" + q)) for q in COMPUTE}
        dsems = [stack.enter_context(nc.semaphore("d_%d" % k)) for k in range(N_DMA_SEMS)]
        dcount = [0] * N_DMA_SEMS
        dlast = [None] * N_DMA_SEMS
        N_SW = 8
        kk = {"pool": 0, "sp": 0}
        for i, op in enumerate(ops):
            if op.ndma:
                if op.queue == "pool":
                    k = kk["pool"] % N_SW
                    kk["pool"] += 1
                else:
                    k = N_SW + kk["sp"] % (N_DMA_SEMS - N_SW)
                    kk["sp"] += 1
                op.sem = ("d", k)
                if dlast[k] is not None:
                    op.deps.add(dlast[k])
                dlast[k] = i
                dcount[k] += 16 * op.ndma
                op.val = dcount[k]
                op.signal = True

        def skip(dop, op):
            if dop.ndma or op.ndma or dop.queue != op.queue:
                return False
            return dop.queue == "pe" or not SAME_ENGINE_SYNC

        opidx = {id(o): i for i, o in enumerate(ops)}

        for op in ops:
            for d in op.deps:
                dop = ops[d]
                if dop.ndma or skip(dop, op):
                    continue
                dop.signal = True
        cnt = {q: 0 for q in COMPUTE}
        for op in ops:
            if not op.ndma:
                if op.signal:
                    cnt[op.queue] += 1
                op.sem = ("e", op.queue)
                op.val = cnt[op.queue]
        kn = {q: {} for q in queues}
        for op in ops:
            kq = kn[op.queue]
            for d in op.deps:
                dop = ops[d]
                if kq.get(dop.sem, 0) < dop.val:
                    kq[dop.sem] = dop.val
                if dop.clock:
                    for s, v in dop.clock.items():
                        if kq.get(s, 0) < v:
                            kq[s] = v
            if op.signal:
                c = dict(kq)
                c[op.sem] = op.val
                op.clock = c

        def semobj(s):
            return esem[s[1]] if s[0] == "e" else dsems[s[1]]

        block = stack.enter_context(nc.Block())
        self.nwaits = 0

        def run_queue(q, eng):
            known = {}
            for i in queues[q]:
                op = ops[i]
                need = {}
                for d in op.deps:
                    dop = ops[d]
                    if skip(dop, op):
                        continue
                    if known.get(dop.sem, 0) >= dop.val:
                        continue
                    if need.get(dop.sem, 0) < dop.val:
                        need[dop.sem] = dop.val
                for d in op.deps:
                    dop = ops[d]
                    if skip(dop, op):
                        continue
                    if dop.clock:
                        for s, v in dop.clock.items():
                            if known.get(s, 0) < v:
                                known[s] = v
                for s, v in need.items():
                    eng.wait_ge(semobj(s), v)
                    self.nwaits += 1
                    if known.get(s, 0) < v:
                        known[s] = v
                ins = op.fn(eng)
                if op.ndma:
                    lst = ins if isinstance(ins, (list, tuple)) else [ins]
                    assert len(lst) == op.ndma
                    for x_ in lst:
                        x_.then_inc(semobj(op.sem), 16)
                elif op.signal:
                    ins.then_inc(semobj(op.sem), 1)

        def mk(q):
            return lambda eng: run_queue(q, eng)

        handlers = {"pe": block.tensor, "act": block.scalar, "dve": block.vector,
                    "pool": block.gpsimd, "sp": block.sync}
        for q in queues:
            handlers[q](mk(q))


def host_consts():
    c = {}
    bf = ml_dtypes.bfloat16
    c["c_ident"] = np.eye(128, dtype=np.float32).astype(bf)
    kk = np.arange(128)[:, None, None]
    jj = np.arange(4)[None, :, None]
    qq = np.arange(512)[None, None, :]
    key = 128 * jj + kk
    m = np.zeros((128, 3, 4, 512), np.float32)
    m[:, 0] = (key <= qq)
    m[:, 1] = ((key // 64) <= (qq // 64))
    m[:, 2] = (key < qq)
    c["c_masks"] = m.astype(bf)
    j = np.arange(128)[:, None]
    s = np.arange(128)[None, :]
    c["c_uincl"] = (-(j >= s).astype(np.float32)).astype(bf)
    c["c_nlower"] = (-(j < s).astype(np.float32)).astype(bf)
    h = np.arange(4, dtype=np.float32)
    log_gamma = np.log1p(-np.power(np.float32(2.0), np.float32(-5.0) - h)).astype(np.float32)
    idx = np.arange(64, dtype=np.float32)
    m_ = np.arange(128)
    dec = np.zeros((128, 4, 128), np.float32)
    for hh in range(4):
        dd = np.exp(log_gamma[hh] * np.abs(m_[:, None] - m_[None, :]).astype(np.float32)).astype(np.float32)
        same = (m_[:, None] // 64) == (m_[None, :] // 64)
        dec[:, hh, :] = np.where(same, dd, 0.0)
    c["c_rdec"] = np.tile(dec[:, :, None, :], (1, 1, 4, 1)).reshape(128, 4, 512).astype(np.float32)
    tail = np.exp(log_gamma[None, :] * (63.0 - idx)[:, None]).astype(np.float32)
    c["c_tail"] = np.tile(tail, (2, 1)).astype(np.float32)
    qh = np.exp(log_gamma[None, :] * (idx + 1.0)[:, None]).astype(np.float32)
    qd = np.tile(qh.T[None, :, None, :], (128, 1, 8, 1)).reshape(128, 4, 512)
    c["c_qdec"] = qd.astype(np.float32)
    a_ = np.exp(log_gamma * np.float32(64.0)).astype(np.float32)
    ad = np.zeros((128, 2), np.float32)
    for hp_ in range(2):
        ad[0:64, hp_] = a_[2 * hp_]
        ad[64:128, hp_] = a_[2 * hp_ + 1]
    c["c_adec"] = ad
    r = np.arange(128)
    invf_r = (np.float32(10000.0) ** (-(r % 32).astype(np.float32) / np.float32(32))).astype(np.float32)
    invf_m = (np.float32(10000.0) ** (-(r % 16).astype(np.float32) / np.float32(16))).astype(np.float32)
    sg_r = np.where((r % 64) < 32, -1.0, 1.0).astype(np.float32)
    sg_m = np.where((r % 32) < 16, -1.0, 1.0).astype(np.float32)
    c["c_rope"] = np.stack([invf_r, invf_m, sg_r, sg_m], axis=1).astype(np.float32)
    return c


CONST_SHAPES = dict(c_ident=([128, 128], BF16), c_masks=([128, 3, 4, 512], BF16), c_uincl=([128, 128], BF16), c_nlower=([128, 128], BF16),
                    c_rdec=([128, 4, 512], F32), c_tail=([128, 4], F32), c_qdec=([128, 4, 512], F32),
                    c_adec=([128, 2], F32), c_rope=([128, 4], F32))


def build(S, dbg=False, nlayers=2):
    NT = S // 512
    NB = S // 128
    NCH = S // 64
    nc = bass.Bass("TRN2", target_bir_lowering=False)
    sch = Sched()

    def din(name, shape, dt=F32):
        return nc.dram_tensor(name, shape, dt, kind="ExternalInput").ap()

    def dscr(name, shape, dt=F32):
        return nc.dram_tensor(name, shape, dt, kind=("ExternalOutput" if dbg else "Internal")).ap()

    x_in = din("x", [S, D])
    pos = din("pos", [1, S], I32)
    w_in = din("w_in", [2, D, NIN])
    w_q_up = din("w_q_up", [2, 256, 384])
    w_kv_up = din("w_kv_up", [2, 128, 512])
    w_out = din("w_out", [2, D, D])
    w_up = din("w_ffn_up", [2, D, DFF])
    w_dn = din("w_ffn_down", [2, DFF, D])
    g_pre = din("g_mix_pre", [2, D])
    g_post = din("g_mix_post", [2, D])
    g_fpre = din("g_ffn_pre", [2, D])
    g_fpost = din("g_ffn_post", [2, D])
    gq_col = din("gq_col", [2, 128, 2])
    gkv_col = din("gkv_col", [2, 128, 1])
    gmo_col = din("gmo_col", [2, 128, 8])
    bf_col = din("bf_col", [2, 4, 1])
    cst = {k: din(k, sh, dt) for k, (sh, dt) in CONST_SHAPES.items()}
    y_out = nc.dram_tensor("y", [S, D], F32, kind="ExternalOutput").ap()

    COSR = dscr("COSR", [128, S]); SINR = dscr("SINR", [128, S])
    COSM = dscr("COSM", [128, S]); SINM = dscr("SINM", [128, S])
    QF = dscr("QF", [4, 67, S], BF16); KF = dscr("KF", [4, 67, S], BF16)
    VF = dscr("VF", [S, 256], BF16); FFL = dscr("FFL", [4, S])
    CNEG = dscr("CNEG", [S, 4])
    QM = dscr("QM", [4, 96, S], BF16); KM = dscr("KM", [4, 96, S], BF16); VM = dscr("VM", [S, 256], BF16)
    QR = dscr("QR", [4, 64, S], BF16); KR = dscr("KR", [4, 64, S], BF16)
    KRT = dscr("KRT", [S, 4, 64], BF16); VR = dscr("VR", [S, 256], BF16); RG = dscr("RG", [256, S])
    QS = dscr("QS", [4, 64, S], BF16); KS = dscr("KS", [4, 64, S], BF16); VS = dscr("VS", [S, 256], BF16)
    OH = dscr("OH", [4, 4, 64, S])
    MIXT = dscr("MIXT", [D, S], BF16)
    X1 = dscr("X1", [S, D])
    X2 = dscr("X2", [S, D])

    st = ExitStack()
    with st:
        def sb(name, shape, dt):
            return st.enter_context(nc.sbuf_tensor("sb_" + name, shape, dt))

        ident = sb("ident", [128, 128], BF16)
        identf = sb("identf", [128, 128], F32)
        uincl = sb("uincl", [128, 128], BF16)
        nlower = sb("nlower", [128, 128], BF16)
        onesb = sb("onesb", [128, 128], BF16)
        onesf = sb("onesf", [128, 128], F32)
        ones64 = sb("ones64", [128, 64], F32)
        c_tail = sb("c_tail", [128, 4], F32)
        c_adec = sb("c_adec", [128, 2], F32)
        c_rope = sb("c_rope", [128, 4], F32)
        gqc = sb("gqc", [128, 2], F32)
        gkvc = sb("gkvc", [128, 1], F32)
        gmoc = sb("gmoc", [128, 8], F32)
        bfc = sb("bfc", [4, 1], F32)
        small = sb("small", [128, 64], F32)
        ARENA = 150 * 1024
        arena = sb("arena", [128, ARENA], mybir.dt.uint8)
        NW = 5
        wk32 = [sb("wk32_%d" % i, [128, 512], F32) for i in range(NW)]
        NWB = 8
        wkbf = [sb("wkbf_%d" % i, [128, 512], BF16) for i in range(NWB)]
        xt = [sb("xt_%d" % i, [128, 1024], F32) for i in range(4)]
        yt = [sb("yt_%d" % i, [128, 1024], F32) for i in range(2)]
        hbf = [sb("hbf_%d" % i, [128, 1024], BF16) for i in range(2)]
        gb = [sb("gb_%d" % i, [128, 1024], F32) for i in range(2)]
        PS = [st.enter_context(nc.psum_tensor("ps%d" % i, [128, 512], F32)) for i in range(8)]

        class Carver:
            def __init__(self):
                self.off = 0

            def reset(self):
                self.off = 0

            def take(self, shape, dt):
                esz = {F32: 4, BF16: 2, I32: 4}[dt]
                n = 1
                for d_ in shape[1:]:
                    n *= d_
                nbytes = (n * esz + 63) // 64 * 64
                assert self.off + nbytes <= ARENA, (self.off, nbytes)
                v = arena[0:shape[0], self.off:self.off + n * esz].bitcast(dt)
                self.off += nbytes
                if len(shape) > 2:
                    names = " ".join("d%d" % i for i in range(1, len(shape)))
                    kw = {"d%d" % i: shape[i] for i in range(1, len(shape))}
                    v = v.rearrange("p (%s) -> p %s" % (names, names), **kw)
                return v

        carve = Carver()
        phase_ctr = [0]

        def phase_barrier():
            phase_ctr[0] += 1
            sch.add("pool", lambda e: e.memset(small[:, 63:64], 0.0), reads=["small63"], writes=["ARENA", "small63"])

        AR = ["ARENA"]

        DRAM_TOKS = set(["QF", "QFc", "KF", "fv_dst", "mla_dst", "rv_dst", "sv_dst", "FFL", "CNEG", "QM", "KM", "QR", "KR",
                         "KRT", "RG", "QS", "KS", "OH0", "OH1", "OH2", "OH3", "MIXT", "X1", "X2", "Y",
                         "COS0", "COS1", "SIN0", "SIN1"] + ["KF1_%d" % j for j in range(4)])

        def dma(q, out, in_, reads=(), writes=()):
            r2 = [t for t in reads if t not in DRAM_TOKS] + [t for t in writes if t in DRAM_TOKS]
            w2 = [t for t in writes if t not in DRAM_TOKS] + [t for t in reads if t in DRAM_TOKS]
            sch.add(q, lambda e: e.dma_start(out=out, in_=in_), r2, w2, ndma=1)

        def mm(out, lhsT, rhs, start, stop, reads, writes, sgc=False):
            if sgc:
                sch.add("pe", lambda e: e.matmul(out, lhsT=lhsT, rhs=rhs, start=start, stop=stop, skip_group_check=True),
                        reads, writes)
            else:
                sch.add("pe", lambda e: e.matmul(out, lhsT=lhsT, rhs=rhs, start=start, stop=stop), reads, writes)

        def tr(out, in_, idn, reads, writes):
            sch.add("pe", lambda e: e.transpose(out=out, in_=in_, identity=idn), reads, writes)

        def act(out, in_, func, reads, writes, bias=None, scale=None, accum=None):
            kw = {}
            if bias is not None:
                kw["bias"] = bias
            if scale is not None:
                kw["scale"] = scale
            if accum is not None:
                kw["accum_out"] = accum
            sch.add("act", lambda e: e.activation(out=out, in_=in_, func=func, **kw), reads, writes)

        def tt(eng, out, in0, in1, op, reads, writes):
            sch.add(eng, lambda e: e.tensor_tensor(out=out, in0=in0, in1=in1, op=op), reads, writes)

        def ts(eng, out, in0, s1, op0, reads, writes, s2=None, op1=None):
            if op1 is None:
                sch.add(eng, lambda e: e.tensor_scalar(out=out, in0=in0, scalar1=s1, scalar2=None, op0=op0), reads, writes)
            else:
                sch.add(eng, lambda e: e.tensor_scalar(out=out, in0=in0, scalar1=s1, scalar2=s2, op0=op0, op1=op1),
                        reads, writes)

        def stt(eng, out, in0, scalar, in1, op0, op1, reads, writes):
            sch.add(eng, lambda e: e.scalar_tensor_tensor(out=out, in0=in0, scalar=scalar, in1=in1, op0=op0, op1=op1),
                    reads, writes)

        def cp(eng, out, in_, reads, writes):
            if eng == "act":
                act(out, in_, AF.Copy, reads, writes)
            else:
                sch.add(eng, lambda e: e.tensor_copy(out=out, in_=in_), reads, writes)

        def recip(out, in_, reads, writes):
            sch.add("dve", lambda e: e.reciprocal(out=out, in_=in_), reads, writes)

        def memset(eng, ap, val, writes):
            sch.add(eng, lambda e: e.memset(ap, val), (), writes)

        rr = {"w32": 0, "wbf": 0, "psA": 0, "ev": 0, "psX3": 0}

        def nxt(key, n):
            v = rr[key]
            rr[key] = (v + 1) % n
            return v

        def w32():
            i = nxt("w32", NW)
            return wk32[i], "wk32_%d" % i

        def wbf():
            i = nxt("wbf", NWB)
            return wkbf[i], "wkbf_%d" % i

        def evq():
            return ("act", "dve")[nxt("ev", 2)]

        def rstd_col(col_ap, tok, n_inv):
            act(col_ap, col_ap, AF.Ln, [tok], [tok], bias=EPS, scale=n_inv)
            act(col_ap, col_ap, AF.Exp, [tok], [tok], scale=-0.5)

        dma("sp", ident[:], cst["c_ident"], (), ["ident"])
        dma("sp", uincl[:], cst["c_uincl"], (), ["uincl"])
        dma("sp", nlower[:], cst["c_nlower"], (), ["nlower"])
        dma("sp", c_tail[:], cst["c_tail"], (), ["c_tail"])
        dma("sp", c_adec[:], cst["c_adec"], (), ["c_adec"])
        dma("sp", c_rope[:], cst["c_rope"], (), ["c_rope"])
        memset("pool", onesb[:], 1.0, ["onesb"])
        memset("pool", onesf[:], 1.0, ["onesf"])
        memset("pool", ones64[:], 1.0 / 64.0, ["ones64"])
        bd64 = sb("bd64", [128, 128], F32)
        memset("pool", bd64[:], 1.0 / 64.0, ["bd64"])
        memset("pool", bd64[0:64, 64:128], 0.0, ["bd64"])
        memset("pool", bd64[64:128, 0:64], 0.0, ["bd64"])
        memset("pool", small[:], 0.0, ["small63"])
        cp("dve", identf[:], ident[:], ["ident"], ["identf"])
        carve.reset()
        ob = carve.take([128, S], BF16)
        sch.add("pool", lambda e: e.memset(ob, 1.0), AR, ["ob"])
        for h in range(4):
            dma("sp", KF[h, 64:67, :], ob[0:3, :], ["ob"] + AR, ["KF1_%d" % h])
        posi = carve.take([128, S], I32)
        posf = carve.take([128, S], F32)
        ang = carve.take([128, S], F32)
        kf_ = carve.take([128, S], F32)
        ki_ = carve.take([128, S], I32)
        dma("sp", posi, pos.partition_broadcast(128), AR, ["posi"])
        cp("dve", posf, posi, ["posi"] + AR, ["posf"])
        TWO_PI = 2.0 * math.pi
        C1 = 6.28125
        C2 = TWO_PI - C1
        for ti, (COS, SIN) in enumerate(((COSR, SINR), (COSM, SINM))):
            ts("dve", ang, posf, c_rope[:, ti:ti + 1], ALU.mult, ["posf", "c_rope"] + AR, ["ang"])
            for which in (0, 1):
                ts("dve", ki_, ang, 1.0 / TWO_PI, ALU.mult, ["ang"] + AR, ["ki"])
                cp("dve", kf_, ki_, ["ki"] + AR, ["kf"])
                stt("dve", posi.bitcast(F32), kf_, -C1, ang, ALU.mult, ALU.add, ["kf", "ang"] + AR, ["red"])
                red = posi.bitcast(F32)
                stt("dve", red, kf_, -C2, red, ALU.mult, ALU.add, ["kf", "red"] + AR, ["red"])
                if which == 1:
                    ts("dve", red, red, math.pi / 2.0, ALU.add, ["red"] + AR, ["red"])
                    ts("dve", kf_, red, math.pi, ALU.is_gt, ["red"] + AR, ["kf"])
                    stt("dve", red, kf_, -TWO_PI, red, ALU.mult, ALU.add, ["kf", "red"] + AR, ["red"])
                ts("dve", red, red, 3.141592, ALU.min, ["red"] + AR, ["red"], s2=-3.141592, op1=ALU.max)
                act(kf_, red, AF.Sin, ["red"] + AR, ["kf"])
                if which == 0:
                    ts("dve", kf_, kf_, c_rope[:, 2 + ti:3 + ti], ALU.mult, ["kf", "c_rope"] + AR, ["kf"])
                    dma("sp", SIN, kf_, ["kf"] + AR, ["SIN%d" % ti])
                else:
                    dma("sp", COS, kf_, ["kf"] + AR, ["COS%d" % ti])
        phase_barrier()

        for L in range(nlayers):
            if dbg == "setup":
                break
            x_src = x_in if L == 0 else X2
            x_dst = y_out if L == nlayers - 1 else X2
            dma("sp", gqc[:], gq_col[L], (), ["gqc"])
            dma("sp", gkvc[:], gkv_col[L], (), ["gkvc"])
            dma("sp", gmoc[:], gmo_col[L], (), ["gmoc"])
            dma("sp", bfc[:], bf_col[L], (), ["bfc"])

            carve.reset()
            winb = carve.take([128, 8, NINP], BF16)
            wq32 = carve.take([128, 2, 384], F32)
            wqn = carve.take([128, 2, 256], BF16)
            wqr = carve.take([128, 2, 128], BF16)
            wqp = carve.take([128, 2, 128], BF16)
            wkv32 = carve.take([128, 512], F32)
            wkn = carve.take([128, 256], BF16)
            wvv = carve.take([128, 256], BF16)
            hT = [carve.take([128, 8, 512], BF16) for _ in range(2)]
            tabs = [[carve.take([128, 512], F32) for _ in range(4)] for _ in range(2)]
            xt4 = [carve.take([128, 1024], F32) for _ in range(4)]
            cq32 = [carve.take([128, 512], F32) for _ in range(2)]
            sq32 = [carve.take([128, 512], F32) for _ in range(3)]
            cqn = [carve.take([128, 512], BF16) for _ in range(2)]
            ckvn = carve.take([128, 512], BF16)
            rstdb = carve.take([128, 512], F32)
            vstg = [carve.take([128, 4, 256], BF16) for _ in range(3)]
            krtstg = carve.take([128, 4, 256], BF16)
            c_qdummy = None

            for kc in range(8):
                rows = slice(kc * 128, (kc + 1) * 128)
                dma("pool", winb[:, kc, 0:NIN], w_in[L, rows, :], AR, ["winb%d" % kc])
                for nm, nmp, nh, hw in (("rq", "rqp", 4, 32), ("rk", "rkp", 4, 32), ("kr", "krp", 1, 16)):
                    src = winb[:, kc, OFF[nm]:OFF[nm] + nh * 2 * hw].rearrange("p (h t c) -> p h t c", h=nh, t=2)
                    dst = winb[:, kc, OFF[nmp]:OFF[nmp] + nh * 2 * hw].rearrange("p (h t c) -> p h t c", h=nh, t=2)
                    cp("pool", dst[:, :, 0, :], src[:, :, 1, :], ["winb%d" % kc] + AR, ["winbp%d" % kc])
                    cp("pool", dst[:, :, 1, :], src[:, :, 0, :], ["winb%d" % kc] + AR, ["winbp%d" % kc])
            if dbg == "p1a":
                break
            dma("sp", wq32, w_q_up[L].rearrange("(c p) n -> p c n", p=128), AR, ["wq32"])
            dma("sp", wkv32, w_kv_up[L], AR, ["wkv32"])
            for c in range(2):
                src4 = wq32[:, c, :].rearrange("p (h e) -> p h e", h=4)
                gcol = gqc[:, c:c + 1]
                ts("dve", wqn[:, c, :].rearrange("p (h e) -> p h e", h=4), src4[:, :, 0:64], gcol, ALU.mult,
                   ["wq32", "gqc"] + AR, ["wqb"])
                ts("dve", wqr[:, c, :].rearrange("p (h e) -> p h e", h=4), src4[:, :, 64:96], gcol, ALU.mult,
                   ["wq32", "gqc"] + AR, ["wqb"])
                dstp = wqp[:, c, :].rearrange("p (h t e) -> p h t e", h=4, t=2)
                ts("dve", dstp[:, :, 0, :], src4[:, :, 80:96], gcol, ALU.mult, ["wq32", "gqc"] + AR, ["wqb"])
                ts("dve", dstp[:, :, 1, :], src4[:, :, 64:80], gcol, ALU.mult, ["wq32", "gqc"] + AR, ["wqb"])
            kv4 = wkv32.rearrange("p (h e) -> p h e", h=4)
            ts("dve", wkn.rearrange("p (h e) -> p h e", h=4), kv4[:, :, 0:64], gkvc[:, 0:1], ALU.mult,
               ["wkv32", "gkvc"] + AR, ["wkvb"])
            ts("dve", wvv.rearrange("p (h e) -> p h e", h=4), kv4[:, :, 64:128], gkvc[:, 0:1], ALU.mult,
               ["wkv32", "gkvc"] + AR, ["wkvb"])
            dma("sp", gb[0][:], g_pre[L:L + 1, :].partition_broadcast(128), (), ["gb0"])

            WIN_ALL = ["winb%d" % k_ for k_ in range(8)] + ["winbp%d" % k_ for k_ in range(8)]

            def fm_group(hTt, hTtok, col_lo, M, ncols_stride=None):
                b = nxt("psA", 4)
                ps = PS[b]
                for kc in range(8):
                    mm(ps[0:M, :], winb[:, kc, col_lo:col_lo + M], hTt[:, kc, :], kc == 0, kc == 7,
                       ["winb%d" % kc, "winbp%d" % kc, hTtok] + AR, ["ps%d" % b])
                return ps, "ps%d" % b

            def store_rows(stg, stok, dsts, tsl):
                for (r0, r1, dram_rows, wtok) in dsts:
                    dma("sp", dram_rows[:, tsl], stg[r0:r1, :], [stok], [wtok])

            def load_tile_inputs(t):
                tsl_ = slice(t * 512, (t + 1) * 512)
                p_ = t % 2
                for sub in range(4):
                    r0 = t * 512 + sub * 128
                    dma("pool", xt4[sub], x_src[r0:r0 + 128, :], ["X2"] + AR, ["xt4_%d" % sub])
                for j, (SRC, tk) in enumerate(((COSR, "COS0"), (SINR, "SIN0"), (COSM, "COS1"), (SINM, "SIN1"))):
                    dma("pool", tabs[p_][j], SRC[:, tsl_], [tk] + AR, ["tab%d_%d" % (p_, j)])

            def norm_tile(t):
                hTt = hT[t % 2]
                hTtok = "hT%d" % (t % 2)
                p_ = t % 2
                for sub in range(4):
                    xi = sub % 2
                    xs = xt4[sub]
                    xtok = "xt4_%d" % sub
                    col = small[:, sub:sub + 1]
                    ctok = "sm%d" % sub
                    act(hbf[xi][:], xs, AF.Square, [xtok] + AR, ["hbf%d" % xi, ctok], accum=col)
                    rstd_col(col, ctok, 1.0 / D)
                    stt("dve", hbf[xi][:], xs, col, gb[0][:], ALU.mult, ALU.mult,
                        [xtok, ctok, "gb0"] + AR, ["hbf%d" % xi])
                    psT = PS[4][:].bitcast(BF16)
                    for kc in range(8):
                        tr(psT[:, kc * 128:(kc + 1) * 128], hbf[xi][:, kc * 128:(kc + 1) * 128], ident[:],
                           ["hbf%d" % xi, "ident"], ["ps4"])
                    cp("act" if sub % 2 else "dve", hTt[:, :, sub * 128:(sub + 1) * 128],
                       psT.rearrange("p (k c) -> p k c", k=8), ["ps4"] + AR, [hTtok])

            load_tile_inputs(0)
            norm_tile(0)
            for t in range(NT):
                tsl = slice(t * 512, (t + 1) * 512)
                hTt = hT[t % 2]
                hTtok = "hT%d" % (t % 2)
                if t + 1 < NT:
                    load_tile_inputs(t + 1)
                cr, sr, cm, sm = tabs[t % 2]
                crk, srk, cmk, smk = ["tab%d_%d" % (t % 2, j) for j in range(4)]
                for c in range(2):
                    ps, ptok = fm_group(hTt, hTtok, OFF["cq"] + c * 128, 128)
                    act(sq32[c], ps[:], AF.Square, [ptok] + AR, ["sq32_%d" % c])
                    cp("dve", cq32[c], ps[:], [ptok] + AR, ["cq32_%d" % c])
                for c in range(2):
                    mm(PS[6][:], onesf[:], sq32[c], c == 0, c == 1, ["onesf", "sq32_%d" % c] + AR, ["ps6"])
                act(rstdb, PS[6][:], AF.Ln, ["ps6"] + AR, ["rstdb"], bias=EPS, scale=1.0 / 256.0)
                act(rstdb, rstdb, AF.Exp, ["rstdb"] + AR, ["rstdb"], scale=-0.5)
                for c in range(2):
                    tt("pool" if c else "dve", cqn[c], cq32[c], rstdb, ALU.mult, ["cq32_%d" % c, "rstdb"] + AR, ["cqn%d" % c])
                ps, ptok = fm_group(hTt, hTtok, OFF["ckv"], 128)
                act(sq32[2], ps[:], AF.Square, [ptok] + AR, ["sq32_2"])
                cp("dve", cq32[0], ps[:], [ptok] + AR, ["cq32_0"])
                mm(PS[6][:], onesf[:], sq32[2], True, True, ["onesf", "sq32_2"] + AR, ["ps6"])
                act(rstdb, PS[6][:], AF.Ln, ["ps6"] + AR, ["rstdb"], bias=EPS, scale=1.0 / 128.0)
                act(rstdb, rstdb, AF.Exp, ["rstdb"] + AR, ["rstdb"], scale=-0.5)
                tt("dve", ckvn, cq32[0], rstdb, ALU.mult, ["cq32_0", "rstdb"] + AR, ["ckvn"])
                def simple_pair(col_lo, scale, DST, wtok):
                    for hp in range(2):
                        ps, ptok = fm_group(hTt, hTtok, col_lo + hp * 128, 128)
                        stg, stok = wbf()
                        if scale is None:
                            cp(evq(), stg[:], ps[:], [ptok], [stok])
                        else:
                            act(stg[:], ps[:], AF.Copy, [ptok], [stok], scale=scale)
                        store_rows(stg, stok, [(0, 64, DST[2 * hp, 0:64], wtok), (64, 128, DST[2 * hp + 1, 0:64], wtok)], tsl)

                simple_pair(OFF["fq"], 0.125, QF, "QF")
                simple_pair(OFF["fk"], None, KF, "KF")
                simple_pair(OFF["sq"], 0.125, QS, "QS")
                simple_pair(OFF["sk"], None, KS, "KS")
                ps, ptok = fm_group(hTt, hTtok, OFF["ff"], 4)
                stg, stok = w32()
                cp("dve", stg[0:4, :], ps[0:4, :], [ptok], [stok])
                dma("sp", FFL[:, tsl], stg[0:4, :], [stok], ["FFL"])
                for c in range(2):
                    ps, ptok = fm_group(hTt, hTtok, OFF["rg"] + c * 128, 128)
                    stg, stok = w32()
                    act(stg[:], ps[:], AF.Silu, [ptok], [stok])
                    dma("sp", RG[c * 128:(c + 1) * 128, tsl], stg[:], [stok], ["RG"])
                for nm, nmp, DST, wtok, scl in (("rq", "rqp", QR, "QR", 1.0), ("rk", "rkp", KR, "KR", 0.125)):
                    for hp in range(2):
                        psa, pta = fm_group(hTt, hTtok, OFF[nm] + hp * 128, 128)
                        psb, ptb = fm_group(hTt, hTtok, OFF[nmp] + hp * 128, 128)
                        t1, k1 = w32()
                        t2, k2 = w32()
                        stt("dve", t1[:], psa[:], scl, cr, ALU.mult, ALU.mult, [pta, crk] + AR, [k1])
                        stt("dve", t2[:], psb[:], scl, sr, ALU.mult, ALU.mult, [ptb, srk] + AR, [k2])
                        stg, stok = wbf()
                        tt("pool", stg[:], t1[:], t2[:], ALU.add, [k1, k2], [stok])
                        store_rows(stg, stok, [(0, 64, DST[2 * hp], wtok), (64, 128, DST[2 * hp + 1], wtok)], tsl)
                        if nm == "rk":
                            psT = PS[5][:].bitcast(BF16)
                            for sub in range(4):
                                tr(psT[:, sub * 128:(sub + 1) * 128], stg[:, sub * 128:(sub + 1) * 128], ident[:],
                                   [stok, "ident"], ["ps5"])
                            for hh in range(2):
                                h = 2 * hp + hh
                                ts("dve", krtstg[:, :, h * 64:(h + 1) * 64],
                                   psT[:, 0:512].rearrange("p (s a d) -> p s a d", s=4, a=2)[:, :, hh, :],
                                   c_tail[:, h:h + 1], ALU.mult, ["ps5", "c_tail"] + AR, ["krtstg"])
                            if hp == 1:
                                dma("sp", KRT[tsl, :, :].rearrange("(s p) h d -> p s (h d)", p=128), krtstg,
                                    ["krtstg"] + AR, ["KRT"])
                if t + 1 < NT:
                    norm_tile(t + 1)
                qsc = 96.0 ** -0.5
                for hp in range(2):
                    b = nxt("psA", 4)
                    for c in range(2):
                        mm(PS[b][:], wqn[:, c, hp * 128:(hp + 1) * 128], cqn[c], c == 0, c == 1,
                           ["wqb", "cqn%d" % c] + AR, ["ps%d" % b])
                    stg, stok = wbf()
                    act(stg[:], PS[b][:], AF.Copy, ["ps%d" % b], [stok], scale=qsc)
                    store_rows(stg, stok, [(0, 64, QM[2 * hp, 0:64], "QM"), (64, 128, QM[2 * hp + 1, 0:64], "QM")], tsl)
                ba = nxt("psA", 4)
                for c in range(2):
                    mm(PS[ba][:], wqr[:, c, :], cqn[c], c == 0, c == 1, ["wqb", "cqn%d" % c] + AR, ["ps%d" % ba])
                bb = nxt("psA", 4)
                for c in range(2):
                    mm(PS[bb][:], wqp[:, c, :], cqn[c], c == 0, c == 1, ["wqb", "cqn%d" % c] + AR, ["ps%d" % bb])
                t1, k1 = w32()
                t2, k2 = w32()
                stt("dve", t1[:], PS[ba][:], qsc, cm, ALU.mult, ALU.mult, ["ps%d" % ba, cmk] + AR, [k1])
                stt("dve", t2[:], PS[bb][:], qsc, sm, ALU.mult, ALU.mult, ["ps%d" % bb, smk] + AR, [k2])
                stg, stok = wbf()
                tt("pool", stg[:], t1[:], t2[:], ALU.add, [k1, k2], [stok])
                store_rows(stg, stok, [(32 * h, 32 * h + 32, QM[h, 64:96], "QM") for h in range(4)], tsl)
                for hp in range(2):
                    b = nxt("psA", 4)
                    mm(PS[b][:], wkn[:, hp * 128:(hp + 1) * 128], ckvn, True, True, ["wkvb", "ckvn"] + AR, ["ps%d" % b])
                    stg, stok = wbf()
                    cp(evq(), stg[:], PS[b][:], ["ps%d" % b], [stok])
                    store_rows(stg, stok, [(0, 64, KM[2 * hp, 0:64], "KM"), (64, 128, KM[2 * hp + 1, 0:64], "KM")], tsl)
                psa, pta = fm_group(hTt, hTtok, OFF["kr"], 32)
                psb, ptb = fm_group(hTt, hTtok, OFF["krp"], 32)
                t1, k1 = w32()
                t2, k2 = w32()
                tt("dve", t1[0:32, :], psa[0:32, :], cm[0:32, :], ALU.mult, [pta, cmk] + AR, [k1])
                tt("dve", t2[0:32, :], psb[0:32, :], sm[0:32, :], ALU.mult, [ptb, smk] + AR, [k2])
                stg, stok = wbf()
                tt("pool", stg[0:32, :], t1[0:32, :], t2[0:32, :], ALU.add, [k1, k2], [stok])
                store_rows(stg, stok, [(0, 32, KM[h, 64:96], "KM") for h in range(4)], tsl)
                for vi, (nm, DSTv) in enumerate((("fv", None), ("rv", None), ("sv", None), ("mla", None))):
                    vs = vstg[vi % 3]
                    vtok = "vstg%d" % (vi % 3)
                    for sub in range(4):
                        b = 6 + (sub % 2)
                        if nm == "mla":
                            mm(PS[b][:, 0:256], ckvn[:, sub * 128:(sub + 1) * 128], wvv, True, True,
                               ["ckvn", "wkvb"] + AR, ["ps%d" % b])
                        else:
                            for kc in range(8):
                                mm(PS[b][:, 0:256], hTt[:, kc, sub * 128:(sub + 1) * 128],
                                   winb[:, kc, OFF[nm]:OFF[nm] + 256], kc == 0, kc == 7,
                                   ["winb%d" % kc, hTtok] + AR, ["ps%d" % b])
                        cp(evq(), vs[:, sub, :], PS[b][:, 0:256], ["ps%d" % b] + AR, [vtok])
                    V_ = {"fv": VF, "mla": VM, "rv": VR, "sv": VS}[nm]
                    dma("sp", V_[tsl, :].rearrange("(s p) n -> p s n", p=128), vs, [vtok] + AR, [nm + "_dst"])
            phase_barrier()

            if dbg == "p1":
                break
            carve.reset()
            fl = carve.take([4, S], F32)
            ones4 = carve.take([4, S], F32)
            cc = carve.take([4, S], F32)
            r1 = carve.take([4, S], F32)
            chi = carve.take([4, S], BF16)
            cmid = carve.take([4, S], BF16)
            clo = carve.take([4, S], BF16)
            cneg = carve.take([128, NB, 4], F32)
            dma("sp", fl, FFL, ["FFL"] + AR, ["fl"])
            sch.add("pool", lambda e: e.memset(ones4, 1.0), AR, ["ones4"])
            act(fl, fl, AF.Identity, ["fl", "bfc"] + AR, ["fl"], bias=bfc[:, 0:1])
            act(fl, fl, AF.Exp, ["fl"] + AR, ["fl"], scale=-1.0)
            act(fl, fl, AF.Ln, ["fl"] + AR, ["fl"], bias=1.0)
            ts("dve", fl, fl, -1.0, ALU.mult, ["fl"] + AR, ["fl"])
            sch.add("dve", lambda e: e.tensor_tensor_scan(out=cc, data0=ones4, data1=fl, initial=0.0,
                                                          op0=ALU.mult, op1=ALU.add), ["fl", "ones4"] + AR, ["cc"])
            cp("dve", chi, cc, ["cc"] + AR, ["chi"])
            tt("dve", r1, cc, chi, ALU.subtract, ["cc", "chi"] + AR, ["r1"])
            cp("dve", cmid, r1, ["r1"] + AR, ["cmid"])
            tt("dve", r1, r1, cmid, ALU.subtract, ["r1", "cmid"] + AR, ["r1"])
            cp("dve", clo, r1, ["r1"] + AR, ["clo"])
            for h in range(4):
                for i, (src, tk) in enumerate(((chi, "chi"), (cmid, "cmid"), (clo, "clo"))):
                    dma("sp", QF[h, 64 + i:65 + i, :], src[h:h + 1, :], [tk] + AR, ["QFc"])
            for tb in range(NB):
                tr(PS[0][:, tb * 4:(tb + 1) * 4], cc[:, tb * 128:(tb + 1) * 128], identf[0:4, 0:4],
                   ["cc", "identf"] + AR, ["ps0"])
            ts("dve", cneg, PS[0][:, 0:NB * 4].rearrange("p (t h) -> p t h", h=4), -1.0, ALU.mult, ["ps0"] + AR, ["cneg"])
            dma("sp", CNEG.rearrange("(t p) h -> p t h", p=128), cneg, ["cneg"] + AR, ["CNEG"])
            phase_barrier()

            def softmax_attention(g, Qd, Kd, Vd, KD, mask_kind, use_bias):
                carve.reset()
                qT = [carve.take([128, S], BF16) for _ in range(2)]
                kT = [carve.take([128, S], BF16) for _ in range(2)]
                vA = [carve.take([128, NB, 128], BF16) for _ in range(2)]
                cng = carve.take([128, NB, 4], F32)
                msk = carve.take([128, 4, 512], BF16)
                dma("sp", msk, cst["c_masks"][:, mask_kind], AR, ["masks"])
                if use_bias:
                    dma("sp", cng, CNEG.rearrange("(t p) h -> p t h", p=128), ["CNEG"] + AR, ["cng"])
                LA = 3

                def load_head(h):
                    i2 = h % 2
                    dma("sp", qT[i2][0:KD, :], Qd[h], ["QF", "QFc", "QM"] + AR, ["qT%d" % i2])
                    dma("sp", kT[i2][0:KD, :], Kd[h], ["KF", "KM"] + ["KF1_%d" % j for j in range(4)] + AR, ["kT%d" % i2])
                    dma("sp", vA[i2][:, :, 0:64], Vd.rearrange("(t p) (h d) -> p t h d", p=128, h=4)[:, :, h, :],
                        ["fv_dst", "mla_dst"] + AR, ["vA%d" % i2])

                for i2_ in range(2):
                    sch.add("pool", (lambda t_: (lambda e: e.memset(t_[:, :, 64:128], 1.0)))(vA[i2_]), AR, ["vAones%d" % i2_])
                load_head(0)
                for h in range(4):
                    i2 = h % 2
                    if h + 1 < 4:
                        load_head(h + 1)
                    blocks = [(T, kb) for T in range(NT) for kb in range(4 * T + 4)]
                    st_ = {}

                    def stage_a(i):
                        T, kb = blocks[i]
                        qsl = slice(T * 512, (T + 1) * 512)
                        xb = nxt("psA", 4)
                        mm(PS[xb][:], kT[i2][0:KD, kb * 128:(kb + 1) * 128], qT[i2][0:KD, qsl], True, True,
                           ["kT%d" % i2, "qT%d" % i2] + AR, ["ps%d" % xb])
                        p_, ptok = wbf()
                        if use_bias:
                            act(p_[:], PS[xb][:], AF.Exp, ["ps%d" % xb, "cng"] + AR, [ptok], bias=cng[:, kb, h:h + 1])
                        else:
                            act(p_[:], PS[xb][:], AF.Exp, ["ps%d" % xb], [ptok])
                        if kb >= 4 * T:
                            tt("pool", p_[:], p_[:], msk[:, kb - 4 * T, :], ALU.mult, [ptok, "masks"] + AR, [ptok])
                        st_[i] = (p_, ptok)

                    def stage_b(i):
                        T, kb = blocks[i]
                        qsl = slice(T * 512, (T + 1) * 512)
                        nkb = 4 * T + 4
                        ob_ = 4 + (T % 2)
                        p_, ptok = st_.pop(i)
                        mm(PS[ob_][:], vA[i2][:, kb, :], p_[:], kb == 0, kb == nkb - 1,
                           ["vA%d" % i2, "vAones%d" % i2, ptok] + AR, ["ps%d" % ob_])
                        if kb == nkb - 1:
                            den, dtok = w32()
                            act(den[0:64, :], PS[ob_][64:128, :], AF.Copy, ["ps%d" % ob_], [dtok])
                            recip(den[0:64, :], den[0:64, :], [dtok], [dtok])
                            o_, otok = w32()
                            tt("dve", o_[0:64, :], PS[ob_][0:64, :], den[0:64, :], ALU.mult, ["ps%d" % ob_, dtok], [otok])
                            dma("sp", OH[g, h, :, qsl], o_[0:64, :], [otok], ["OH%d" % g])

                    n_ = len(blocks)
                    for i in range(n_ + LA):
                        if i < n_:
                            stage_a(i)
                        if i >= LA:
                            stage_b(i - LA)
                phase_barrier()

            softmax_attention(0, QF, KF, VF, 67, 0, True)
            softmax_attention(1, QM, KM, VM, 96, 1, False)

            carve.reset()
            qT = [carve.take([128, S], BF16) for _ in range(4)]
            kT = [carve.take([128, S], BF16) for _ in range(4)]
            nkT = [carve.take([128, S], BF16) for _ in range(4)]
            vS_ = carve.take([128, NB, 256], BF16)
            msk = carve.take([128, 4, 512], BF16)
            dma("sp", msk, cst["c_masks"][:, 2], AR, ["masks"])
            dma("sp", vS_, VS.rearrange("(t p) n -> p t n", p=128), ["sv_dst"] + AR, ["vS"])
            LA = 2
            NHL = 2 * 2 * (LA + 2)
            hl_pool = [carve.take([128, 512], BF16) for _ in range(NHL)]
            w_pool = [carve.take([128, 512], BF16) for _ in range(6)]
            rr["hl"] = 0
            rr["wp"] = 0

            def hl_tile():
                i_ = nxt("hl", NHL)
                return hl_pool[i_], "hl%d" % i_

            def w_tile():
                i_ = nxt("wp", 6)
                return w_pool[i_], "wp%d" % i_
            ACCB = (3, 6)
            OB = (4, 5)

            def load_head_sb(h):
                dma("sp", qT[h][0:64, :], QS[h], ["QS"] + AR, ["qT%d" % h])
                dma("sp", kT[h][0:64, :], KS[h], ["KS"] + AR, ["kT%d" % h])

            for h in range(4):
                e1, e2 = ("pool", "dve") if h % 2 == 0 else ("dve", "pool")
                sch.add(e1, (lambda t_: (lambda e: e.memset(t_[64:128, :], 0.0)))(qT[h]), AR, ["qTpad%d" % h])
                sch.add(e2, (lambda t_: (lambda e: e.memset(t_[64:128, :], 0.0)))(kT[h]), AR, ["kTpad%d" % h])
                sch.add(e1, (lambda t_: (lambda e: e.memset(t_[64:128, :], 0.0)))(nkT[h]), AR, ["nkTpad%d" % h])
                load_head_sb(h)
            for h in range(4):
                ts("dve" if h % 2 == 0 else "pool", nkT[h][0:64, :], kT[h][0:64, :], -1.0, ALU.mult,
                   ["kT%d" % h] + AR, ["nkT%d" % h])
            for hp in range(2):
                heads = (2 * hp, 2 * hp + 1)
                blocks = [(T, kb) for T in range(NT) for kb in range(4 * T + 3, -1, -1)]
                st_ = {}

                XB = (0, 1, 2, 7)

                def stage_z(i, c):
                    h = heads[c]
                    T, kb = blocks[i]
                    qsl = slice(T * 512, (T + 1) * 512)
                    xb = XB[(2 * i + c) % 4]
                    mm(PS[xb][:], kT[h][:, kb * 128:(kb + 1) * 128], qT[h][:, qsl], True, True,
                       ["kT%d" % h, "qT%d" % h, "kTpad%d" % h, "qTpad%d" % h] + AR, ["ps%d" % xb])

                def stage_a(i, c):
                    h = heads[c]
                    T, kb = blocks[i]
                    diag = kb >= 4 * T
                    xb = XB[(2 * i + c) % 4]
                    e_, etok = w32()
                    act(e_[:], PS[xb][:], AF.Exp, ["ps%d" % xb], [etok])
                    act(e_[:], e_[:], AF.Ln, [etok], [etok], bias=1.0)
                    if diag:
                        tt("pool", e_[:], e_[:], msk[:, kb - 4 * T, :], ALU.mult, [etok, "masks"] + AR, [etok])
                    hi, hitok = hl_tile()
                    lo, lotok = hl_tile()
                    cp("dve", hi[:], e_[:], [etok] + AR, [hitok])
                    tt("dve", lo[:], e_[:], hi[:], ALU.subtract, [etok, hitok] + AR, [lotok])
                    st_[(i, c)] = [hi, hitok, lo, lotok]

                def stage_b1(i, c):
                    h = heads[c]
                    T, kb = blocks[i]
                    qsl = slice(T * 512, (T + 1) * 512)
                    nkb = 4 * T + 4
                    diag = kb >= 4 * T
                    first = kb == nkb - 1
                    hi, hitok, lo, lotok = st_[(i, c)]
                    acc = PS[ACCB[c]]
                    atok = "ps%d" % ACCB[c]
                    mm(acc[:], uincl[:], hi[:], first, False, ["uincl", hitok] + AR, [atok], sgc=True)
                    mm(acc[:], uincl[:], lo[:], False, False, ["uincl", lotok] + AR, [atok], sgc=True)
                    mm(acc[:], kT[h][:, kb * 128:(kb + 1) * 128], qT[h][:, qsl], False, True,
                       ["kT%d" % h, "qT%d" % h, "kTpad%d" % h, "qTpad%d" % h] + AR, [atok], sgc=True)
                    w_, wtok = w_tile()
                    act(w_[:], acc[:], AF.Exp, [atok] + AR, [wtok])
                    if diag:
                        tt("pool", w_[:], w_[:], msk[:, kb - 4 * T, :], ALU.mult, [wtok, "masks"] + AR, [wtok])
                    st_[(i, c)] += [w_, wtok]

                def stage_b2(i, c):
                    h = heads[c]
                    T, kb = blocks[i]
                    qsl = slice(T * 512, (T + 1) * 512)
                    nkb = 4 * T + 4
                    first = kb == nkb - 1
                    last = kb == 0
                    hi, hitok, lo, lotok, w_, wtok = st_.pop((i, c))
                    acc = PS[ACCB[c]]
                    atok = "ps%d" % ACCB[c]
                    if not last:
                        mm(acc[:], nkT[h][:, kb * 128:(kb + 1) * 128], qT[h][:, qsl], False, False,
                           ["nkT%d" % h, "nkTpad%d" % h, "qT%d" % h, "qTpad%d" % h] + AR, [atok], sgc=True)
                        mm(acc[:], nlower[:], hi[:], False, False, ["nlower", hitok] + AR, [atok], sgc=True)
                        mm(acc[:], nlower[:], lo[:], False, True, ["nlower", lotok] + AR, [atok], sgc=True)
                    ob_ = OB[c]
                    mm(PS[ob_][:], vS_[:, kb, hp * 128:(hp + 1) * 128], w_[:], first, last,
                       ["vS", wtok] + AR, ["ps%d" % ob_])
                    if last:
                        o_, otok = w32()
                        rs_ = slice(c * 64, c * 64 + 64)
                        cp("act", o_[rs_, :], PS[ob_][rs_, :], ["ps%d" % ob_], [otok])
                        dma("sp", OH[3, h, :, qsl], o_[rs_, :], [otok], ["OH3"])

                n_ = len(blocks)
                stage_z(0, 0)
                stage_z(0, 1)
                for i in range(n_ + LA):
                    if i >= LA:
                        stage_b1(i - LA, 0)
                        stage_b1(i - LA, 1)
                    if i + 1 < n_:
                        stage_z(i + 1, 0)
                        stage_z(i + 1, 1)
                    if i < n_:
                        stage_a(i, 0)
                        stage_a(i, 1)
                    if i >= LA:
                        stage_b2(i - LA, 0)
                        stage_b2(i - LA, 1)
            phase_barrier()

            carve.reset()
            qR = [carve.take([128, S], BF16) for _ in range(2)]
            kR = [carve.take([128, S], BF16) for _ in range(2)]
            kRT = carve.take([128, NB, 256], BF16)
            vR = carve.take([128, NB, 256], BF16)
            rdec = carve.take([128, 4, 512], F32)
            qdec = carve.take([128, 4, 512], F32)
            KVs = carve.take([128, 2, NCH, 64], F32)
            SPb = carve.take([128, 2, NCH, 64], BF16)
            dma("sp", kRT, KRT.rearrange("(t p) h d -> p t (h d)", p=128), ["KRT"] + AR, ["kRT"])
            dma("sp", vR, VR.rearrange("(t p) n -> p t n", p=128), ["rv_dst"] + AR, ["vR"])
            dma("sp", rdec, cst["c_rdec"], AR, ["rdec"])
            dma("sp", qdec, cst["c_qdec"], AR, ["qdec"])
            for tb in range(NB):
                for j in range(2):
                    ch = 2 * tb + j
                    rs = slice(j * 64, (j + 1) * 64)
                    for hp in range(2):
                        b = nxt("psA", 4)
                        mm(PS[b][:, 0:128], kRT[rs, tb, hp * 128:(hp + 1) * 128], vR[rs, tb, hp * 128:(hp + 1) * 128],
                           True, True, ["kRT", "vR"] + AR, ["ps%d" % b])
                        cp("dve", KVs[0:64, hp, ch, :], PS[b][0:64, 0:64], ["ps%d" % b] + AR, ["KVs"])
                        cp("act", KVs[64:128, hp, ch, :], PS[b][64:128, 64:128], ["ps%d" % b] + AR, ["KVs"])
            for ch in range(1, NCH):
                for hp in range(2):
                    stt("dve", KVs[:, hp, ch, :], KVs[:, hp, ch - 1, :], c_adec[:, hp:hp + 1], KVs[:, hp, ch, :],
                        ALU.mult, ALU.add, ["KVs", "c_adec"] + AR, ["KVs"])
            cp("dve", SPb, KVs, ["KVs"] + AR, ["SPb"])
            for h in range(4):
                i2 = h % 2
                hp = h // 2
                prs = slice(i2 * 64, i2 * 64 + 64)
                dma("sp", qR[i2][prs, :], QR[h], ["QR"] + AR, ["qR%d" % i2])
                dma("sp", kR[i2][prs, :], KR[h], ["KR"] + AR, ["kR%d" % i2])
                for T in range(NT):
                    qsl = slice(T * 512, (T + 1) * 512)
                    xb = nxt("psA", 4)
                    for s4 in range(4):
                        csl = slice(T * 512 + s4 * 128, T * 512 + (s4 + 1) * 128)
                        mm(PS[xb][:, s4 * 128:(s4 + 1) * 128], kR[i2][prs, csl], qR[i2][prs, csl], s4 == 0, s4 == 3,
                           ["kR%d" % i2, "qR%d" % i2] + AR, ["ps%d" % xb])
                    sm_, smtok = wbf()
                    tt("dve", sm_[:], PS[xb][:], rdec[:, h, :], ALU.mult, ["ps%d" % xb, "rdec"] + AR, [smtok])
                    ob_ = 4 + (T % 2)
                    for s4 in range(4):
                        tb = T * 4 + s4
                        mm(PS[ob_][0:64, s4 * 128:(s4 + 1) * 128], vR[:, tb, h * 64:(h + 1) * 64],
                           sm_[:, s4 * 128:(s4 + 1) * 128], s4 == 0, False, ["vR", smtok] + AR, ["ps%d" % ob_])
                    qd_, qdtok = wbf()
                    tt("pool", qd_[prs, :], qR[i2][prs, qsl], qdec[prs, h, :], ALU.mult,
                       ["qR%d" % i2, "qdec"] + AR, [qdtok])
                    for c8 in range(8):
                        ch = T * 8 + c8
                        if ch == 0:
                            continue
                        mm(PS[ob_][0:64, c8 * 64:(c8 + 1) * 64], SPb[prs, hp, ch - 1, :], qd_[prs, c8 * 64:(c8 + 1) * 64],
                           False, c8 == 7, ["SPb", qdtok] + AR, ["ps%d" % ob_])
                    o_, otok = w32()
                    cp("act", o_[0:64, :], PS[ob_][0:64, :], ["ps%d" % ob_], [otok])
                    dma("sp", OH[2, h, :, qsl], o_[0:64, :], [otok], ["OH2"])
            phase_barrier()

            carve.reset()
            oh = [carve.take([128, 512], F32) for _ in range(4)]
            sqt = [carve.take([128, 512], F32) for _ in range(4)]
            rst = [carve.take([128, 512], F32) for _ in range(2)]
            ohr = [carve.take([128, 512], F32) for _ in range(3)]
            rgt = [carve.take([128, 512], F32) for _ in range(3)]
            dt_ = [carve.take([128, 512], F32) for _ in range(3)]
            st2 = [carve.take([128, 512], F32) for _ in range(3)]
            mo = [carve.take([128, 512], BF16) for _ in range(6)]
            rr["mo"] = 0
            units = []
            for T in range(NT):
                units += [(T, "rms", g) for g in (0, 1, 3)] + [(T, "ret", hp) for hp in range(2)]
            krms = [0]
            kret = [0]
            ust = {}

            def gn_a(k):
                T, kind, idx = units[k]
                qsl = slice(T * 512, (T + 1) * 512)
                if kind == "rms":
                    g = idx
                    p_ = krms[0] % 2
                    krms[0] += 1
                    bank = (0, 3)[p_]
                    src = OH[g].rearrange("h d s -> (h d) s")
                    for hp in range(2):
                        o_ = oh[2 * p_ + hp]
                        otok = "oh%d" % (2 * p_ + hp)
                        dma("sp", o_, src[hp * 128:(hp + 1) * 128, qsl], ["OH%d" % g] + AR, [otok])
                        act(sqt[2 * p_ + hp], o_, AF.Square, [otok] + AR, ["sqt%d" % (2 * p_ + hp)])
                        mm(PS[bank][:], onesf[:], sqt[2 * p_ + hp], hp == 0, hp == 1,
                           ["onesf", "sqt%d" % (2 * p_ + hp)] + AR, ["ps%d" % bank])
                    ust[k] = (p_, bank)
                else:
                    hp = idx
                    p_ = kret[0] % 3
                    q_ = kret[0] % 2
                    kret[0] += 1
                    b1, b2 = ((1, 2), (4, 5))[q_]
                    o_ = ohr[p_]
                    otok = "ohr%d" % p_
                    dma("sp", o_, OH[2].rearrange("h d s -> (h d) s")[hp * 128:(hp + 1) * 128, qsl], ["OH2"] + AR, [otok])
                    dma("sp", rgt[p_], RG[hp * 128:(hp + 1) * 128, qsl], ["RG"] + AR, ["rgt%d" % p_])
                    mm(PS[b1][:], bd64[:], o_, True, True, ["bd64", otok] + AR, ["ps%d" % b1])
                    tt("dve", dt_[p_], o_, PS[b1][:], ALU.subtract, [otok, "ps%d" % b1] + AR, ["dt%d" % p_])
                    act(st2[p_], dt_[p_], AF.Square, ["dt%d" % p_] + AR, ["st2_%d" % p_])
                    mm(PS[b2][:], bd64[:], st2[p_], True, True, ["bd64", "st2_%d" % p_] + AR, ["ps%d" % b2])
                    ust[k] = (p_, b2)

            def gn_b(k):
                T, kind, idx = units[k]
                qsl = slice(T * 512, (T + 1) * 512)
                p_, bank = ust.pop(k)
                if kind == "rms":
                    g = idx
                    rs_ = rst[p_]
                    rtok = "rst%d" % p_
                    act(rs_, PS[bank][:], AF.Ln, ["ps%d" % bank] + AR, [rtok], bias=EPS, scale=1.0 / 256.0)
                    act(rs_, rs_, AF.Exp, [rtok] + AR, [rtok], scale=-0.5)
                    for hp in range(2):
                        o_ = oh[2 * p_ + hp]
                        otok = "oh%d" % (2 * p_ + hp)
                        mi = nxt("mo", 6)
                        stt("dve", mo[mi], o_, gmoc[:, 2 * g + hp:2 * g + hp + 1], rs_, ALU.mult, ALU.mult,
                            [otok, "gmoc", rtok] + AR, ["mo%d" % mi])
                        dma("pool", MIXT[g * 256 + hp * 128:g * 256 + (hp + 1) * 128, qsl], mo[mi], ["mo%d" % mi] + AR, ["MIXT"])
                else:
                    hp = idx
                    s_ = st2[p_]
                    stok = "st2_%d" % p_
                    act(s_, PS[bank][:], AF.Ln, ["ps%d" % bank] + AR, [stok], bias=EPS)
                    act(s_, s_, AF.Exp, [stok] + AR, [stok], scale=-0.5)
                    stt("dve", dt_[p_], dt_[p_], gmoc[:, 4 + hp:5 + hp], s_, ALU.mult, ALU.mult,
                        ["dt%d" % p_, "gmoc", stok] + AR, ["dt%d" % p_])
                    mi = nxt("mo", 6)
                    tt("dve", mo[mi], dt_[p_], rgt[p_], ALU.mult, ["dt%d" % p_, "rgt%d" % p_] + AR, ["mo%d" % mi])
                    dma("pool", MIXT[512 + hp * 128:512 + (hp + 1) * 128, qsl], mo[mi], ["mo%d" % mi] + AR, ["MIXT"])

            for k in range(len(units) + 1):
                if k < len(units):
                    gn_a(k)
                if k >= 1:
                    gn_b(k - 1)
            phase_barrier()

            carve.reset()
            woutb = carve.take([128, 8, D], BF16)
            mixt = [carve.take([128, 8, 512], BF16) for _ in range(2)]
            for kc in range(8):
                dma("pool", woutb[:, kc, :], w_out[L, kc * 128:(kc + 1) * 128, :], AR, ["woutb%d" % kc])
            dma("sp", gb[1][:], g_post[L:L + 1, :].partition_broadcast(128), (), ["gb1"])
            for T in range(NT):
                mt = mixt[T % 2]
                mtok_ = "mixt%d" % (T % 2)
                dma("sp", mt, MIXT.rearrange("(k p) s -> p k s", p=128)[:, :, T * 512:(T + 1) * 512], ["MIXT"] + AR, [mtok_])
                for sub in range(4):
                    r0 = T * 512 + sub * 128
                    xi = sub % 2
                    dma("sp", xt[sub][:], x_src[r0:r0 + 128, :], ["X2"], ["xt%d" % sub])
                    for half in range(2):
                        b = 2 * xi + half
                        for kc in range(8):
                            mm(PS[b][:], mt[:, kc, sub * 128:(sub + 1) * 128], woutb[:, kc, half * 512:(half + 1) * 512],
                               kc == 0, kc == 7, [mtok_, "woutb%d" % kc] + AR, ["ps%d" % b])
                        act(yt[xi][:, half * 512:(half + 1) * 512], PS[b][:], AF.Square, ["ps%d" % b],
                            ["yt%d" % xi, "smc%d" % (8 + 2 * xi + half)], accum=small[:, 8 + 2 * xi + half:9 + 2 * xi + half])
                    col = small[:, 8 + 2 * xi:9 + 2 * xi]
                    ctok = "smc%d" % (8 + 2 * xi)
                    tt("dve", col, col, small[:, 9 + 2 * xi:10 + 2 * xi], ALU.add, [ctok, "smc%d" % (9 + 2 * xi)], [ctok])
                    rstd_col(col, ctok, 1.0 / D)
                    for half in range(2):
                        b = 2 * xi + half
                        hs = slice(half * 512, (half + 1) * 512)
                        stt("dve", yt[xi][:, hs], PS[b][:], col, gb[1][:, hs], ALU.mult, ALU.mult,
                            ["ps%d" % b, ctok, "gb1"], ["yt%d" % xi])
                    tt("dve", yt[xi][:], yt[xi][:], xt[sub][:], ALU.add, ["yt%d" % xi, "xt%d" % sub], ["yt%d" % xi])
                    dma("pool", X1[r0:r0 + 128, :], yt[xi][:], ["yt%d" % xi], ["X1"])
            phase_barrier()

            carve.reset()
            wupb = carve.take([128, 8, DFF], BF16)
            wdnb = carve.take([128, 32, D], BF16)
            TOK = 256
            uT = carve.take([128, 32, TOK], BF16)
            h2T = carve.take([128, 8, TOK], BF16)
            for kc in range(8):
                dma("pool", wupb[:, kc, :], w_up[L, kc * 128:(kc + 1) * 128, :], AR, ["wupb%d" % kc])
            for f4 in range(8):
                dma("pool", wdnb[:, 4 * f4:4 * f4 + 4, :],
                    w_dn[L, f4 * 512:(f4 + 1) * 512, :].rearrange("(f p) n -> p f n", p=128), AR, ["wdnb%d" % f4])
            dma("sp", gb[0][:], g_fpre[L:L + 1, :].partition_broadcast(128), (), ["gb0"])
            dma("sp", gb[1][:], g_fpost[L:L + 1, :].partition_broadcast(128), (), ["gb1"])
            NTT = S // TOK

            def f_load_x(T):
                for sub in range(2):
                    r0 = T * TOK + sub * 128
                    xq = (T % 2) * 2 + sub
                    dma("sp", xt[xq][:], X1[r0:r0 + 128, :], ["X1"], ["xt%d" % xq])

            def f_norm_pre(T, sub):
                xq = (T % 2) * 2 + sub
                col = small[:, 16 + sub:17 + sub]
                ctok = "smd%d" % sub
                act(hbf[sub][:], xt[xq][:], AF.Square, ["xt%d" % xq], ["hbf%d" % sub, ctok], accum=col)
                rstd_col(col, ctok, 1.0 / D)
                stt("dve", hbf[sub][:], xt[xq][:], col, gb[0][:], ALU.mult, ALU.mult,
                    ["xt%d" % xq, ctok, "gb0"], ["hbf%d" % sub])

            def f_tr(T, sub):
                psT = PS[sub][:].bitcast(BF16)
                for kc in range(8):
                    tr(psT[:, kc * 128:(kc + 1) * 128], hbf[sub][:, kc * 128:(kc + 1) * 128], ident[:],
                       ["hbf%d" % sub, "ident"], ["ps%d" % sub])
                cp("act" if sub else "dve", h2T[:, :, sub * 128:(sub + 1) * 128],
                   psT.rearrange("p (k c) -> p k c", k=8), ["ps%d" % sub] + AR, ["h2T"])

            def f_up(T):
                for fc in range(32):
                    b = fc % 4
                    for kc in range(8):
                        mm(PS[b][:, 0:TOK], wupb[:, kc, fc * 128:(fc + 1) * 128], h2T[:, kc, :], kc == 0, kc == 7,
                           ["wupb%d" % kc, "h2T"] + AR, ["ps%d" % b])
                    r_, rtok = w32()
                    act(r_[:, 0:TOK], PS[b][:, 0:TOK], AF.Relu, ["ps%d" % b], [rtok])
                    tt("pool" if fc % 2 else "dve", uT[:, fc, :], r_[:, 0:TOK], r_[:, 0:TOK], ALU.mult, [rtok] + AR, ["uT"])

            def f_down(T, sub):
                for half in range(2):
                    b = 4 + 2 * sub + half
                    for fc in range(32):
                        mm(PS[b][:], uT[:, fc, sub * 128:(sub + 1) * 128], wdnb[:, fc, half * 512:(half + 1) * 512],
                           fc == 0, fc == 31, ["uT", "wdnb%d" % (fc // 4)] + AR, ["ps%d" % b])

            def f_post(T, sub):
                r0 = T * TOK + sub * 128
                xq = (T % 2) * 2 + sub
                for half in range(2):
                    b = 4 + 2 * sub + half
                    act(yt[sub][:, half * 512:(half + 1) * 512], PS[b][:], AF.Square, ["ps%d" % b],
                        ["yt%d" % sub, "sme%d" % (2 * sub + half)],
                        accum=small[:, 24 + 2 * sub + half:25 + 2 * sub + half])
                col = small[:, 24 + 2 * sub:25 + 2 * sub]
                ctok = "sme%d" % (2 * sub)
                tt("dve", col, col, small[:, 25 + 2 * sub:26 + 2 * sub], ALU.add, [ctok, "sme%d" % (2 * sub + 1)], [ctok])
                rstd_col(col, ctok, 1.0 / D)
                for half in range(2):
                    b = 4 + 2 * sub + half
                    hs = slice(half * 512, (half + 1) * 512)
                    stt("dve", yt[sub][:, hs], PS[b][:], col, gb[1][:, hs], ALU.mult, ALU.mult,
                        ["ps%d" % b, ctok, "gb1"], ["yt%d" % sub])
                tt("pool", yt[sub][:], yt[sub][:], xt[xq][:], ALU.add, ["yt%d" % sub, "xt%d" % xq], ["yt%d" % sub])
                dma("sp", x_dst[r0:r0 + 128, :], yt[sub][:], ["yt%d" % sub], ["X2" if x_dst is X2 else "Y"])

            f_load_x(0)
            for sub in range(2):
                f_norm_pre(0, sub)
                f_tr(0, sub)
            for T in range(NTT):
                nxt_ = T + 1 < NTT
                if nxt_:
                    f_load_x(T + 1)
                f_up(T)
                f_down(T, 0)
                f_post(T, 0)
                if nxt_:
                    f_norm_pre(T + 1, 0)
                f_down(T, 1)
                if nxt_:
                    f_norm_pre(T + 1, 1)
                    f_tr(T + 1, 0)
                    f_tr(T + 1, 1)
                f_post(T, 1)
            phase_barrier()

        sch.add("sp", lambda e: e.nop(), (), list(set(sch.lastw.keys()) | set(sch.readers.keys())), force=True)
        sch.emit(nc, st)
    return nc, sch


_CACHE = {}


def make_in_maps(inputs, S, n_cores):
    consts = host_consts()
    maps = []
    f = lambda a: np.ascontiguousarray(np.asarray(a, dtype=np.float32))
    gq = f(inputs["g_q_lora"]).reshape(2, 2, 128).transpose(0, 2, 1)
    gkv = f(inputs["g_kv_lora"]).reshape(2, 128, 1)
    gmo = f(inputs["g_mix_out"]).reshape(2, 8, 128).transpose(0, 2, 1)
    bfc = f(inputs["b_forget"]).reshape(2, 4, 1)
    shared = dict(w_in=f(inputs["w_in"]), w_q_up=f(inputs["w_q_up"]), w_kv_up=f(inputs["w_kv_up"]),
                  w_out=f(inputs["w_out"]), w_ffn_up=f(inputs["w_ffn_up"]), w_ffn_down=f(inputs["w_ffn_down"]),
                  g_mix_pre=f(inputs["g_mix_pre"]), g_mix_post=f(inputs["g_mix_post"]),
                  g_ffn_pre=f(inputs["g_ffn_pre"]), g_ffn_post=f(inputs["g_ffn_post"]),
                  gq_col=np.ascontiguousarray(gq), gkv_col=np.ascontiguousarray(gkv),
                  gmo_col=np.ascontiguousarray(gmo), bf_col=np.ascontiguousarray(bfc))
    shared.update(consts)
    xs = f(inputs["x"])
    ps = np.asarray(inputs["positions"]).astype(np.int32)
    for c in range(n_cores):
        m = dict(shared)
        m["x"] = np.ascontiguousarray(xs[c])
        m["pos"] = np.ascontiguousarray(ps[c:c + 1])
        maps.append(m)
    return maps


def kernel(**inputs):
    x = np.asarray(inputs["x"])
    B, S, _ = x.shape
    if S not in _CACHE:
        _CACHE[S] = build(S)[0]
    nc = _CACHE[S]
    maps = make_in_maps(inputs, S, B)
    res = run_bass_kernel_spmd(nc, maps, core_ids=list(range(B)))
    return np.stack([np.asarray(r["y"], dtype=np.float32) for r in res.results], axis=0)
```

```python
import math
import numpy as np
import ml_dtypes
from contextlib import ExitStack
import concourse.bass as bass
import concourse.mybir as mybir
from concourse.bass_utils import run_bass_kernel_spmd

F32 = mybir.dt.float32
BF16 = mybir.dt.bfloat16
I32 = mybir.dt.int32
AF = mybir.ActivationFunctionType
ALU = mybir.AluOpType

COMPUTE = ("pe", "act", "dve", "pool", "sp")
N_DMA_SEMS = 48
SAME_ENGINE_SYNC = True

D = 1024
DFF = 4096
NIN = 2980
EPS = 1e-6
OFF = dict(fq=0, fk=256, fv=512, ff=768, cq=772, ckv=1028, kr=1156, rq=1188, rk=1444, rv=1700,
           rg=1956, sq=2212, sk=2468, sv=2724, rqp=2980, rkp=3236, krp=3492)
NINP = 3524


class _Op:
    __slots__ = ("fn", "deps", "raw", "ndma", "signal", "sem", "val", "clock", "queue")

    def __init__(self, fn, deps, ndma, queue, raw=()):
        self.fn = fn
        self.deps = deps
        self.raw = raw
        self.ndma = ndma
        self.signal = False
        self.sem = None
        self.val = 0
        self.clock = None
        self.queue = queue


class Sched:
    def __init__(self):
        self.ops = []
        self.lastw = {}
        self.readers = {}

    def add(self, eng, fn, reads=(), writes=(), ndma=0, force=False):
        import os
        mx = int(os.environ.get("MAX_OPS", "0"))
        if mx and len(self.ops) >= mx and not force:
            return -1
        i = len(self.ops)
        deps = set()
        if any(isinstance(t, str) and t.startswith("ps") for t in reads):
            writes = list(writes) + [t for t in reads if isinstance(t, str) and t.startswith("ps") and t not in writes]
            reads = [t for t in reads if not (isinstance(t, str) and t.startswith("ps"))]
        raw = set()
        for t in reads:
            w = self.lastw.get(t)
            if w is not None:
                deps.add(w)
                raw.add(w)
        for t in writes:
            w = self.lastw.get(t)
            if w is not None:
                deps.add(w)
            r = self.readers.get(t)
            if r:
                deps.update(r)
        for t in reads:
            self.readers.setdefault(t, []).append(i)
        for t in writes:
            self.lastw[t] = i
            self.readers[t] = []
        self.ops.append(_Op(fn, deps, ndma, eng, raw))
        return i

    def emit(self, nc, stack):
        ops = self.ops
        queues = {}
        for i, op in enumerate(ops):
            queues.setdefault(op.queue, []).append(i)
        esem = {q: stack.enter_context(nc.semaphore("s_" + q)) for q in COMPUTE}
        dsems = [stack.enter_context(nc.semaphore("d_%d" % k)) for k in range(N_DMA_SEMS)]
        dcount = [0] * N_DMA_SEMS
        dlast = [None] * N_DMA_SEMS
        N_SW = 8
        kk = {"pool": 0, "sp": 0}
        for i, op in enumerate(ops):
            if op.ndma:
                if op.queue == "pool":
                    k = kk["pool"] % N_SW
                    kk["pool"] += 1
                else:
                    k = N_SW + kk["sp"] % (N_DMA_SEMS - N_SW)
                    kk["sp"] += 1
                op.sem = ("d", k)
                if dlast[k] is not None:
                    op.deps.add(dlast[k])
                dlast[k] = i
                dcount[k] += 16 * op.ndma
                op.val = dcount[k]
                op.signal = True

        def skip(dop, op):
            if dop.ndma or op.ndma or dop.queue != op.queue:
                return False
            return dop.queue == "pe" or not SAME_ENGINE_SYNC

        opidx = {id(o): i for i, o in enumerate(ops)}

        for op in ops:
            for d in op.deps:
                dop = ops[d]
                if dop.ndma or skip(dop, op):
                    continue
                dop.signal = True
        cnt = {q: 0 for q in COMPUTE}
        for op in ops:
            if not op.ndma:
                if op.signal:
                    cnt[op.queue] += 1
                op.sem = ("e", op.queue)
                op.val = cnt[op.queue]
        kn = {q: {} for q in queues}
        for op in ops:
            kq = kn[op.queue]
            for d in op.deps:
                dop = ops[d]
                if kq.get(dop.sem, 0) < dop.val:
                    kq[dop.sem] = dop.val
                if dop.clock:
                    for s, v in dop.clock.items():
                        if kq.get(s, 0) < v:
                            kq[s] = v
            if op.signal:
                c = dict(kq)
                c[op.sem] = op.val
                op.clock = c

        def semobj(s):
            return esem[s[1]] if s[0] == "e" else dsems[s[1]]

        block = stack.enter_context(nc.Block())
        self.nwaits = 0

        def run_queue(q, eng):
            known = {}
            for i in queues[q]:
                op = ops[i]
                need = {}
                for d in op.deps:
                    dop = ops[d]
                    if skip(dop, op):
                        continue
                    if known.get(dop.sem, 0) >= dop.val:
                        continue
                    if need.get(dop.sem, 0) < dop.val:
                        need[dop.sem] = dop.val
                for d in op.deps:
                    dop = ops[d]
                    if skip(dop, op):
                        continue
                    if dop.clock:
                        for s, v in dop.clock.items():
                            if known.get(s, 0) < v:
                                known[s] = v
                for s, v in need.items():
                    eng.wait_ge(semobj(s), v)
                    self.nwaits += 1
                    if known.get(s, 0) < v:
                        known[s] = v
                ins = op.fn(eng)
                if op.ndma:
                    lst = ins if isinstance(ins, (list, tuple)) else [ins]
                    assert len(lst) == op.ndma
                    for x_ in lst:
                        x_.then_inc(semobj(op.sem), 16)
                elif op.signal:
                    ins.then_inc(semobj(op.sem), 1)

        def mk(q):
            return lambda eng: run_queue(q, eng)

        handlers = {"pe": block.tensor, "act": block.scalar, "dve": block.vector,
                    "pool": block.gpsimd, "sp": block.sync}
        for q in queues:
            handlers[q](mk(q))


def host_consts():
    c = {}
    bf = ml_dtypes.bfloat16
    c["c_ident"] = np.eye(128, dtype=np.float32).astype(bf)
    kk = np.arange(128)[:, None, None]
    jj = np.arange(4)[None, :, None]
    qq = np.arange(512)[None, None, :]
    key = 128 * jj + kk
    m = np.zeros((128, 3, 4, 512), np.float32)
    m[:, 0] = (key <= qq)
    m[:, 1] = ((key // 64) <= (qq // 64))
    m[:, 2] = (key < qq)
    c["c_masks"] = m.astype(bf)
    j = np.arange(128)[:, None]
    s = np.arange(128)[None, :]
    c["c_uincl"] = (-(j >= s).astype(np.float32)).astype(bf)
    c["c_nlower"] = (-(j < s).astype(np.float32)).astype(bf)
    h = np.arange(4, dtype=np.float32)
    log_gamma = np.log1p(-np.power(np.float32(2.0), np.float32(-5.0) - h)).astype(np.float32)
    idx = np.arange(64, dtype=np.float32)
    m_ = np.arange(128)
    dec = np.zeros((128, 4, 128), np.float32)
    for hh in range(4):
        dd = np.exp(log_gamma[hh] * np.abs(m_[:, None] - m_[None, :]).astype(np.float32)).astype(np.float32)
        same = (m_[:, None] // 64) == (m_[None, :] // 64)
        dec[:, hh, :] = np.where(same, dd, 0.0)
    c["c_rdec"] = np.tile(dec[:, :, None, :], (1, 1, 4, 1)).reshape(128, 4, 512).astype(np.float32)
    tail = np.exp(log_gamma[None, :] * (63.0 - idx)[:, None]).astype(np.float32)
    c["c_tail"] = np.tile(tail, (2, 1)).astype(np.float32)
    qh = np.exp(log_gamma[None, :] * (idx + 1.0)[:, None]).astype(np.float32)
    qd = np.tile(qh.T[None, :, None, :], (128, 1, 8, 1)).reshape(128, 4, 512)
    c["c_qdec"] = qd.astype(np.float32)
    a_ = np.exp(log_gamma * np.float32(64.0)).astype(np.float32)
    ad = np.zeros((128, 2), np.float32)
    for hp_ in range(2):
        ad[0:64, hp_] = a_[2 * hp_]
        ad[64:128, hp_] = a_[2 * hp_ + 1]
    c["c_adec"] = ad
    r = np.arange(128)
    invf_r = (np.float32(10000.0) ** (-(r % 32).astype(np.float32) / np.float32(32))).astype(np.float32)
    invf_m = (np.float32(10000.0) ** (-(r % 16).astype(np.float32) / np.float32(16))).astype(np.float32)
    sg_r = np.where((r % 64) < 32, -1.0, 1.0).astype(np.float32)
    sg_m = np.where((r % 32) < 16, -1.0, 1.0).astype(np.float32)
    c["c_rope"] = np.stack([invf_r, invf_m, sg_r, sg_m], axis=1).astype(np.float32)
    return c


CONST_SHAPES = dict(c_ident=([128, 128], BF16), c_masks=([128, 3, 4, 512], BF16), c_uincl=([128, 128], BF16), c_nlower=([128, 128], BF16),
                    c_rdec=([128, 4, 512], F32), c_tail=([128, 4], F32), c_qdec=([128, 4, 512], F32),
                    c_adec=([128, 2], F32), c_rope=([128, 4], F32))


def build(S, dbg=False, nlayers=2):
    NT = S // 512
    NB = S // 128
    NCH = S // 64
    nc = bass.Bass("TRN2", target_bir_lowering=False)
    sch = Sched()

    def din(name, shape, dt=F32):
        return nc.dram_tensor(name, shape, dt, kind="ExternalInput").ap()

    def dscr(name, shape, dt=F32):
        return nc.dram_tensor(name, shape, dt, kind=("ExternalOutput" if dbg else "Internal")).ap()

    x_in = din("x", [S, D])
    pos = din("pos", [1, S], I32)
    w_in = din("w_in", [2, D, NIN])
    w_q_up = din("w_q_up", [2, 256, 384])
    w_kv_up = din("w_kv_up", [2, 128, 512])
    w_out = din("w_out", [2, D, D])
    w_up = din("w_ffn_up", [2, D, DFF])
    w_dn = din("w_ffn_down", [2, DFF, D])
    g_pre = din("g_mix_pre", [2, D])
    g_post = din("g_mix_post", [2, D])
    g_fpre = din("g_ffn_pre", [2, D])
    g_fpost = din("g_ffn_post", [2, D])
    gq_col = din("gq_col", [2, 128, 2])
    gkv_col = din("gkv_col", [2, 128, 1])
    gmo_col = din("gmo_col", [2, 128, 8])
    bf_col = din("bf_col", [2, 4, 1])
    cst = {k: din(k, sh, dt) for k, (sh, dt) in CONST_SHAPES.items()}
    y_out = nc.dram_tensor("y", [S, D], F32, kind="ExternalOutput").ap()

    COSR = dscr("COSR", [128, S]); SINR = dscr("SINR", [128, S])
    COSM = dscr("COSM", [128, S]); SINM = dscr("SINM", [128, S])
    QF = dscr("QF", [4, 67, S], BF16); KF = dscr("KF", [4, 67, S], BF16)
    VF = dscr("VF", [S, 256], BF16); FFL = dscr("FFL", [4, S])
    CNEG = dscr("CNEG", [S, 4])
    QM = dscr("QM", [4, 96, S], BF16); KM = dscr("KM", [4, 96, S], BF16); VM = dscr("VM", [S, 256], BF16)
    QR = dscr("QR", [4, 64, S], BF16); KR = dscr("KR", [4, 64, S], BF16)
    KRT = dscr("KRT", [S, 4, 64], BF16); VR = dscr("VR", [S, 256], BF16); RG = dscr("RG", [256, S])
    QS = dscr("QS", [4, 64, S], BF16); KS = dscr("KS", [4, 64, S], BF16); VS = dscr("VS", [S, 256], BF16)
    OH = dscr("OH", [4, 4, 64, S])
    MIXT = dscr("MIXT", [D, S], BF16)
    X1 = dscr("X1", [S, D])
    X2 = dscr("X2", [S, D])

    st = ExitStack()
    with st:
        def sb(name, shape, dt):
            return st.enter_context(nc.sbuf_tensor("sb_" + name, shape, dt))

        ident = sb("ident", [128, 128], BF16)
        identf = sb("identf", [128, 128], F32)
        uincl = sb("uincl", [128, 128], BF16)
        nlower = sb("nlower", [128, 128], BF16)
        onesb = sb("onesb", [128, 128], BF16)
        onesf = sb("onesf", [128, 128], F32)
        ones64 = sb("ones64", [128, 64], F32)
        c_tail = sb("c_tail", [128, 4], F32)
        c_adec = sb("c_adec", [128, 2], F32)
        c_rope = sb("c_rope", [128, 4], F32)
        gqc = sb("gqc", [128, 2], F32)
        gkvc = sb("gkvc", [128, 1], F32)
        gmoc = sb("gmoc", [128, 8], F32)
        bfc = sb("bfc", [4, 1], F32)
        small = sb("small", [128, 64], F32)
        ARENA = 150 * 1024
        arena = sb("arena", [128, ARENA], mybir.dt.uint8)
        NW = 5
        wk32 = [sb("wk32_%d" % i, [128, 512], F32) for i in range(NW)]
        NWB = 8
        wkbf = [sb("wkbf_%d" % i, [128, 512], BF16) for i in range(NWB)]
        xt = [sb("xt_%d" % i, [128, 1024], F32) for i in range(4)]
        yt = [sb("yt_%d" % i, [128, 1024], F32) for i in range(2)]
        hbf = [sb("hbf_%d" % i, [128, 1024], BF16) for i in range(2)]
        gb = [sb("gb_%d" % i, [128, 1024], F32) for i in range(2)]
        PS = [st.enter_context(nc.psum_tensor("ps%d" % i, [128, 512], F32)) for i in range(8)]

        class Carver:
            def __init__(self):
                self.off = 0

            def reset(self):
                self.off = 0

            def take(self, shape, dt):
                esz = {F32: 4, BF16: 2, I32: 4}[dt]
                n = 1
                for d_ in shape[1:]:
                    n *= d_
                nbytes = (n * esz + 63) // 64 * 64
                assert self.off + nbytes <= ARENA, (self.off, nbytes)
                v = arena[0:shape[0], self.off:self.off + n * esz].bitcast(dt)
                self.off += nbytes
                if len(shape) > 2:
                    names = " ".join("d%d" % i for i in range(1, len(shape)))
                    kw = {"d%d" % i: shape[i] for i in range(1, len(shape))}
                    v = v.rearrange("p (%s) -> p %s" % (names, names), **kw)
                return v

        carve = Carver()
        phase_ctr = [0]

        def phase_barrier():
            phase_ctr[0] += 1
            sch.add("pool", lambda e: e.memset(small[:, 63:64], 0.0), reads=["small63"], writes=["ARENA", "small63"])

        AR = ["ARENA"]

        DRAM_TOKS = set(["QF", "QFc", "KF", "fv_dst", "mla_dst", "rv_dst", "sv_dst", "FFL", "CNEG", "QM", "KM", "QR", "KR",
                         "KRT", "RG", "QS", "KS", "OH0", "OH1", "OH2", "OH3", "MIXT", "X1", "X2", "Y",
                         "COS0", "COS1", "SIN0", "SIN1"] + ["KF1_%d" % j for j in range(4)])

        def dma(q, out, in_, reads=(), writes=()):
            r2 = [t for t in reads if t not in DRAM_TOKS] + [t for t in writes if t in DRAM_TOKS]
            w2 = [t for t in writes if t not in DRAM_TOKS] + [t for t in reads if t in DRAM_TOKS]
            sch.add(q, lambda e: e.dma_start(out=out, in_=in_), r2, w2, ndma=1)

        def mm(out, lhsT, rhs, start, stop, reads, writes, sgc=False):
            if sgc:
                sch.add("pe", lambda e: e.matmul(out, lhsT=lhsT, rhs=rhs, start=start, stop=stop, skip_group_check=True),
                        reads, writes)
            else:
                sch.add("pe", lambda e: e.matmul(out, lhsT=lhsT, rhs=rhs, start=start, stop=stop), reads, writes)

        def tr(out, in_, idn, reads, writes):
            sch.add("pe", lambda e: e.transpose(out=out, in_=in_, identity=idn), reads, writes)

        def act(out, in_, func, reads, writes, bias=None, scale=None, accum=None):
            kw = {}
            if bias is not None:
                kw["bias"] = bias
            if scale is not None:
                kw["scale"] = scale
            if accum is not None:
                kw["accum_out"] = accum
            sch.add("act", lambda e: e.activation(out=out, in_=in_, func=func, **kw), reads, writes)

        def tt(eng, out, in0, in1, op, reads, writes):
            sch.add(eng, lambda e: e.tensor_tensor(out=out, in0=in0, in1=in1, op=op), reads, writes)

        def ts(eng, out, in0, s1, op0, reads, writes, s2=None, op1=None):
            if op1 is None:
                sch.add(eng, lambda e: e.tensor_scalar(out=out, in0=in0, scalar1=s1, scalar2=None, op0=op0), reads, writes)
            else:
                sch.add(eng, lambda e: e.tensor_scalar(out=out, in0=in0, scalar1=s1, scalar2=s2, op0=op0, op1=op1),
                        reads, writes)

        def stt(eng, out, in0, scalar, in1, op0, op1, reads, writes):
            sch.add(eng, lambda e: e.scalar_tensor_tensor(out=out, in0=in0, scalar=scalar, in1=in1, op0=op0, op1=op1),
                    reads, writes)

        def cp(eng, out, in_, reads, writes):
            if eng == "act":
                act(out, in_, AF.Copy, reads, writes)
            else:
                sch.add(eng, lambda e: e.tensor_copy(out=out, in_=in_), reads, writes)

        def recip(out, in_, reads, writes):
            sch.add("dve", lambda e: e.reciprocal(out=out, in_=in_), reads, writes)

        def memset(eng, ap, val, writes):
            sch.add(eng, lambda e: e.memset(ap, val), (), writes)

        rr = {"w32": 0, "wbf": 0, "psA": 0, "ev": 0, "psX3": 0}

        def nxt(key, n):
            v = rr[key]
            rr[key] = (v + 1) % n
            return v

        def w32():
            i = nxt("w32", NW)
            return wk32[i], "wk32_%d" % i

        def wbf():
            i = nxt("wbf", NWB)
            return wkbf[i], "wkbf_%d" % i

        def evq():
            return ("act", "dve")[nxt("ev", 2)]

        def rstd_col(col_ap, tok, n_inv):
            act(col_ap, col_ap, AF.Ln, [tok], [tok], bias=EPS, scale=n_inv)
            act(col_ap, col_ap, AF.Exp, [tok], [tok], scale=-0.5)

        dma("sp", ident[:], cst["c_ident"], (), ["ident"])
        dma("sp", uincl[:], cst["c_uincl"], (), ["uincl"])
        dma("sp", nlower[:], cst["c_nlower"], (), ["nlower"])
        dma("sp", c_tail[:], cst["c_tail"], (), ["c_tail"])
        dma("sp", c_adec[:], cst["c_adec"], (), ["c_adec"])
        dma("sp", c_rope[:], cst["c_rope"], (), ["c_rope"])
        memset("pool", onesb[:], 1.0, ["onesb"])
        memset("pool", onesf[:], 1.0, ["onesf"])
        memset("pool", ones64[:], 1.0 / 64.0, ["ones64"])
        bd64 = sb("bd64", [128, 128], F32)
        memset("pool", bd64[:], 1.0 / 64.0, ["bd64"])
        memset("pool", bd64[0:64, 64:128], 0.0, ["bd64"])
        memset("pool", bd64[64:128, 0:64], 0.0, ["bd64"])
        memset("pool", small[:], 0.0, ["small63"])
        cp("dve", identf[:], ident[:], ["ident"], ["identf"])
        carve.reset()
        ob = carve.take([128, S], BF16)
        sch.add("pool", lambda e: e.memset(ob, 1.0), AR, ["ob"])
        for h in range(4):
            dma("sp", KF[h, 64:67, :], ob[0:3, :], ["ob"] + AR, ["KF1_%d" % h])
        posi = carve.take([128, S], I32)
        posf = carve.take([128, S], F32)
        ang = carve.take([128, S], F32)
        kf_ = carve.take([128, S], F32)
        ki_ = carve.take([128, S], I32)
        dma("sp", posi, pos.partition_broadcast(128), AR, ["posi"])
        cp("dve", posf, posi, ["posi"] + AR, ["posf"])
        TWO_PI = 2.0 * math.pi
        C1 = 6.28125
        C2 = TWO_PI - C1
        for ti, (COS, SIN) in enumerate(((COSR, SINR), (COSM, SINM))):
            ts("dve", ang, posf, c_rope[:, ti:ti + 1], ALU.mult, ["posf", "c_rope"] + AR, ["ang"])
            for which in (0, 1):
                ts("dve", ki_, ang, 1.0 / TWO_PI, ALU.mult, ["ang"] + AR, ["ki"])
                cp("dve", kf_, ki_, ["ki"] + AR, ["kf"])
                stt("dve", posi.bitcast(F32), kf_, -C1, ang, ALU.mult, ALU.add, ["kf", "ang"] + AR, ["red"])
                red = posi.bitcast(F32)
                stt("dve", red, kf_, -C2, red, ALU.mult, ALU.add, ["kf", "red"] + AR, ["red"])
                if which == 1:
                    ts("dve", red, red, math.pi / 2.0, ALU.add, ["red"] + AR, ["red"])
                    ts("dve", kf_, red, math.pi, ALU.is_gt, ["red"] + AR, ["kf"])
                    stt("dve", red, kf_, -TWO_PI, red, ALU.mult, ALU.add, ["kf", "red"] + AR, ["red"])
                ts("dve", red, red, 3.141592, ALU.min, ["red"] + AR, ["red"], s2=-3.141592, op1=ALU.max)
                act(kf_, red, AF.Sin, ["red"] + AR, ["kf"])
                if which == 0:
                    ts("dve", kf_, kf_, c_rope[:, 2 + ti:3 + ti], ALU.mult, ["kf", "c_rope"] + AR, ["kf"])
                    dma("sp", SIN, kf_, ["kf"] + AR, ["SIN%d" % ti])
                else:
                    dma("sp", COS, kf_, ["kf"] + AR, ["COS%d" % ti])
        phase_barrier()

        for L in range(nlayers):
            if dbg == "setup":
                break
            x_src = x_in if L == 0 else X2
            x_dst = y_out if L == nlayers - 1 else X2
            dma("sp", gqc[:], gq_col[L], (), ["gqc"])
            dma("sp", gkvc[:], gkv_col[L], (), ["gkvc"])
            dma("sp", gmoc[:], gmo_col[L], (), ["gmoc"])
            dma("sp", bfc[:], bf_col[L], (), ["bfc"])

            carve.reset()
            winb = carve.take([128, 8, NINP], BF16)
            wq32 = carve.take([128, 2, 384], F32)
            wqn = carve.take([128, 2, 256], BF16)
            wqr = carve.take([128, 2, 128], BF16)
            wqp = carve.take([128, 2, 128], BF16)
            wkv32 = carve.take([128, 512], F32)
            wkn = carve.take([128, 256], BF16)
            wvv = carve.take([128, 256], BF16)
            hT = [carve.take([128, 8, 512], BF16) for _ in range(2)]
            tabs = [[carve.take([128, 512], F32) for _ in range(4)] for _ in range(2)]
            xt4 = [carve.take([128, 1024], F32) for _ in range(4)]
            cq32 = [carve.take([128, 512], F32) for _ in range(2)]
            sq32 = [carve.take([128, 512], F32) for _ in range(3)]
            cqn = [carve.take([128, 512], BF16) for _ in range(2)]
            ckvn = carve.take([128, 512], BF16)
            rstdb = carve.take([128, 512], F32)
            vstg = [carve.take([128, 4, 256], BF16) for _ in range(3)]
            krtstg = carve.take([128, 4, 256], BF16)
            c_qdummy = None

            for kc in range(8):
                rows = slice(kc * 128, (kc + 1) * 128)
                dma("pool", winb[:, kc, 0:NIN], w_in[L, rows, :], AR, ["winb%d" % kc])
                for nm, nmp, nh, hw in (("rq", "rqp", 4, 32), ("rk", "rkp", 4, 32), ("kr", "krp", 1, 16)):
                    src = winb[:, kc, OFF[nm]:OFF[nm] + nh * 2 * hw].rearrange("p (h t c) -> p h t c", h=nh, t=2)
                    dst = winb[:, kc, OFF[nmp]:OFF[nmp] + nh * 2 * hw].rearrange("p (h t c) -> p h t c", h=nh, t=2)
                    cp("pool", dst[:, :, 0, :], src[:, :, 1, :], ["winb%d" % kc] + AR, ["winbp%d" % kc])
                    cp("pool", dst[:, :, 1, :], src[:, :, 0, :], ["winb%d" % kc] + AR, ["winbp%d" % kc])
            if dbg == "p1a":
                break
            dma("sp", wq32, w_q_up[L].rearrange("(c p) n -> p c n", p=128), AR, ["wq32"])
            dma("sp", wkv32, w_kv_up[L], AR, ["wkv32"])
            for c in range(2):
                src4 = wq32[:, c, :].rearrange("p (h e) -> p h e", h=4)
                gcol = gqc[:, c:c + 1]
                ts("dve", wqn[:, c, :].rearrange("p (h e) -> p h e", h=4), src4[:, :, 0:64], gcol, ALU.mult,
                   ["wq32", "gqc"] + AR, ["wqb"])
                ts("dve", wqr[:, c, :].rearrange("p (h e) -> p h e", h=4), src4[:, :, 64:96], gcol, ALU.mult,
                   ["wq32", "gqc"] + AR, ["wqb"])
                dstp = wqp[:, c, :].rearrange("p (h t e) -> p h t e", h=4, t=2)
                ts("dve", dstp[:, :, 0, :], src4[:, :, 80:96], gcol, ALU.mult, ["wq32", "gqc"] + AR, ["wqb"])
                ts("dve", dstp[:, :, 1, :], src4[:, :, 64:80], gcol, ALU.mult, ["wq32", "gqc"] + AR, ["wqb"])
            kv4 = wkv32.rearrange("p (h e) -> p h e", h=4)
            ts("dve", wkn.rearrange("p (h e) -> p h e", h=4), kv4[:, :, 0:64], gkvc[:, 0:1], ALU.mult,
               ["wkv32", "gkvc"] + AR, ["wkvb"])
            ts("dve", wvv.rearrange("p (h e) -> p h e", h=4), kv4[:, :, 64:128], gkvc[:, 0:1], ALU.mult,
               ["wkv32", "gkvc"] + AR, ["wkvb"])
            dma("sp", gb[0][:], g_pre[L:L + 1, :].partition_broadcast(128), (), ["gb0"])

            WIN_ALL = ["winb%d" % k_ for k_ in range(8)] + ["winbp%d" % k_ for k_ in range(8)]

            def fm_group(hTt, hTtok, col_lo, M, ncols_stride=None):
                b = nxt("psA", 4)
                ps = PS[b]
                for kc in range(8):
                    mm(ps[0:M, :], winb[:, kc, col_lo:col_lo + M], hTt[:, kc, :], kc == 0, kc == 7,
                       ["winb%d" % kc, "winbp%d" % kc, hTtok] + AR, ["ps%d" % b])
                return ps, "ps%d" % b

            def store_rows(stg, stok, dsts, tsl):
                for (r0, r1, dram_rows, wtok) in dsts:
                    dma("sp", dram_rows[:, tsl], stg[r0:r1, :], [stok], [wtok])

            def load_tile_inputs(t):
                tsl_ = slice(t * 512, (t + 1) * 512)
                p_ = t % 2
                for sub in range(4):
                    r0 = t * 512 + sub * 128
                    dma("pool", xt4[sub], x_src[r0:r0 + 128, :], ["X2"] + AR, ["xt4_%d" % sub])
                for j, (SRC, tk) in enumerate(((COSR, "COS0"), (SINR, "SIN0"), (COSM, "COS1"), (SINM, "SIN1"))):
                    dma("pool", tabs[p_][j], SRC[:, tsl_], [tk] + AR, ["tab%d_%d" % (p_, j)])

            def norm_tile(t):
                hTt = hT[t % 2]
                hTtok = "hT%d" % (t % 2)
                p_ = t % 2
                for sub in range(4):
                    xi = sub % 2
                    xs = xt4[sub]
                    xtok = "xt4_%d" % sub
                    col = small[:, sub:sub + 1]
                    ctok = "sm%d" % sub
                    act(hbf[xi][:], xs, AF.Square, [xtok] + AR, ["hbf%d" % xi, ctok], accum=col)
                    rstd_col(col, ctok, 1.0 / D)
                    stt("dve", hbf[xi][:], xs, col, gb[0][:], ALU.mult, ALU.mult,
                        [xtok, ctok, "gb0"] + AR, ["hbf%d" % xi])
                    psT = PS[4][:].bitcast(BF16)
                    for kc in range(8):
                        tr(psT[:, kc * 128:(kc + 1) * 128], hbf[xi][:, kc * 128:(kc + 1) * 128], ident[:],
                           ["hbf%d" % xi, "ident"], ["ps4"])
                    cp("act" if sub % 2 else "dve", hTt[:, :, sub * 128:(sub + 1) * 128],
                       psT.rearrange("p (k c) -> p k c", k=8), ["ps4"] + AR, [hTtok])

            load_tile_inputs(0)
            norm_tile(0)
            for t in range(NT):
                tsl = slice(t * 512, (t + 1) * 512)
                hTt = hT[t % 2]
                hTtok = "hT%d" % (t % 2)
                if t + 1 < NT:
                    load_tile_inputs(t + 1)
                cr, sr, cm, sm = tabs[t % 2]
                crk, srk, cmk, smk = ["tab%d_%d" % (t % 2, j) for j in range(4)]
                for c in range(2):
                    ps, ptok = fm_group(hTt, hTtok, OFF["cq"] + c * 128, 128)
                    act(sq32[c], ps[:], AF.Square, [ptok] + AR, ["sq32_%d" % c])
                    cp("dve", cq32[c], ps[:], [ptok] + AR, ["cq32_%d" % c])
                for c in range(2):
                    mm(PS[6][:], onesf[:], sq32[c], c == 0, c == 1, ["onesf", "sq32_%d" % c] + AR, ["ps6"])
                act(rstdb, PS[6][:], AF.Ln, ["ps6"] + AR, ["rstdb"], bias=EPS, scale=1.0 / 256.0)
                act(rstdb, rstdb, AF.Exp, ["rstdb"] + AR, ["rstdb"], scale=-0.5)
                for c in range(2):
                    tt("pool" if c else "dve", cqn[c], cq32[c], rstdb, ALU.mult, ["cq32_%d" % c, "rstdb"] + AR, ["cqn%d" % c])
                ps, ptok = fm_group(hTt, hTtok, OFF["ckv"], 128)
                act(sq32[2], ps[:], AF.Square, [ptok] + AR, ["sq32_2"])
                cp("dve", cq32[0], ps[:], [ptok] + AR, ["cq32_0"])
                mm(PS[6][:], onesf[:], sq32[2], True, True, ["onesf", "sq32_2"] + AR, ["ps6"])
                act(rstdb, PS[6][:], AF.Ln, ["ps6"] + AR, ["rstdb"], bias=EPS, scale=1.0 / 128.0)
                act(rstdb, rstdb, AF.Exp, ["rstdb"] + AR, ["rstdb"], scale=-0.5)
                tt("dve", ckvn, cq32[0], rstdb, ALU.mult, ["cq32_0", "rstdb"] + AR, ["ckvn"])
                def simple_pair(col_lo, scale, DST, wtok):
                    for hp in range(2):
                        ps, ptok = fm_group(hTt, hTtok, col_lo + hp * 128, 128)
                        stg, stok = wbf()
                        if scale is None:
                            cp(evq(), stg[:], ps[:], [ptok], [stok])
                        else:
                            act(stg[:], ps[:], AF.Copy, [ptok], [stok], scale=scale)
                        store_rows(stg, stok, [(0, 64, DST[2 * hp, 0:64], wtok), (64, 128, DST[2 * hp + 1, 0:64], wtok)], tsl)

                simple_pair(OFF["fq"], 0.125, QF, "QF")
                simple_pair(OFF["fk"], None, KF, "KF")
                simple_pair(OFF["sq"], 0.125, QS, "QS")
                simple_pair(OFF["sk"], None, KS, "KS")
                ps, ptok = fm_group(hTt, hTtok, OFF["ff"], 4)
                stg, stok = w32()
                cp("dve", stg[0:4, :], ps[0:4, :], [ptok], [stok])
                dma("sp", FFL[:, tsl], stg[0:4, :], [stok], ["FFL"])
                for c in range(2):
                    ps, ptok = fm_group(hTt, hTtok, OFF["rg"] + c * 128, 128)
                    stg, stok = w32()
                    act(stg[:], ps[:], AF.Silu, [ptok], [stok])
                    dma("sp", RG[c * 128:(c + 1) * 128, tsl], stg[:], [stok], ["RG"])
                for nm, nmp, DST, wtok, scl in (("rq", "rqp", QR, "QR", 1.0), ("rk", "rkp", KR, "KR", 0.125)):
                    for hp in range(2):
                        psa, pta = fm_group(hTt, hTtok, OFF[nm] + hp * 128, 128)
                        psb, ptb = fm_group(hTt, hTtok, OFF[nmp] + hp * 128, 128)
                        t1, k1 = w32()
                        t2, k2 = w32()
                        stt("dve", t1[:], psa[:], scl, cr, ALU.mult, ALU.mult, [pta, crk] + AR, [k1])
                        stt("dve", t2[:], psb[:], scl, sr, ALU.mult, ALU.mult, [ptb, srk] + AR, [k2])
                        stg, stok = wbf()
                        tt("pool", stg[:], t1[:], t2[:], ALU.add, [k1, k2], [stok])
                        store_rows(stg, stok, [(0, 64, DST[2 * hp], wtok), (64, 128, DST[2 * hp + 1], wtok)], tsl)
                        if nm == "rk":
                            psT = PS[5][:].bitcast(BF16)
                            for sub in range(4):
                                tr(psT[:, sub * 128:(sub + 1) * 128], stg[:, sub * 128:(sub + 1) * 128], ident[:],
                                   [stok, "ident"], ["ps5"])
                            for hh in range(2):
                                h = 2 * hp + hh
                                ts("dve", krtstg[:, :, h * 64:(h + 1) * 64],
                                   psT[:, 0:512].rearrange("p (s a d) -> p s a d", s=4, a=2)[:, :, hh, :],
                                   c_tail[:, h:h + 1], ALU.mult, ["ps5", "c_tail"] + AR, ["krtstg"])
                            if hp == 1:
                                dma("sp", KRT[tsl, :, :].rearrange("(s p) h d -> p s (h d)", p=128), krtstg,
                                    ["krtstg"] + AR, ["KRT"])
                if t + 1 < NT:
                    norm_tile(t + 1)
                qsc = 96.0 ** -0.5
                for hp in range(2):
                    b = nxt("psA", 4)
                    for c in range(2):
                        mm(PS[b][:], wqn[:, c, hp * 128:(hp + 1) * 128], cqn[c], c == 0, c == 1,
                           ["wqb", "cqn%d" % c] + AR, ["ps%d" % b])
                    stg, stok = wbf()
                    act(stg[:], PS[b][:], AF.Copy, ["ps%d" % b], [stok], scale=qsc)
                    store_rows(stg, stok, [(0, 64, QM[2 * hp, 0:64], "QM"), (64, 128, QM[2 * hp + 1, 0:64], "QM")], tsl)
                ba = nxt("psA", 4)
                for c in range(2):
                    mm(PS[ba][:], wqr[:, c, :], cqn[c], c == 0, c == 1, ["wqb", "cqn%d" % c] + AR, ["ps%d" % ba])
                bb = nxt("psA", 4)
                for c in range(2):
                    mm(PS[bb][:], wqp[:, c, :], cqn[c], c == 0, c == 1, ["wqb", "cqn%d" % c] + AR, ["ps%d" % bb])
                t1, k1 = w32()
                t2, k2 = w32()
                stt("dve", t1[:], PS[ba][:], qsc, cm, ALU.mult, ALU.mult, ["ps%d" % ba, cmk] + AR, [k1])
                stt("dve", t2[:], PS[bb][:], qsc, sm, ALU.mult, ALU.mult, ["ps%d" % bb, smk] + AR, [k2])
                stg, stok = wbf()
                tt("pool", stg[:], t1[:], t2[:], ALU.add, [k1, k2], [stok])
                store_rows(stg, stok, [(32 * h, 32 * h + 32, QM[h, 64:96], "QM") for h in range(4)], tsl)
                for hp in range(2):
                    b = nxt("psA", 4)
                    mm(PS[b][:], wkn[:, hp * 128:(hp + 1) * 128], ckvn, True, True, ["wkvb", "ckvn"] + AR, ["ps%d" % b])
                    stg, stok = wbf()
                    cp(evq(), stg[:], PS[b][:], ["ps%d" % b], [stok])
                    store_rows(stg, stok, [(0, 64, KM[2 * hp, 0:64], "KM"), (64, 128, KM[2 * hp + 1, 0:64], "KM")], tsl)
                psa, pta = fm_group(hTt, hTtok, OFF["kr"], 32)
                psb, ptb = fm_group(hTt, hTtok, OFF["krp"], 32)
                t1, k1 = w32()
                t2, k2 = w32()
                tt("dve", t1[0:32, :], psa[0:32, :], cm[0:32, :], ALU.mult, [pta, cmk] + AR, [k1])
                tt("dve", t2[0:32, :], psb[0:32, :], sm[0:32, :], ALU.mult, [ptb, smk] + AR, [k2])
                stg, stok = wbf()
                tt("pool", stg[0:32, :], t1[0:32, :], t2[0:32, :], ALU.add, [k1, k2], [stok])
                store_rows(stg, stok, [(0, 32, KM[h, 64:96], "KM") for h in range(4)], tsl)
                for vi, (nm, DSTv) in enumerate((("fv", None), ("rv", None), ("sv", None), ("mla", None))):
                    vs = vstg[vi % 3]
                    vtok = "vstg%d" % (vi % 3)
                    for sub in range(4):
                        b = 6 + (sub % 2)
                        if nm == "mla":
                            mm(PS[b][:, 0:256], ckvn[:, sub * 128:(sub + 1) * 128], wvv, True, True,
                               ["ckvn", "wkvb"] + AR, ["ps%d" % b])
                        else:
                            for kc in range(8):
                                mm(PS[b][:, 0:256], hTt[:, kc, sub * 128:(sub + 1) * 128],
                                   winb[:, kc, OFF[nm]:OFF[nm] + 256], kc == 0, kc == 7,
                                   ["winb%d" % kc, hTtok] + AR, ["ps%d" % b])
                        cp(evq(), vs[:, sub, :], PS[b][:, 0:256], ["ps%d" % b] + AR, [vtok])
                    V_ = {"fv": VF, "mla": VM, "rv": VR, "sv": VS}[nm]
                    dma("sp", V_[tsl, :].rearrange("(s p) n -> p s n", p=128), vs, [vtok] + AR, [nm + "_dst"])
            phase_barrier()

            if dbg == "p1":
                break
            carve.reset()
            fl = carve.take([4, S], F32)
            ones4 = carve.take([4, S], F32)
            cc = carve.take([4, S], F32)
            r1 = carve.take([4, S], F32)
            chi = carve.take([4, S], BF16)
            cmid = carve.take([4, S], BF16)
            clo = carve.take([4, S], BF16)
            cneg = carve.take([128, NB, 4], F32)
            dma("sp", fl, FFL, ["FFL"] + AR, ["fl"])
            sch.add("pool", lambda e: e.memset(ones4, 1.0), AR, ["ones4"])
            act(fl, fl, AF.Identity, ["fl", "bfc"] + AR, ["fl"], bias=bfc[:, 0:1])
            act(fl, fl, AF.Exp, ["fl"] + AR, ["fl"], scale=-1.0)
            act(fl, fl, AF.Ln, ["fl"] + AR, ["fl"], bias=1.0)
            ts("dve", fl, fl, -1.0, ALU.mult, ["fl"] + AR, ["fl"])
            sch.add("dve", lambda e: e.tensor_tensor_scan(out=cc, data0=ones4, data1=fl, initial=0.0,
                                                          op0=ALU.mult, op1=ALU.add), ["fl", "ones4"] + AR, ["cc"])
            cp("dve", chi, cc, ["cc"] + AR, ["chi"])
            tt("dve", r1, cc, chi, ALU.subtract, ["cc", "chi"] + AR, ["r1"])
            cp("dve", cmid, r1, ["r1"] + AR, ["cmid"])
            tt("dve", r1, r1, cmid, ALU.subtract, ["r1", "cmid"] + AR, ["r1"])
            cp("dve", clo, r1, ["r1"] + AR, ["clo"])
            for h in range(4):
                for i, (src, tk) in enumerate(((chi, "chi"), (cmid, "cmid"), (clo, "clo"))):
                    dma("sp", QF[h, 64 + i:65 + i, :], src[h:h + 1, :], [tk] + AR, ["QFc"])
            for tb in range(NB):
                tr(PS[0][:, tb * 4:(tb + 1) * 4], cc[:, tb * 128:(tb + 1) * 128], identf[0:4, 0:4],
                   ["cc", "identf"] + AR, ["ps0"])
            ts("dve", cneg, PS[0][:, 0:NB * 4].rearrange("p (t h) -> p t h", h=4), -1.0, ALU.mult, ["ps0"] + AR, ["cneg"])
            dma("sp", CNEG.rearrange("(t p) h -> p t h", p=128), cneg, ["cneg"] + AR, ["CNEG"])
            phase_barrier()

            def softmax_attention(g, Qd, Kd, Vd, KD, mask_kind, use_bias):
                carve.reset()
                qT = [carve.take([128, S], BF16) for _ in range(2)]
                kT = [carve.take([128, S], BF16) for _ in range(2)]
                vA = [carve.take([128, NB, 128], BF16) for _ in range(2)]
                cng = carve.take([128, NB, 4], F32)
                msk = carve.take([128, 4, 512], BF16)
                dma("sp", msk, cst["c_masks"][:, mask_kind], AR, ["masks"])
                if use_bias:
                    dma("sp", cng, CNEG.rearrange("(t p) h -> p t h", p=128), ["CNEG"] + AR, ["cng"])
                LA = 3

                def load_head(h):
                    i2 = h % 2
                    dma("sp", qT[i2][0:KD, :], Qd[h], ["QF", "QFc", "QM"] + AR, ["qT%d" % i2])
                    dma("sp", kT[i2][0:KD, :], Kd[h], ["KF", "KM"] + ["KF1_%d" % j for j in range(4)] + AR, ["kT%d" % i2])
                    dma("sp", vA[i2][:, :, 0:64], Vd.rearrange("(t p) (h d) -> p t h d", p=128, h=4)[:, :, h, :],
                        ["fv_dst", "mla_dst"] + AR, ["vA%d" % i2])

                for i2_ in range(2):
                    sch.add("pool", (lambda t_: (lambda e: e.memset(t_[:, :, 64:128], 1.0)))(vA[i2_]), AR, ["vAones%d" % i2_])
                load_head(0)
                for h in range(4):
                    i2 = h % 2
                    if h + 1 < 4:
                        load_head(h + 1)
                    blocks = [(T, kb) for T in range(NT) for kb in range(4 * T + 4)]
                    st_ = {}

                    def stage_a(i):
                        T, kb = blocks[i]
                        qsl = slice(T * 512, (T + 1) * 512)
                        xb = nxt("psA", 4)
                        mm(PS[xb][:], kT[i2][0:KD, kb * 128:(kb + 1) * 128], qT[i2][0:KD, qsl], True, True,
                           ["kT%d" % i2, "qT%d" % i2] + AR, ["ps%d" % xb])
                        p_, ptok = wbf()
                        if use_bias:
                            act(p_[:], PS[xb][:], AF.Exp, ["ps%d" % xb, "cng"] + AR, [ptok], bias=cng[:, kb, h:h + 1])
                        else:
                            act(p_[:], PS[xb][:], AF.Exp, ["ps%d" % xb], [ptok])
                        if kb >= 4 * T:
                            tt("pool", p_[:], p_[:], msk[:, kb - 4 * T, :], ALU.mult, [ptok, "masks"] + AR, [ptok])
                        st_[i] = (p_, ptok)

                    def stage_b(i):
                        T, kb = blocks[i]
                        qsl = slice(T * 512, (T + 1) * 512)
                        nkb = 4 * T + 4
                        ob_ = 4 + (T % 2)
                        p_, ptok = st_.pop(i)
                        mm(PS[ob_][:], vA[i2][:, kb, :], p_[:], kb == 0, kb == nkb - 1,
                           ["vA%d" % i2, "vAones%d" % i2, ptok] + AR, ["ps%d" % ob_])
                        if kb == nkb - 1:
                            den, dtok = w32()
                            act(den[0:64, :], PS[ob_][64:128, :], AF.Copy, ["ps%d" % ob_], [dtok])
                            recip(den[0:64, :], den[0:64, :], [dtok], [dtok])
                            o_, otok = w32()
                            tt("dve", o_[0:64, :], PS[ob_][0:64, :], den[0:64, :], ALU.mult, ["ps%d" % ob_, dtok], [otok])
                            dma("sp", OH[g, h, :, qsl], o_[0:64, :], [otok], ["OH%d" % g])

                    n_ = len(blocks)
                    for i in range(n_ + LA):
                        if i < n_:
                            stage_a(i)
                        if i >= LA:
                            stage_b(i - LA)
                phase_barrier()

            softmax_attention(0, QF, KF, VF, 67, 0, True)
            softmax_attention(1, QM, KM, VM, 96, 1, False)

            carve.reset()
            qT = [carve.take([128, S], BF16) for _ in range(4)]
            kT = [carve.take([128, S], BF16) for _ in range(4)]
            nkT = [carve.take([128, S], BF16) for _ in range(4)]
            vS_ = carve.take([128, NB, 256], BF16)
            msk = carve.take([128, 4, 512], BF16)
            dma("sp", msk, cst["c_masks"][:, 2], AR, ["masks"])
            dma("sp", vS_, VS.rearrange("(t p) n -> p t n", p=128), ["sv_dst"] + AR, ["vS"])
            LA = 2
            NHL = 2 * 2 * (LA + 2)
            hl_pool = [carve.take([128, 512], BF16) for _ in range(NHL)]
            w_pool = [carve.take([128, 512], BF16) for _ in range(6)]
            rr["hl"] = 0
            rr["wp"] = 0

            def hl_tile():
                i_ = nxt("hl", NHL)
                return hl_pool[i_], "hl%d" % i_

            def w_tile():
                i_ = nxt("wp", 6)
                return w_pool[i_], "wp%d" % i_
            ACCB = (3, 6)
            OB = (4, 5)

            def load_head_sb(h):
                dma("sp", qT[h][0:64, :], QS[h], ["QS"] + AR, ["qT%d" % h])
                dma("sp", kT[h][0:64, :], KS[h], ["KS"] + AR, ["kT%d" % h])

            def prep_head_sb(h):
                e1, e2 = ("pool", "dve") if h % 2 == 0 else ("dve", "pool")
                sch.add(e1, (lambda t_: (lambda e: e.memset(t_[64:128, :], 0.0)))(qT[h]), AR, ["qTpad%d" % h])
                sch.add(e2, (lambda t_: (lambda e: e.memset(t_[64:128, :], 0.0)))(kT[h]), AR, ["kTpad%d" % h])
                sch.add(e1, (lambda t_: (lambda e: e.memset(t_[64:128, :], 0.0)))(nkT[h]), AR, ["nkTpad%d" % h])
                load_head_sb(h)
                ts("dve" if h % 2 == 0 else "pool", nkT[h][0:64, :], kT[h][0:64, :], -1.0, ALU.mult,
                   ["kT%d" % h] + AR, ["nkT%d" % h])

            prep_head_sb(0)
            prep_head_sb(1)
            for hp in range(2):
                heads = (2 * hp, 2 * hp + 1)
                blocks = [(T, kb) for T in range(NT) for kb in range(4 * T + 3, -1, -1)]
                st_ = {}

                XB = (0, 1, 2, 7)

                def stage_z(i, c):
                    h = heads[c]
                    T, kb = blocks[i]
                    qsl = slice(T * 512, (T + 1) * 512)
                    xb = XB[(2 * i + c) % 4]
                    mm(PS[xb][:], kT[h][:, kb * 128:(kb + 1) * 128], qT[h][:, qsl], True, True,
                       ["kT%d" % h, "qT%d" % h, "kTpad%d" % h, "qTpad%d" % h] + AR, ["ps%d" % xb])

                def stage_a(i, c):
                    h = heads[c]
                    T, kb = blocks[i]
                    diag = kb >= 4 * T
                    xb = XB[(2 * i + c) % 4]
                    e_, etok = w32()
                    act(e_[:], PS[xb][:], AF.Exp, ["ps%d" % xb], [etok])
                    act(e_[:], e_[:], AF.Ln, [etok], [etok], bias=1.0)
                    if diag:
                        tt("pool", e_[:], e_[:], msk[:, kb - 4 * T, :], ALU.mult, [etok, "masks"] + AR, [etok])
                    hi, hitok = hl_tile()
                    lo, lotok = hl_tile()
                    cp("dve", hi[:], e_[:], [etok] + AR, [hitok])
                    tt("dve", lo[:], e_[:], hi[:], ALU.subtract, [etok, hitok] + AR, [lotok])
                    st_[(i, c)] = [hi, hitok, lo, lotok]

                def stage_b1(i, c):
                    h = heads[c]
                    T, kb = blocks[i]
                    qsl = slice(T * 512, (T + 1) * 512)
                    nkb = 4 * T + 4
                    diag = kb >= 4 * T
                    first = kb == nkb - 1
                    hi, hitok, lo, lotok = st_[(i, c)]
                    acc = PS[ACCB[c]]
                    atok = "ps%d" % ACCB[c]
                    mm(acc[:], uincl[:], hi[:], first, False, ["uincl", hitok] + AR, [atok], sgc=True)
                    mm(acc[:], uincl[:], lo[:], False, False, ["uincl", lotok] + AR, [atok], sgc=True)
                    mm(acc[:], kT[h][:, kb * 128:(kb + 1) * 128], qT[h][:, qsl], False, True,
                       ["kT%d" % h, "qT%d" % h, "kTpad%d" % h, "qTpad%d" % h] + AR, [atok], sgc=True)
                    w_, wtok = w_tile()
                    act(w_[:], acc[:], AF.Exp, [atok] + AR, [wtok])
                    if diag:
                        tt("pool", w_[:], w_[:], msk[:, kb - 4 * T, :], ALU.mult, [wtok, "masks"] + AR, [wtok])
                    st_[(i, c)] += [w_, wtok]

                def stage_b2(i, c):
                    h = heads[c]
                    T, kb = blocks[i]
                    qsl = slice(T * 512, (T + 1) * 512)
                    nkb = 4 * T + 4
                    first = kb == nkb - 1
                    last = kb == 0
                    hi, hitok, lo, lotok, w_, wtok = st_.pop((i, c))
                    acc = PS[ACCB[c]]
                    atok = "ps%d" % ACCB[c]
                    if not last:
                        mm(acc[:], nkT[h][:, kb * 128:(kb + 1) * 128], qT[h][:, qsl], False, False,
                           ["nkT%d" % h, "nkTpad%d" % h, "qT%d" % h, "qTpad%d" % h] + AR, [atok], sgc=True)
                        mm(acc[:], nlower[:], hi[:], False, False, ["nlower", hitok] + AR, [atok], sgc=True)
                        mm(acc[:], nlower[:], lo[:], False, True, ["nlower", lotok] + AR, [atok], sgc=True)
                    ob_ = OB[c]
                    mm(PS[ob_][:], vS_[:, kb, hp * 128:(hp + 1) * 128], w_[:], first, last,
                       ["vS", wtok] + AR, ["ps%d" % ob_])
                    if last:
                        o_, otok = w32()
                        rs_ = slice(c * 64, c * 64 + 64)
                        cp("act", o_[rs_, :], PS[ob_][rs_, :], ["ps%d" % ob_], [otok])
                        dma("sp", OH[3, h, :, qsl], o_[rs_, :], [otok], ["OH3"])

                n_ = len(blocks)
                stage_z(0, 0)
                stage_z(0, 1)
                for i in range(n_ + LA):
                    if hp == 0 and i == min(6, n_ - 1):
                        prep_head_sb(2)
                        prep_head_sb(3)
                    if i >= LA:
                        stage_b1(i - LA, 0)
                        stage_b1(i - LA, 1)
                    if i + 1 < n_:
                        stage_z(i + 1, 0)
                        stage_z(i + 1, 1)
                    if i < n_:
                        stage_a(i, 0)
                        stage_a(i, 1)
                    if i >= LA:
                        stage_b2(i - LA, 0)
                        stage_b2(i - LA, 1)
            phase_barrier()

            carve.reset()
            qR = [carve.take([128, S], BF16) for _ in range(2)]
            kR = [carve.take([128, S], BF16) for _ in range(2)]
            kRT = carve.take([128, NB, 256], BF16)
            vR = carve.take([128, NB, 256], BF16)
            rdec = carve.take([128, 4, 512], F32)
            qdec = carve.take([128, 4, 512], F32)
            KVs = carve.take([128, 2, NCH, 64], F32)
            SPb = carve.take([128, 2, NCH, 64], BF16)
            dma("sp", kRT, KRT.rearrange("(t p) h d -> p t (h d)", p=128), ["KRT"] + AR, ["kRT"])
            dma("sp", vR, VR.rearrange("(t p) n -> p t n", p=128), ["rv_dst"] + AR, ["vR"])
            dma("sp", rdec, cst["c_rdec"], AR, ["rdec"])
            dma("sp", qdec, cst["c_qdec"], AR, ["qdec"])
            for tb in range(NB):
                for j in range(2):
                    ch = 2 * tb + j
                    rs = slice(j * 64, (j + 1) * 64)
                    for hp in range(2):
                        b = nxt("psA", 4)
                        mm(PS[b][:, 0:128], kRT[rs, tb, hp * 128:(hp + 1) * 128], vR[rs, tb, hp * 128:(hp + 1) * 128],
                           True, True, ["kRT", "vR"] + AR, ["ps%d" % b])
                        cp("dve", KVs[0:64, hp, ch, :], PS[b][0:64, 0:64], ["ps%d" % b] + AR, ["KVs"])
                        cp("act", KVs[64:128, hp, ch, :], PS[b][64:128, 64:128], ["ps%d" % b] + AR, ["KVs"])
            for ch in range(1, NCH):
                for hp in range(2):
                    stt("dve", KVs[:, hp, ch, :], KVs[:, hp, ch - 1, :], c_adec[:, hp:hp + 1], KVs[:, hp, ch, :],
                        ALU.mult, ALU.add, ["KVs", "c_adec"] + AR, ["KVs"])
            cp("dve", SPb, KVs, ["KVs"] + AR, ["SPb"])
            for h in range(4):
                i2 = h % 2
                hp = h // 2
                prs = slice(i2 * 64, i2 * 64 + 64)
                dma("sp", qR[i2][prs, :], QR[h], ["QR"] + AR, ["qR%d" % i2])
                dma("sp", kR[i2][prs, :], KR[h], ["KR"] + AR, ["kR%d" % i2])
                for T in range(NT):
                    qsl = slice(T * 512, (T + 1) * 512)
                    xb = nxt("psA", 4)
                    for s4 in range(4):
                        csl = slice(T * 512 + s4 * 128, T * 512 + (s4 + 1) * 128)
                        mm(PS[xb][:, s4 * 128:(s4 + 1) * 128], kR[i2][prs, csl], qR[i2][prs, csl], s4 == 0, s4 == 3,
                           ["kR%d" % i2, "qR%d" % i2] + AR, ["ps%d" % xb])
                    sm_, smtok = wbf()
                    tt("dve", sm_[:], PS[xb][:], rdec[:, h, :], ALU.mult, ["ps%d" % xb, "rdec"] + AR, [smtok])
                    ob_ = 4 + (T % 2)
                    for s4 in range(4):
                        tb = T * 4 + s4
                        mm(PS[ob_][0:64, s4 * 128:(s4 + 1) * 128], vR[:, tb, h * 64:(h + 1) * 64],
                           sm_[:, s4 * 128:(s4 + 1) * 128], s4 == 0, False, ["vR", smtok] + AR, ["ps%d" % ob_])
                    qd_, qdtok = wbf()
                    tt("pool", qd_[prs, :], qR[i2][prs, qsl], qdec[prs, h, :], ALU.mult,
                       ["qR%d" % i2, "qdec"] + AR, [qdtok])
                    for c8 in range(8):
                        ch = T * 8 + c8
                        if ch == 0:
                            continue
                        mm(PS[ob_][0:64, c8 * 64:(c8 + 1) * 64], SPb[prs, hp, ch - 1, :], qd_[prs, c8 * 64:(c8 + 1) * 64],
                           False, c8 == 7, ["SPb", qdtok] + AR, ["ps%d" % ob_])
                    o_, otok = w32()
                    cp("act", o_[0:64, :], PS[ob_][0:64, :], ["ps%d" % ob_], [otok])
                    dma("sp", OH[2, h, :, qsl], o_[0:64, :], [otok], ["OH2"])
            phase_barrier()

            carve.reset()
            oh = [carve.take([128, 512], F32) for _ in range(4)]
            sqt = [carve.take([128, 512], F32) for _ in range(4)]
            rst = [carve.take([128, 512], F32) for _ in range(2)]
            ohr = [carve.take([128, 512], F32) for _ in range(3)]
            rgt = [carve.take([128, 512], F32) for _ in range(3)]
            dt_ = [carve.take([128, 512], F32) for _ in range(3)]
            st2 = [carve.take([128, 512], F32) for _ in range(3)]
            mo = [carve.take([128, 512], BF16) for _ in range(6)]
            rr["mo"] = 0
            units = []
            for T in range(NT):
                units += [(T, "rms", g) for g in (0, 1, 3)] + [(T, "ret", hp) for hp in range(2)]
            krms = [0]
            kret = [0]
            ust = {}

            def gn_a(k):
                T, kind, idx = units[k]
                qsl = slice(T * 512, (T + 1) * 512)
                if kind == "rms":
                    g = idx
                    p_ = krms[0] % 2
                    krms[0] += 1
                    bank = (0, 3)[p_]
                    src = OH[g].rearrange("h d s -> (h d) s")
                    for hp in range(2):
                        o_ = oh[2 * p_ + hp]
                        otok = "oh%d" % (2 * p_ + hp)
                        dma("sp", o_, src[hp * 128:(hp + 1) * 128, qsl], ["OH%d" % g] + AR, [otok])
                        act(sqt[2 * p_ + hp], o_, AF.Square, [otok] + AR, ["sqt%d" % (2 * p_ + hp)])
                        mm(PS[bank][:], onesf[:], sqt[2 * p_ + hp], hp == 0, hp == 1,
                           ["onesf", "sqt%d" % (2 * p_ + hp)] + AR, ["ps%d" % bank])
                    ust[k] = (p_, bank)
                else:
                    hp = idx
                    p_ = kret[0] % 3
                    q_ = kret[0] % 2
                    kret[0] += 1
                    b1, b2 = ((1, 2), (4, 5))[q_]
                    o_ = ohr[p_]
                    otok = "ohr%d" % p_
                    dma("sp", o_, OH[2].rearrange("h d s -> (h d) s")[hp * 128:(hp + 1) * 128, qsl], ["OH2"] + AR, [otok])
                    dma("sp", rgt[p_], RG[hp * 128:(hp + 1) * 128, qsl], ["RG"] + AR, ["rgt%d" % p_])
                    mm(PS[b1][:], bd64[:], o_, True, True, ["bd64", otok] + AR, ["ps%d" % b1])
                    tt("dve", dt_[p_], o_, PS[b1][:], ALU.subtract, [otok, "ps%d" % b1] + AR, ["dt%d" % p_])
                    act(st2[p_], dt_[p_], AF.Square, ["dt%d" % p_] + AR, ["st2_%d" % p_])
                    mm(PS[b2][:], bd64[:], st2[p_], True, True, ["bd64", "st2_%d" % p_] + AR, ["ps%d" % b2])
                    ust[k] = (p_, b2)

            def gn_b(k):
                T, kind, idx = units[k]
                qsl = slice(T * 512, (T + 1) * 512)
                p_, bank = ust.pop(k)
                if kind == "rms":
                    g = idx
                    rs_ = rst[p_]
                    rtok = "rst%d" % p_
                    act(rs_, PS[bank][:], AF.Ln, ["ps%d" % bank] + AR, [rtok], bias=EPS, scale=1.0 / 256.0)
                    act(rs_, rs_, AF.Exp, [rtok] + AR, [rtok], scale=-0.5)
                    for hp in range(2):
                        o_ = oh[2 * p_ + hp]
                        otok = "oh%d" % (2 * p_ + hp)
                        mi = nxt("mo", 6)
                        stt("dve", mo[mi], o_, gmoc[:, 2 * g + hp:2 * g + hp + 1], rs_, ALU.mult, ALU.mult,
                            [otok, "gmoc", rtok] + AR, ["mo%d" % mi])
                        dma("pool", MIXT[g * 256 + hp * 128:g * 256 + (hp + 1) * 128, qsl], mo[mi], ["mo%d" % mi] + AR, ["MIXT"])
                else:
                    hp = idx
                    s_ = st2[p_]
                    stok = "st2_%d" % p_
                    act(s_, PS[bank][:], AF.Ln, ["ps%d" % bank] + AR, [stok], bias=EPS)
                    act(s_, s_, AF.Exp, [stok] + AR, [stok], scale=-0.5)
                    stt("dve", dt_[p_], dt_[p_], gmoc[:, 4 + hp:5 + hp], s_, ALU.mult, ALU.mult,
                        ["dt%d" % p_, "gmoc", stok] + AR, ["dt%d" % p_])
                    mi = nxt("mo", 6)
                    tt("dve", mo[mi], dt_[p_], rgt[p_], ALU.mult, ["dt%d" % p_, "rgt%d" % p_] + AR, ["mo%d" % mi])
                    dma("pool", MIXT[512 + hp * 128:512 + (hp + 1) * 128, qsl], mo[mi], ["mo%d" % mi] + AR, ["MIXT"])

            for k in range(len(units) + 1):
                if k < len(units):
                    gn_a(k)
                if k >= 1:
                    gn_b(k - 1)
            phase_barrier()

            carve.reset()
            woutb = carve.take([128, 8, D], BF16)
            mixt = [carve.take([128, 8, 512], BF16) for _ in range(2)]
            for kc in range(8):
                dma("pool", woutb[:, kc, :], w_out[L, kc * 128:(kc + 1) * 128, :], AR, ["woutb%d" % kc])
            dma("sp", gb[1][:], g_post[L:L + 1, :].partition_broadcast(128), (), ["gb1"])
            for T in range(NT):
                mt = mixt[T % 2]
                mtok_ = "mixt%d" % (T % 2)
                dma("sp", mt, MIXT.rearrange("(k p) s -> p k s", p=128)[:, :, T * 512:(T + 1) * 512], ["MIXT"] + AR, [mtok_])
                for sub in range(4):
                    r0 = T * 512 + sub * 128
                    xi = sub % 2
                    dma("sp", xt[sub][:], x_src[r0:r0 + 128, :], ["X2"], ["xt%d" % sub])
                    for half in range(2):
                        b = 2 * xi + half
                        for kc in range(8):
                            mm(PS[b][:], mt[:, kc, sub * 128:(sub + 1) * 128], woutb[:, kc, half * 512:(half + 1) * 512],
                               kc == 0, kc == 7, [mtok_, "woutb%d" % kc] + AR, ["ps%d" % b])
                        act(yt[xi][:, half * 512:(half + 1) * 512], PS[b][:], AF.Square, ["ps%d" % b],
                            ["yt%d" % xi, "smc%d" % (8 + 2 * xi + half)], accum=small[:, 8 + 2 * xi + half:9 + 2 * xi + half])
                    col = small[:, 8 + 2 * xi:9 + 2 * xi]
                    ctok = "smc%d" % (8 + 2 * xi)
                    tt("dve", col, col, small[:, 9 + 2 * xi:10 + 2 * xi], ALU.add, [ctok, "smc%d" % (9 + 2 * xi)], [ctok])
                    rstd_col(col, ctok, 1.0 / D)
                    for half in range(2):
                        b = 2 * xi + half
                        hs = slice(half * 512, (half + 1) * 512)
                        stt("dve", yt[xi][:, hs], PS[b][:], col, gb[1][:, hs], ALU.mult, ALU.mult,
                            ["ps%d" % b, ctok, "gb1"], ["yt%d" % xi])
                    tt("dve", yt[xi][:], yt[xi][:], xt[sub][:], ALU.add, ["yt%d" % xi, "xt%d" % sub], ["yt%d" % xi])
                    dma("pool", X1[r0:r0 + 128, :], yt[xi][:], ["yt%d" % xi], ["X1"])
            phase_barrier()

            carve.reset()
            wupb = carve.take([128, 8, DFF], BF16)
            wdnb = carve.take([128, 32, D], BF16)
            TOK = 256
            uT = carve.take([128, 32, TOK], BF16)
            h2T = carve.take([128, 8, TOK], BF16)
            for kc in range(8):
                dma("pool", wupb[:, kc, :], w_up[L, kc * 128:(kc + 1) * 128, :], AR, ["wupb%d" % kc])
            for f4 in range(8):
                dma("pool", wdnb[:, 4 * f4:4 * f4 + 4, :],
                    w_dn[L, f4 * 512:(f4 + 1) * 512, :].rearrange("(f p) n -> p f n", p=128), AR, ["wdnb%d" % f4])
            dma("sp", gb[0][:], g_fpre[L:L + 1, :].partition_broadcast(128), (), ["gb0"])
            dma("sp", gb[1][:], g_fpost[L:L + 1, :].partition_broadcast(128), (), ["gb1"])
            NTT = S // TOK

            def f_load_x(T):
                for sub in range(2):
                    r0 = T * TOK + sub * 128
                    xq = (T % 2) * 2 + sub
                    dma("sp", xt[xq][:], X1[r0:r0 + 128, :], ["X1"], ["xt%d" % xq])

            def f_norm_pre(T, sub):
                xq = (T % 2) * 2 + sub
                col = small[:, 16 + sub:17 + sub]
                ctok = "smd%d" % sub
                act(hbf[sub][:], xt[xq][:], AF.Square, ["xt%d" % xq], ["hbf%d" % sub, ctok], accum=col)
                rstd_col(col, ctok, 1.0 / D)
                stt("dve", hbf[sub][:], xt[xq][:], col, gb[0][:], ALU.mult, ALU.mult,
                    ["xt%d" % xq, ctok, "gb0"], ["hbf%d" % sub])

            def f_tr(T, sub):
                psT = PS[sub][:].bitcast(BF16)
                for kc in range(8):
                    tr(psT[:, kc * 128:(kc + 1) * 128], hbf[sub][:, kc * 128:(kc + 1) * 128], ident[:],
                       ["hbf%d" % sub, "ident"], ["ps%d" % sub])
                cp("act" if sub else "dve", h2T[:, :, sub * 128:(sub + 1) * 128],
                   psT.rearrange("p (k c) -> p k c", k=8), ["ps%d" % sub] + AR, ["h2T"])

            def f_up(T):
                for fc in range(32):
                    b = fc % 4
                    for kc in range(8):
                        mm(PS[b][:, 0:TOK], wupb[:, kc, fc * 128:(fc + 1) * 128], h2T[:, kc, :], kc == 0, kc == 7,
                           ["wupb%d" % kc, "h2T"] + AR, ["ps%d" % b])
                    r_, rtok = w32()
                    act(r_[:, 0:TOK], PS[b][:, 0:TOK], AF.Relu, ["ps%d" % b], [rtok])
                    tt("pool" if fc % 2 else "dve", uT[:, fc, :], r_[:, 0:TOK], r_[:, 0:TOK], ALU.mult, [rtok] + AR, ["uT"])

            def f_down(T, sub):
                for half in range(2):
                    b = 4 + 2 * sub + half
                    for fc in range(32):
                        mm(PS[b][:], uT[:, fc, sub * 128:(sub + 1) * 128], wdnb[:, fc, half * 512:(half + 1) * 512],
                           fc == 0, fc == 31, ["uT", "wdnb%d" % (fc // 4)] + AR, ["ps%d" % b])

            def f_post(T, sub):
                r0 = T * TOK + sub * 128
                xq = (T % 2) * 2 + sub
                for half in range(2):
                    b = 4 + 2 * sub + half
                    act(yt[sub][:, half * 512:(half + 1) * 512], PS[b][:], AF.Square, ["ps%d" % b],
                        ["yt%d" % sub, "sme%d" % (2 * sub + half)],
                        accum=small[:, 24 + 2 * sub + half:25 + 2 * sub + half])
                col = small[:, 24 + 2 * sub:25 + 2 * sub]
                ctok = "sme%d" % (2 * sub)
                tt("dve", col, col, small[:, 25 + 2 * sub:26 + 2 * sub], ALU.add, [ctok, "sme%d" % (2 * sub + 1)], [ctok])
                rstd_col(col, ctok, 1.0 / D)
                for half in range(2):
                    b = 4 + 2 * sub + half
                    hs = slice(half * 512, (half + 1) * 512)
                    stt("dve", yt[sub][:, hs], PS[b][:], col, gb[1][:, hs], ALU.mult, ALU.mult,
                        ["ps%d" % b, ctok, "gb1"], ["yt%d" % sub])
                tt("pool", yt[sub][:], yt[sub][:], xt[xq][:], ALU.add, ["yt%d" % sub, "xt%d" % xq], ["yt%d" % sub])
                dma("sp", x_dst[r0:r0 + 128, :], yt[sub][:], ["yt%d" % sub], ["X2" if x_dst is X2 else "Y"])

            f_load_x(0)
            for sub in range(2):
                f_norm_pre(0, sub)
                f_tr(0, sub)
            for T in range(NTT):
                nxt_ = T + 1 < NTT
                if nxt_:
                    f_load_x(T + 1)
                f_up(T)
                f_down(T, 0)
                f_post(T, 0)
                if nxt_:
                    f_norm_pre(T + 1, 0)
                f_down(T, 1)
                if nxt_:
                    f_norm_pre(T + 1, 1)
                    f_tr(T + 1, 0)
                    f_tr(T + 1, 1)
                f_post(T, 1)
            phase_barrier()

        sch.add("sp", lambda e: e.nop(), (), list(set(sch.lastw.keys()) | set(sch.readers.keys())), force=True)
        sch.emit(nc, st)
    return nc, sch


_CACHE = {}


def make_in_maps(inputs, S, n_cores):
    consts = host_consts()
    maps = []
    f = lambda a: np.ascontiguousarray(np.asarray(a, dtype=np.float32))
    gq = f(inputs["g_q_lora"]).reshape(2, 2, 128).transpose(0, 2, 1)
    gkv = f(inputs["g_kv_lora"]).reshape(2, 128, 1)
    gmo = f(inputs["g_mix_out"]).reshape(2, 8, 128).transpose(0, 2, 1)
    bfc = f(inputs["b_forget"]).reshape(2, 4, 1)
    shared = dict(w_in=f(inputs["w_in"]), w_q_up=f(inputs["w_q_up"]), w_kv_up=f(inputs["w_kv_up"]),
                  w_out=f(inputs["w_out"]), w_ffn_up=f(inputs["w_ffn_up"]), w_ffn_down=f(inputs["w_ffn_down"]),
                  g_mix_pre=f(inputs["g_mix_pre"]), g_mix_post=f(inputs["g_mix_post"]),
                  g_ffn_pre=f(inputs["g_ffn_pre"]), g_ffn_post=f(inputs["g_ffn_post"]),
                  gq_col=np.ascontiguousarray(gq), gkv_col=np.ascontiguousarray(gkv),
                  gmo_col=np.ascontiguousarray(gmo), bf_col=np.ascontiguousarray(bfc))
    shared.update(consts)
    xs = f(inputs["x"])
    ps = np.asarray(inputs["positions"]).astype(np.int32)
    for c in range(n_cores):
        m = dict(shared)
        m["x"] = np.ascontiguousarray(xs[c])
        m["pos"] = np.ascontiguousarray(ps[c:c + 1])
        maps.append(m)
    return maps


def kernel(**inputs):
    x = np.asarray(inputs["x"])
    B, S, _ = x.shape
    if S not in _CACHE:
        _CACHE[S] = build(S)[0]
    nc = _CACHE[S]
    maps = make_in_maps(inputs, S, B)
    res = run_bass_kernel_spmd(nc, maps, core_ids=list(range(B)))
    return np.stack([np.asarray(r["y"], dtype=np.float32) for r in res.results], axis=0)
```
